# Optimizing a Trainium2 kernel written in Bass

```python
import math
import jax, jax.numpy as jnp
from jax import lax
import numpy as np

D_MODEL = 2048
BATCH = 16
SEQ = 256
DEPTH = 2
DEC_BATCH = 2
DEC_SEQ = 1024
PAST_LEN = 512

GRID_W = 64
N_EVEN = (DEPTH + 1) // 2
N_ODD = DEPTH // 2
N_DIR = 2
N_MOD = 9
EPS = 1e-6
D_FF = 5632

S5_WIDTH = D_MODEL // 2
S5_GROUP = 16
S5_GROUPS = S5_WIDTH // S5_GROUP
S5_STATE = 64
S5_DT_MIN = 1e-3
S5_DT_MAX = 1e-1

GLA_HEADS = 4
GLA_V = D_MODEL // 2
GLA_QK = GLA_V // 2
GLA_HEAD_K = GLA_QK // GLA_HEADS
GLA_HEAD_V = GLA_V // GLA_HEADS
GLA_RANK = 16
GLA_TAU = 16.0
GLA_CHUNK = 64

EV_IN = S5_WIDTH + 2 * GLA_QK + 2 * GLA_V + N_DIR * GLA_RANK
EV_MIX = S5_WIDTH + GLA_V

LRU_WIDTH = D_MODEL
LRU_HEADS = 8
LRU_BLOCK = LRU_WIDTH // LRU_HEADS
LRU_C = 8.0
CONV_W = 4
CONV_LEFT = 2

kernel_name = 'hybrid_s5_gla_rglru_diffusion_step'


def rmsnorm(x, g):
    xf = x.astype(jnp.float32)
    y = xf * lax.rsqrt(jnp.mean(xf * xf, axis=-1, keepdims=True) + EPS)
    return (y * g.astype(jnp.float32)).astype(x.dtype)


def swiglu(h, w_in, w_out):
    a, b = jnp.split(h @ w_in, 2, axis=-1)
    return (jax.nn.silu(a) * b) @ w_out


def flip(t):
    return t[:, ::-1]


def _cplx_combine(e1, e2):
    a1r, a1i, b1r, b1i = e1
    a2r, a2i, b2r, b2i = e2
    return (a2r * a1r - a2i * a1i,
            a2r * a1i + a2i * a1r,
            a2r * b1r - a2i * b1i + b2r,
            a2r * b1i + a2i * b1r + b2i)


def _real_combine(e1, e2):
    a1, b1 = e1
    a2, b2 = e2
    return a1 * a2, a2 * b1 + b2


def s5_direction(u, lam_re, lam_im, log_step, b_re, b_im, c_re, c_im, h0_re, h0_im):
    dt = jnp.exp(log_step)[:, None]
    z_re, z_im = lam_re * dt, lam_im * dt
    mag = jnp.exp(z_re)
    ab_re, ab_im = mag * jnp.cos(z_im), mag * jnp.sin(z_im)
    den = lam_re * lam_re + lam_im * lam_im
    n_re = ab_re - 1.0
    f_re = (n_re * lam_re + ab_im * lam_im) / den
    f_im = (ab_im * lam_re - n_re * lam_im) / den
    bb_re = f_re[..., None] * b_re - f_im[..., None] * b_im
    bb_im = f_re[..., None] * b_im + f_im[..., None] * b_re
    bu_re = jnp.einsum('gpn,blgn->blgp', bb_re, u)
    bu_im = jnp.einsum('gpn,blgn->blgp', bb_im, u)
    bu_re = bu_re.at[:, 0].add(ab_re * h0_re - ab_im * h0_im)
    bu_im = bu_im.at[:, 0].add(ab_re * h0_im + ab_im * h0_re)
    a_re = jnp.broadcast_to(ab_re, bu_re.shape)
    a_im = jnp.broadcast_to(ab_im, bu_im.shape)
    _, _, h_re, h_im = lax.associative_scan(_cplx_combine, (a_re, a_im, bu_re, bu_im), axis=1)
    y = jnp.einsum('gnp,blgp->blgn', c_re, h_re) - jnp.einsum('gnp,blgp->blgn', c_im, h_im)
    return y, h_re[:, -1], h_im[:, -1]


def gla_direction(q, k, v, log_a, s0):
    bsz, L, H, _ = q.shape
    n = L // GLA_CHUNK

    def chunks(t):
        return t.reshape(bsz, n, GLA_CHUNK, H, t.shape[-1]).transpose(1, 0, 3, 2, 4)

    qc, kc, vc, gc = chunks(q), chunks(k), chunks(v), chunks(log_a)
    bcum = jnp.cumsum(gc, axis=3)
    blast = bcum[:, :, :, -1:, :]
    q_t = qc * jnp.exp(bcum)
    k_t = kc * jnp.exp(-bcum)
    k_end = kc * jnp.exp(blast - bcum)
    mask = jnp.tril(jnp.ones((GLA_CHUNK, GLA_CHUNK), dtype=bool))
    att = jnp.where(mask, jnp.einsum('nbhid,nbhjd->nbhij', q_t, k_t), 0.0)
    o_intra = jnp.einsum('nbhij,nbhjv->nbhiv', att, vc)
    kv_chunk = jnp.einsum('nbhjd,nbhjv->nbhdv', k_end, vc)
    decay_chunk = jnp.exp(blast[:, :, :, 0, :])

    def step(s, inp):
        qt, kv, dec = inp
        o_inter = jnp.einsum('bhid,bhdv->bhiv', qt, s)
        return dec[..., None] * s + kv, o_inter

    s_fin, o_inter = lax.scan(step, s0, (q_t, kv_chunk, decay_chunk))
    o = (o_intra + o_inter).transpose(1, 0, 3, 2, 4).reshape(bsz, L, H, v.shape[-1])
    return o, s_fin


def even_mixer(h, p, e, s5_h0_re, s5_h0_im, gla_s0):
    f32 = jnp.float32
    bsz, L, _ = h.shape
    cuts = [S5_WIDTH, S5_WIDTH + GLA_QK, S5_WIDTH + 2 * GLA_QK,
            S5_WIDTH + 2 * GLA_QK + GLA_V, S5_WIDTH + 2 * GLA_QK + 2 * GLA_V]
    u, q, k, v, g, glr = jnp.split(h @ p['ev_w_in'][e], cuts, axis=-1)

    uf = u.astype(f32)
    ug = uf.reshape(bsz, L, S5_GROUPS, S5_GROUP)
    y_s5 = 0.0
    s5_re, s5_im = [], []
    for d in range(N_DIR):
        orient = (lambda t: t) if d == 0 else flip
        y, hr, hi = s5_direction(orient(ug),
                                 p['s5_lam_re'][e, d].astype(f32), p['s5_lam_im'][e, d].astype(f32),
                                 p['s5_log_step'][e, d].astype(f32),
                                 p['s5_b_re'][e, d].astype(f32), p['s5_b_im'][e, d].astype(f32),
                                 p['s5_c_re'][e, d].astype(f32), p['s5_c_im'][e, d].astype(f32),
                                 s5_h0_re[:, d].astype(f32), s5_h0_im[:, d].astype(f32))
        y_s5 = y_s5 + orient(y)
        s5_re.append(hr)
        s5_im.append(hi)
    y_s5 = y_s5.reshape(bsz, L, S5_WIDTH) + p['s5_d'][e].astype(f32) * uf
    y_s5 = jax.nn.gelu(y_s5)
    y_s5 = y_s5 * jax.nn.sigmoid(y_s5 @ p['s5_glu_w'][e].astype(f32) + p['s5_glu_b'][e].astype(f32))

    qh = q.astype(f32).reshape(bsz, L, GLA_HEADS, GLA_HEAD_K) * (GLA_HEAD_K ** -0.5)
    kh = k.astype(f32).reshape(bsz, L, GLA_HEADS, GLA_HEAD_K)
    vh = v.astype(f32).reshape(bsz, L, GLA_HEADS, GLA_HEAD_V)
    glr = glr.astype(f32).reshape(bsz, L, N_DIR, GLA_RANK)
    o_sum = 0.0
    gla_s = []
    for d in range(N_DIR):
        orient = (lambda t: t) if d == 0 else flip
        log_a = jax.nn.log_sigmoid(glr[:, :, d] @ p['gla_gate_w2'][e, d].astype(f32)
                                   + p['gla_gate_b'][e, d].astype(f32)) / GLA_TAU
        log_a = log_a.reshape(bsz, L, GLA_HEADS, GLA_HEAD_K)
        o, s = gla_direction(orient(qh), orient(kh), orient(vh), orient(log_a), gla_s0[:, d].astype(f32))
        o_sum = o_sum + orient(o)
        gla_s.append(s)
    o = o_sum * lax.rsqrt(jnp.mean(o_sum * o_sum, axis=-1, keepdims=True) + EPS) * p['gla_norm_g'][e].astype(f32)
    o = o.reshape(bsz, L, GLA_V) * jax.nn.silu(g.astype(f32))

    y = jnp.concatenate([y_s5, o], axis=-1).astype(h.dtype) @ p['ev_w_out'][e]
    return (y, jnp.stack(s5_re, 1).astype(h.dtype), jnp.stack(s5_im, 1).astype(h.dtype),
            jnp.stack(gla_s, 1).astype(h.dtype))


def depthwise_conv(x, w, b):
    y = lax.conv_general_dilated(x, w[:, None, :], window_strides=(1,),
                                 padding=[(CONV_LEFT, CONV_W - 1 - CONV_LEFT)],
                                 dimension_numbers=('NWC', 'WIO', 'NWC'),
                                 feature_group_count=x.shape[-1])
    return y + b


def rglru_direction(x, wa, ba, wx, bx, lam, h0):
    bsz, L, W = x.shape
    xb = x.reshape(bsz, L, LRU_HEADS, LRU_BLOCK)
    r = jax.nn.sigmoid(jnp.einsum('blhi,hij->blhj', xb, wa).reshape(bsz, L, W) + ba)
    i = jax.nn.sigmoid(jnp.einsum('blhi,hij->blhj', xb, wx).reshape(bsz, L, W) + bx)
    log_a = -LRU_C * r * jax.nn.softplus(-lam)
    a = jnp.exp(log_a)
    b = jnp.sqrt(-jnp.expm1(2.0 * log_a)) * (i * x)
    b = b.at[:, 0].add(a[:, 0] * h0)
    _, hs = lax.associative_scan(_real_combine, (a, b), axis=1)
    return hs, hs[:, -1]


def odd_mixer(h, p, o, lru_h0, grid):
    f32 = jnp.float32
    bsz, L, _ = h.shape
    gate_br, x_br = jnp.split(h @ p['od_w_in'][o], 2, axis=-1)
    cw = p['lru_conv_w'][o].astype(f32)
    cb = p['lru_conv_b'][o].astype(f32)
    xf = x_br.astype(f32)
    if grid:
        rows = L // GRID_W
        xc = depthwise_conv(xf.reshape(bsz * rows, GRID_W, LRU_WIDTH), cw, cb).reshape(bsz, L, LRU_WIDTH)
    else:
        xc = depthwise_conv(xf, cw, cb)
    hsum = 0.0
    states = []
    for d in range(N_DIR):
        orient = (lambda t: t) if d == 0 else flip
        hs, s = rglru_direction(orient(xc),
                                p['lru_wa'][o, d].astype(f32), p['lru_ba'][o, d].astype(f32),
                                p['lru_wx'][o, d].astype(f32), p['lru_bx'][o, d].astype(f32),
                                p['lru_lam'][o, d].astype(f32), lru_h0[:, d].astype(f32))
        hsum = hsum + orient(hs)
        states.append(s)
    y = (hsum * jax.nn.gelu(gate_br.astype(f32))).astype(h.dtype) @ p['od_w_out'][o]
    return y, jnp.stack(states, 1).astype(h.dtype)


def trunk(x, cond, s5_re0, s5_im0, gla0, lru0, p, grid):
    cond_act = jax.nn.silu(cond)
    s5_re_out, s5_im_out, gla_out, lru_out = [], [], [], []
    for l in range(DEPTH):
        mod = (cond_act @ p['ada_w'][l] + p['ada_b'][l]).reshape(cond.shape[0], 1, N_MOD, D_MODEL)
        sh1, sc1, g1, sh2, sc2, g2, sh3, sc3, g3 = [mod[:, :, j] for j in range(N_MOD)]
        hm = rmsnorm(x, p['norm_g'][l, 0]) * (1.0 + sc1) + sh1
        x = x + 0.5 * g1 * swiglu(hm, p['ffn_w_in'][l, 0], p['ffn_w_out'][l, 0])
        hm = rmsnorm(x, p['norm_g'][l, 1]) * (1.0 + sc2) + sh2
        if l % 2 == 0:
            e = l // 2
            y, sr, si, sg = even_mixer(hm, p, e, s5_re0[:, e], s5_im0[:, e], gla0[:, e])
            s5_re_out.append(sr)
            s5_im_out.append(si)
            gla_out.append(sg)
        else:
            o = l // 2
            y, sl = odd_mixer(hm, p, o, lru0[:, o], grid)
            lru_out.append(sl)
        x = x + g2 * y
        hm = rmsnorm(x, p['norm_g'][l, 2]) * (1.0 + sc3) + sh3
        x = x + 0.5 * g3 * swiglu(hm, p['ffn_w_in'][l, 1], p['ffn_w_out'][l, 1])
    return (rmsnorm(x, p['final_norm_g']), jnp.stack(s5_re_out, 1), jnp.stack(s5_im_out, 1),
            jnp.stack(gla_out, 1), jnp.stack(lru_out, 1))


def setup_inputs(seed: int = 0) -> dict:
    key = jax.random.key(seed)
    ks = iter(jax.random.split(key, 48))
    f32 = jnp.float32

    def nrm(shape, scale):
        return scale * jax.random.normal(next(ks), shape, f32)

    def unif(shape, lo, hi):
        return jax.random.uniform(next(ks), shape, f32, lo, hi)

    lam_im_base = jnp.pi * jnp.arange(S5_STATE, dtype=f32)
    lru_s = unif((N_ODD, N_DIR, LRU_WIDTH), 0.9, 0.999) ** (1.0 / LRU_C)
    return {
        'x_prompt': nrm((BATCH, SEQ, D_MODEL), 1.0),
        'x_sample': nrm((DEC_BATCH, DEC_SEQ, D_MODEL), 1.0),
        'state_s5_re': nrm((DEC_BATCH, N_EVEN, N_DIR, S5_GROUPS, S5_STATE), 0.5),
        'state_s5_im': nrm((DEC_BATCH, N_EVEN, N_DIR, S5_GROUPS, S5_STATE), 0.5),
        'state_gla': nrm((DEC_BATCH, N_EVEN, N_DIR, GLA_HEADS, GLA_HEAD_K, GLA_HEAD_V), 1.0),
        'state_lru': nrm((DEC_BATCH, N_ODD, N_DIR, LRU_WIDTH), 0.5),
        'c': nrm((DEC_BATCH, D_MODEL), 1.0),
        'c_ctx': nrm((D_MODEL,), 1.0),
        'norm_g': 1.0 + nrm((DEPTH, 3, D_MODEL), 0.02),
        'ada_w': nrm((DEPTH, D_MODEL, N_MOD * D_MODEL), 0.5 * D_MODEL ** -0.5),
        'ada_b': nrm((DEPTH, N_MOD * D_MODEL), 0.02),
        'ffn_w_in': nrm((DEPTH, 2, D_MODEL, 2 * D_FF), D_MODEL ** -0.5),
        'ffn_w_out': nrm((DEPTH, 2, D_FF, D_MODEL), D_FF ** -0.5),
        'final_norm_g': 1.0 + nrm((D_MODEL,), 0.02),
        'ev_w_in': nrm((N_EVEN, D_MODEL, EV_IN), D_MODEL ** -0.5),
        'ev_w_out': nrm((N_EVEN, EV_MIX, D_MODEL), EV_MIX ** -0.5),
        's5_lam_re': -0.5 + nrm((N_EVEN, N_DIR, S5_GROUPS, S5_STATE), 0.01),
        's5_lam_im': lam_im_base + nrm((N_EVEN, N_DIR, S5_GROUPS, S5_STATE), 0.01),
        's5_log_step': unif((N_EVEN, N_DIR, S5_GROUPS), math.log(S5_DT_MIN), math.log(S5_DT_MAX)),
        's5_b_re': nrm((N_EVEN, N_DIR, S5_GROUPS, S5_STATE, S5_GROUP), (2 * S5_GROUP) ** -0.5),
        's5_b_im': nrm((N_EVEN, N_DIR, S5_GROUPS, S5_STATE, S5_GROUP), (2 * S5_GROUP) ** -0.5),
        's5_c_re': nrm((N_EVEN, N_DIR, S5_GROUPS, S5_GROUP, S5_STATE), S5_STATE ** -0.5),
        's5_c_im': nrm((N_EVEN, N_DIR, S5_GROUPS, S5_GROUP, S5_STATE), S5_STATE ** -0.5),
        's5_d': nrm((N_EVEN, S5_WIDTH), 0.5),
        's5_glu_w': nrm((N_EVEN, S5_WIDTH, S5_WIDTH), S5_WIDTH ** -0.5),
        's5_glu_b': nrm((N_EVEN, S5_WIDTH), 0.02),
        'gla_gate_w2': nrm((N_EVEN, N_DIR, GLA_RANK, GLA_QK), GLA_RANK ** -0.5),
        'gla_gate_b': nrm((N_EVEN, N_DIR, GLA_QK), 0.1),
        'gla_norm_g': 1.0 + nrm((N_EVEN, GLA_HEAD_V), 0.02),
        'od_w_in': nrm((N_ODD, D_MODEL, 2 * LRU_WIDTH), D_MODEL ** -0.5),
        'od_w_out': nrm((N_ODD, LRU_WIDTH, D_MODEL), LRU_WIDTH ** -0.5),
        'lru_conv_w': nrm((N_ODD, CONV_W, LRU_WIDTH), CONV_W ** -0.5),
        'lru_conv_b': nrm((N_ODD, LRU_WIDTH), 0.02),
        'lru_wa': nrm((N_ODD, N_DIR, LRU_HEADS, LRU_BLOCK, LRU_BLOCK), LRU_BLOCK ** -0.5),
        'lru_ba': nrm((N_ODD, N_DIR, LRU_WIDTH), 0.02),
        'lru_wx': nrm((N_ODD, N_DIR, LRU_HEADS, LRU_BLOCK, LRU_BLOCK), LRU_BLOCK ** -0.5),
        'lru_bx': nrm((N_ODD, N_DIR, LRU_WIDTH), 0.02),
        'lru_lam': jnp.log(lru_s) - jnp.log1p(-lru_s),
    }


def reference(x_prompt, x_sample, state_s5_re, state_s5_im, state_gla, state_lru, c, c_ctx,
              norm_g, ada_w, ada_b, ffn_w_in, ffn_w_out, final_norm_g,
              ev_w_in, ev_w_out, s5_lam_re, s5_lam_im, s5_log_step, s5_b_re, s5_b_im, s5_c_re, s5_c_im,
              s5_d, s5_glu_w, s5_glu_b, gla_gate_w2, gla_gate_b, gla_norm_g,
              od_w_in, od_w_out, lru_conv_w, lru_conv_b, lru_wa, lru_ba, lru_wx, lru_bx, lru_lam):
    p = dict(norm_g=norm_g, ada_w=ada_w, ada_b=ada_b, ffn_w_in=ffn_w_in, ffn_w_out=ffn_w_out,
             final_norm_g=final_norm_g, ev_w_in=ev_w_in, ev_w_out=ev_w_out,
             s5_lam_re=s5_lam_re, s5_lam_im=s5_lam_im, s5_log_step=s5_log_step,
             s5_b_re=s5_b_re, s5_b_im=s5_b_im, s5_c_re=s5_c_re, s5_c_im=s5_c_im,
             s5_d=s5_d, s5_glu_w=s5_glu_w, s5_glu_b=s5_glu_b,
             gla_gate_w2=gla_gate_w2, gla_gate_b=gla_gate_b, gla_norm_g=gla_norm_g,
             od_w_in=od_w_in, od_w_out=od_w_out, lru_conv_w=lru_conv_w, lru_conv_b=lru_conv_b,
             lru_wa=lru_wa, lru_ba=lru_ba, lru_wx=lru_wx, lru_bx=lru_bx, lru_lam=lru_lam)
    bsz = x_prompt.shape[0]
    dt = x_prompt.dtype
    y_prompt, new_s5_re, new_s5_im, new_gla, new_lru = trunk(
        x_prompt, c_ctx[None, :],
        jnp.zeros((bsz, N_EVEN, N_DIR, S5_GROUPS, S5_STATE), dt),
        jnp.zeros((bsz, N_EVEN, N_DIR, S5_GROUPS, S5_STATE), dt),
        jnp.zeros((bsz, N_EVEN, N_DIR, GLA_HEADS, GLA_HEAD_K, GLA_HEAD_V), dt),
        jnp.zeros((bsz, N_ODD, N_DIR, LRU_WIDTH), dt),
        p, False)
    y_sample, _, _, _, _ = trunk(x_sample, c, state_s5_re, state_s5_im, state_gla, state_lru, p, True)
    return (y_prompt, y_sample, new_s5_re, new_s5_im, new_gla, new_lru)
```

```python
import contextlib
import math
import numpy as np
import concourse.bass as bass
import concourse.mybir as mybir
from concourse.bass_utils import run_bass_kernel_spmd

F32 = mybir.dt.float32
BF16 = mybir.dt.bfloat16
I32 = mybir.dt.int32
ACT = mybir.ActivationFunctionType
ALU = mybir.AluOpType

D = 2048
NCH = 16
NT = 1024
TB = 512
NTB = 2
SL = 256
NSL = 4
DFF = 5632
NSLOT = 10
TSZ = 2048
EPS = 1e-6
N_CORES = 8
ARENA_F32 = 11520
GELU_C = 1.5957691216057308


class Tok:
    __slots__ = ("eng", "val")

    def __init__(self, eng, val):
        self.eng = eng
        self.val = val


class Prog:
    def __init__(self, nc, es):
        self.nc = nc
        self.es = es
        self.engs = {"pe": nc.tensor, "act": nc.scalar, "dve": nc.vector, "pool": nc.gpsimd, "sp": nc.sync}
        self.ms = {k: es.enter_context(nc.semaphore("ms_" + k)) for k in ("pe", "act", "dve")}
        self.msn = {k: 0 for k in self.ms}
        self.waited = {}
        self.dsems = {}
        self.ps = es.enter_context(nc.psum_tensor("ps", [128, 8, 512], F32))
        self.bank_free = [None] * 8
        self.bank_rr = 0
        self.ring = es.enter_context(nc.sbuf_tensor("ring", [128, NSLOT, TSZ], BF16))
        self.slot_ld = [es.enter_context(nc.semaphore("ld%d" % s)) for s in range(NSLOT)]
        self.slot_ldn = [0] * NSLOT
        self.slot_free = [None] * NSLOT
        self.slot_rr = 0
        self.slot_open = [False] * NSLOT
        self.last_pe = None
        self.scr = es.enter_context(nc.sbuf_tensor("scr", [128, 4], F32))

    def wait(self, eng, sem, val):
        key = (eng, id(sem))
        if self.waited.get(key, 0) >= val:
            return
        self.waited[key] = val
        self.engs[eng].wait_ge(sem, val)

    def wait_tok(self, eng, tok):
        if tok is None or tok.eng == eng:
            return
        self.wait(eng, self.ms[tok.eng], tok.val)

    def fence(self, eng, tok):
        self.wait(eng, self.ms[tok.eng], tok.val)

    def mark(self, eng, ins):
        ins.then_inc(self.ms[eng], 1)
        self.msn[eng] += 1
        return Tok(eng, self.msn[eng])

    def dma(self, eng, out, in_, name):
        if name not in self.dsems:
            self.dsems[name] = [self.es.enter_context(self.nc.semaphore("d_" + name)), 0]
        s = self.dsems[name]
        self.engs[eng].dma_start(out=out, in_=in_).then_inc(s[0], 16)
        s[1] += 16
        return (s[0], s[1])

    def wait_dma(self, eng, d):
        self.wait(eng, d[0], d[1])

    def barrier(self):
        nc = self.nc
        ta = self.mark("act", nc.scalar.activation(out=self.scr[:, 0:1], in_=self.scr[:, 0:1], func=ACT.Copy))
        td = self.mark("dve", nc.vector.memset(self.scr[:, 1:2], 0.0))
        for e in ("pe", "act", "dve"):
            self.wait_tok(e, ta)
            self.wait_tok(e, td)
            self.wait_tok(e, self.last_pe)

    def sp_sync(self):
        self.barrier()
        self.wait("sp", self.ms["act"], self.msn["act"])
        self.wait("sp", self.ms["dve"], self.msn["dve"])
        self.wait_tok("sp", self.last_pe)

    def pool_sync(self):
        self.barrier()
        self.wait("pool", self.ms["act"], self.msn["act"])
        self.wait("pool", self.ms["dve"], self.msn["dve"])
        self.wait_tok("pool", self.last_pe)

    def bank(self):
        b = self.bank_rr
        self.bank_rr = (b + 1) % 8
        return b

    def bank_pair(self):
        if self.bank_rr % 2:
            self.bank_rr = (self.bank_rr + 1) % 8
        b = self.bank_rr
        self.bank_rr = (b + 2) % 8
        return b, b + 1

    def load_tile(self, src_ap, n=TSZ):
        s = self.slot_rr
        self.slot_rr = (s + 1) % NSLOT
        assert not self.slot_open[s], "ring slot %d still open" % s
        self.slot_open[s] = True
        self.wait_tok("pool", self.slot_free[s])
        self.nc.gpsimd.dma_start(out=self.ring[:, s, 0:n], in_=src_ap).then_inc(self.slot_ld[s], 16)
        self.slot_ldn[s] += 16
        return (s, self.slot_ldn[s])

    def release(self, t):
        self.slot_open[t[0]] = False

    def tile_ap(self, t):
        return self.ring[:, t[0], :]

    def transpose(self, out_ap, in_ap, ident, banks, waits=()):
        for b in banks:
            self.wait_tok("pe", self.bank_free[b])
        for w in waits:
            self.wait_tok("pe", w)
        ins = self.nc.tensor.transpose(out=out_ap, in_=in_ap, identity=ident)
        tok = self.mark("pe", ins)
        self.last_pe = tok
        return tok

    def group2(self, items, banks, tiles=(), waits=()):
        for b in banks:
            self.wait_tok("pe", self.bank_free[b])
        for t in tiles:
            self.wait("pe", self.slot_ld[t[0]], t[1])
        for w in waits:
            self.wait_tok("pe", w)
        n = len(items)
        ins = None
        for i, (o, l, r) in enumerate(items):
            ins = self.nc.tensor.matmul(o, l, r, start=(i == 0), stop=(i == n - 1))
        tok = self.mark("pe", ins)
        self.last_pe = tok
        for t in tiles:
            self.slot_free[t[0]] = tok
        return tok

    def group(self, out_ap, pairs, banks, tiles=(), waits=(), mark=True):
        for b in banks:
            self.wait_tok("pe", self.bank_free[b])
        for t in tiles:
            self.wait("pe", self.slot_ld[t[0]], t[1])
        for w in waits:
            self.wait_tok("pe", w)
        n = len(pairs)
        ins = None
        for i, (l, r) in enumerate(pairs):
            ins = self.nc.tensor.matmul(out_ap, l, r, start=(i == 0), stop=(i == n - 1))
        if not mark:
            return None
        tok = self.mark("pe", ins)
        self.last_pe = tok
        for t in tiles:
            self.slot_free[t[0]] = tok
        return tok


class Arena:
    def __init__(self, ap):
        self.ap = ap
        self.off = 0

    def reset(self):
        self.off = 0

    def f32(self, *shape):
        n = int(np.prod(shape))
        v = self.ap[:, self.off:self.off + n]
        self.off += n
        assert self.off <= ARENA_F32, ("arena overflow", self.off)
        if len(shape) == 2:
            return v.rearrange("p (a b) -> p a b", a=shape[0])
        if len(shape) == 3:
            return v.rearrange("p (a b c) -> p a b c", a=shape[0], b=shape[1])
        return v

    def bf16(self, *shape):
        n = int(np.prod(shape))
        assert n % 2 == 0
        v = self.ap[:, self.off:self.off + n // 2].bitcast(BF16)
        self.off += n // 2
        assert self.off <= ARENA_F32, ("arena overflow", self.off)
        if len(shape) == 2:
            return v.rearrange("p (a b) -> p a b", a=shape[0])
        if len(shape) == 3:
            return v.rearrange("p (a b c) -> p a b c", a=shape[0], b=shape[1])
        return v


def build_program(mix_even=True, mix_odd=True, debug=False, mix_gla=None, mix_s5=None):
    if mix_gla is None:
        mix_gla = mix_even
    if mix_s5 is None:
        mix_s5 = mix_even
    nc = bass.Bass("TRN2", target_bir_lowering=False)
    es = contextlib.ExitStack()
    dbg_list = []

    def din(name, shape):
        return nc.dram_tensor(name, list(shape), F32, kind="ExternalInput").ap()

    def dout(name, shape):
        return nc.dram_tensor(name, list(shape), F32, kind="ExternalOutput").ap()

    xT_in = din("xT", [128, NCH * NT])
    cond_in = din("cond", [128, NCH])
    vecs_in = din("vecs", [128, NVEC])
    cst_in = din("consts", [128, NCST])
    pc_in = din("pcore", [128, NPC])
    w_ada = din("w_ada", [2 * 144, 128, TSZ])
    w_in = din("w_in", [4 * 88, 128, TSZ])
    w_out = din("w_out", [4 * 44, 128, TSZ])
    w_odin = din("w_odin", [32, 128, TSZ])
    w_odout = din("w_odout", [16, 128, TSZ])
    w_lru = din("w_lru", [8, 128, TSZ])
    w_evin = din("w_evin", [33, 128, TSZ])
    w_evout = din("w_evout", [16, 128, TSZ])
    gw2_in = din("gw2", [16, 1024])
    gs0_in = din("gs0", [4, 128, 2 * SL])
    w_glu = din("w_glu", [8, 128, 1024])
    s5lb_in = din("s5lb", [128, 192])
    s5cb_in = din("s5cb", [128, 2048])
    s5la_in = din("s5la", [128, 3072])
    s5ba_in = din("s5ba", [128, 2048])
    s5h0_in = din("s5h0", [128, 128])
    yT_out = dout("yT", [128, NCH * NT])
    lru_out = dout("lru_st", [128, NSL * 2 * NCH])
    gla_out = dout("gla_st", [NSL * 2 * 4, 128, SL])
    s5_out = dout("s5_st", [128, NSL * 2 * 64])
    dbg_out = dout("dbg", [8, 128, 2048]) if debug else None

    with es:
        P = Prog(nc, es)
        sb = lambda name, shape, dt=F32: es.enter_context(nc.sbuf_tensor(name, list(shape), dt))
        XT = sb("XT", [128, NCH, NT])
        HM = sb("HM", [128, NCH, NT], BF16)
        ARN = sb("ARN", [128, ARENA_F32])
        AR = Arena(ARN[:])
        VEC = sb("VEC", [128, NVEC])
        CST = sb("CST", [128, NCST])
        PC = sb("PC", [128, NPC])
        MOD = sb("MOD", [128, 2, 144])
        MV = sb("MV", [128, 2, 9, 16])
        CA = sb("CA", [128, NCH], BF16)
        CAf = sb("CAf", [128, NCH])
        ONES = sb("ONES", [128, 128], BF16)
        EPSB = sb("EPSB", [128, 1])
        LRUST = sb("LRUST", [128, NSL, 2, NCH])
        NSP = sb("NSP", [128, 2, 2, NCH])

        state = {"xt_w": None, "hm_r": None}

        def vcol(name, j=0, w=16):
            o = VOFF[name] + j * w
            return VEC[:, o:o + w]

        def ccol(name, w):
            o = COFF[name]
            return CST[:, o:o + w]

        def pcol(name, w):
            o = POFF[name]
            return PC[:, o:o + w]

        def dump(ap2d):
            if not debug:
                return
            i = len(dbg_list)
            dbg_list.append(i)
            P.barrier()
            P.wait("sp", P.ms["dve"], P.msn["dve"])
            P.wait("sp", P.ms["act"], P.msn["act"])
            P.wait_tok("sp", P.last_pe)
            d = P.dma("sp", dbg_out[i, :, 0:ap2d.shape[1]], ap2d, "dbg")
            for e in ("pe", "act", "dve"):
                P.wait_dma(e, d)

        ld = []
        for q in range(4):
            ld.append(P.dma("sp", XT[:, 4 * q:4 * q + 4, :], xT_in[:, 4 * q * NT:(4 * q + 4) * NT].rearrange("p (c t) -> p c t", c=4), "inx%d" % q))
        ldv = P.dma("sp", VEC[:], vecs_in[:, :], "inv")
        ldk = P.dma("sp", CST[:], cst_in[:, :], "ink")
        ldp = P.dma("sp", PC[:], pc_in[:, :], "inp")
        ldc = P.dma("sp", CAf[:], cond_in[:, :], "inc")
        for e in ("act", "dve", "pool", "pe"):
            for dd in (ldc, ldv, ldk, ldp):
                P.wait_dma(e, dd)
        ones_ready = es.enter_context(nc.semaphore("ones"))
        nc.gpsimd.memset(EPSB[:], EPS)
        nc.gpsimd.memset(P.scr[:], 0.0)
        nc.gpsimd.memset(ONES[:], 1.0).then_inc(ones_ready, 1)
        for e in ("pe", "act", "dve"):
            P.wait(e, ones_ready, 1)
        ins = nc.scalar.activation(out=CA[:], in_=CAf[:], func=ACT.Silu)
        t_ca = P.mark("act", ins)

        def ada(l):
            b = P.bank()
            tok = None
            for i in range(144):
                t = P.load_tile(w_ada[l * 144 + i])
                ap = P.tile_ap(t).rearrange("p (k n) -> p k n", k=16)
                tok = P.group(P.ps[:, b, i:i + 1], [(ap[:, kc, :], CA[:, kc:kc + 1]) for kc in range(16)],
                              [b] if i == 0 else [], tiles=[t], waits=[t_ca])
                P.release(t)
            P.wait_tok("dve", tok)
            ab = VOFF["ada_b"] + l * 144
            nc.vector.tensor_tensor(out=MOD[:, l, :], in0=P.ps[:, b, 0:144], in1=VEC[:, ab:ab + 144], op=ALU.add)
            for s in range(3):
                sh = MOD[:, l, (3 * s) * 16:(3 * s) * 16 + 16]
                sc = MOD[:, l, (3 * s + 1) * 16:(3 * s + 1) * 16 + 16]
                g = MOD[:, l, (3 * s + 2) * 16:(3 * s + 2) * 16 + 16]
                ng = vcol("norm_g", l * 3 + s)
                nc.vector.scalar_tensor_tensor(out=MV[:, l, 3 * s, :], in0=sc, scalar=1.0, in1=ng, op0=ALU.add, op1=ALU.mult)
                nc.vector.tensor_copy(out=MV[:, l, 3 * s + 1, :], in_=sh)
                ins = nc.vector.tensor_scalar(out=MV[:, l, 3 * s + 2, :], in0=g, scalar1=(1.0 if s == 1 else 0.5), scalar2=None, op0=ALU.mult)
            tk = P.mark("dve", ins)
            P.bank_free[b] = tk
            return tk

        def norm(A, sh, mv_tok, final=False):
            AR.reset()
            RS = AR.f32(TB)
            TMP = AR.f32(TB)
            P.barrier()
            prev = None
            for tb in range(NTB):
                ts = slice(tb * TB, (tb + 1) * TB)
                P.wait_tok("act", prev)
                for q in range(4):
                    P.wait_dma("act", ld[q])
                    P.wait_dma("dve", ld[q])
                for c in range(NCH):
                    ins = nc.scalar.activation(out=HM[:, c, ts], in_=XT[:, c, ts], func=ACT.Square)
                tS = P.mark("act", ins)
                b = P.bank()
                tP = P.group(P.ps[:, b, :], [(ONES[:, :], HM[:, c, ts]) for c in range(NCH)], [b], waits=[tS])
                P.wait_tok("act", tP)
                ins = nc.scalar.activation(out=RS, in_=P.ps[:, b, :], func=ACT.Sqrt, scale=1.0 / D, bias=EPSB[:, 0:1])
                tR = P.mark("act", ins)
                P.bank_free[b] = tR
                P.wait_tok("dve", tR)
                P.wait_tok("dve", mv_tok)
                nc.vector.reciprocal(out=RS, in_=RS)
                for c in range(NCH):
                    if final:
                        ins = nc.vector.scalar_tensor_tensor(out=XT[:, c, ts], in0=XT[:, c, ts], scalar=A[:, c:c + 1], in1=RS, op0=ALU.mult, op1=ALU.mult)
                    else:
                        nc.vector.scalar_tensor_tensor(out=TMP, in0=XT[:, c, ts], scalar=A[:, c:c + 1], in1=RS, op0=ALU.mult, op1=ALU.mult)
                        ins = nc.vector.tensor_scalar(out=HM[:, c, ts], in0=TMP, scalar1=sh[:, c:c + 1], scalar2=None, op0=ALU.add)
                prev = P.mark("dve", ins)
            P.barrier()
            return prev

        def ffn(f, G, hm_tok):
            AR.reset()
            PN = 8
            H = AR.bf16(PN, NT)
            lastB = None
            for p0 in range(0, 44, PN):
                npan = min(PN, 44 - p0)
                tokD = None
                for j in range(npan):
                    hc = p0 + j
                    ta = P.load_tile(w_in[f * 88 + 2 * hc])
                    tb_ = P.load_tile(w_in[f * 88 + 2 * hc + 1])
                    apa = P.tile_ap(ta).rearrange("p (k n) -> p k n", k=16)
                    apb = P.tile_ap(tb_).rearrange("p (k n) -> p k n", k=16)
                    for tb in range(NTB):
                        ts = slice(tb * TB, (tb + 1) * TB)
                        ba, bb = P.bank_pair()
                        P.group(P.ps[:, ba, :], [(apa[:, kc, :], HM[:, kc, ts]) for kc in range(16)], [ba], tiles=[ta], waits=[hm_tok])
                        tpb = P.group(P.ps[:, bb, :], [(apb[:, kc, :], HM[:, kc, ts]) for kc in range(16)], [bb], tiles=[tb_])
                        P.wait_tok("act", tpb)
                        P.wait_tok("act", lastB)
                        ins = nc.scalar.activation(out=H[:, j, ts], in_=P.ps[:, ba, :], func=ACT.Silu)
                        tA = P.mark("act", ins)
                        P.wait_tok("dve", tA)
                        ins = nc.vector.tensor_tensor(out=H[:, j, ts], in0=H[:, j, ts], in1=P.ps[:, bb, :], op=ALU.mult)
                        tokD = P.mark("dve", ins)
                        P.bank_free[ba] = tokD
                        P.bank_free[bb] = tokD
                    P.release(ta)
                    P.release(tb_)
                tiles = [P.load_tile(w_out[f * 44 + p0 + j]) for j in range(npan)]
                aps = [P.tile_ap(t) for t in tiles]
                for oc in range(NCH):
                    for tb in range(NTB):
                        ts = slice(tb * TB, (tb + 1) * TB)
                        b = P.bank()
                        pairs = [(aps[j][:, oc * 128:(oc + 1) * 128], H[:, j, ts]) for j in range(npan)]
                        lastB = P.group(P.ps[:, b, :], pairs, [b], tiles=tiles, waits=[tokD])
                        P.wait_tok("dve", lastB)
                        ins = nc.vector.scalar_tensor_tensor(out=XT[:, oc, ts], in0=P.ps[:, b, :], scalar=G[:, oc:oc + 1], in1=XT[:, oc, ts], op0=ALU.mult, op1=ALU.add)
                        P.bank_free[b] = P.mark("dve", ins)
                for t in tiles:
                    P.release(t)

        def lru_prep():
            lam = VEC[:, VOFF["lru_lam"]:VOFF["lru_lam"] + 32]
            t = NSP[:, 0, :, :].rearrange("p d c -> p (d c)")
            t2 = NSP[:, 1, :, :].rearrange("p d c -> p (d c)")
            P.wait_dma("act", ldv)
            nc.scalar.activation(out=t, in_=lam, func=ACT.Exp, scale=-1.0)
            ins = nc.scalar.activation(out=t, in_=t, func=ACT.Ln, bias=1.0)
            tk = P.mark("act", ins)
            P.wait_tok("dve", tk)
            nc.vector.tensor_scalar(out=t2, in0=t, scalar1=-16.0, scalar2=None, op0=ALU.mult)
            nc.vector.tensor_scalar(out=t, in0=t, scalar1=-8.0, scalar2=None, op0=ALU.mult)

        def odd_mixer(l, G2):
            P.barrier()
            AR.reset()
            XC = AR.f32(2, NT)
            HS = AR.f32(2, NT)
            XCb = AR.bf16(2, NT)
            Y = AR.bf16(2, NT)
            T12 = AR.f32(2, 2 * SL)
            T1 = T12[:, 0, :]
            T2 = T12[:, 1, :]
            T3 = AR.f32(2 * SL)
            T4 = AR.f32(2, SL)
            XF = AR.f32(2, SL)
            XFb = AR.bf16(2, SL)
            TT = AR.f32(128)
            HIN = AR.f32(4)
            PEB = AR.f32(2)
            IDENT = ccol("ident", 128)
            JREV = ccol("jrev", 128)
            CVM = pcol("cvm", 3 * SL).rearrange("p (j t) -> p j t", j=3)
            CARRY = pcol("carry", 1)
            LH0 = pcol("lh0", 2 * NCH).rearrange("p (d c) -> p d c", d=2)
            cw = VEC[:, VOFF["conv_w"]:VOFF["conv_w"] + 64].rearrange("p (j c) -> p j c", j=4)
            cb = vcol("conv_b")
            nba = VEC[:, VOFF["lru_ba"]:VOFF["lru_ba"] + 32].rearrange("p (d c) -> p d c", d=2)
            nbx = VEC[:, VOFF["lru_bx"]:VOFF["lru_bx"] + 32].rearrange("p (d c) -> p d c", d=2)
            NB = AR.f32(2, 2, NCH)
            nc.vector.tensor_scalar(out=NB[:, 0, :, :], in0=nba, scalar1=-1.0, scalar2=None, op0=ALU.mult)
            nc.vector.tensor_scalar(out=NB[:, 1, :, :], in0=nbx, scalar1=-1.0, scalar2=None, op0=ALU.mult)
            P.barrier()
            st = {"prev": None}

            def lru_pair(apl, tl, d, h, src_f32, src_bf, out_fn, hin_fn, wtok):
                br, bi = P.bank_pair()
                tpi = None
                for oc in range(2):
                    ocs = slice(oc * 128, (oc + 1) * 128)
                    cs = slice(oc * SL, (oc + 1) * SL)
                    P.group(P.ps[:, br, cs], [(apl[:, 0, d, kc, ocs], src_bf(kc)) for kc in range(2)], [br] if oc == 0 else [], tiles=[tl], waits=[wtok])
                    tpi = P.group(P.ps[:, bi, cs], [(apl[:, 1, d, kc, ocs], src_bf(kc)) for kc in range(2)], [bi] if oc == 0 else [], tiles=[tl])
                P.wait_tok("act", tpi)
                P.wait_tok("act", st["prev"])
                for oc in range(2):
                    ch = 2 * h + oc
                    cs = slice(oc * SL, (oc + 1) * SL)
                    nc.scalar.activation(out=T1[:, cs], in_=P.ps[:, br, cs], func=ACT.Exp, scale=-1.0, bias=NB[:, 0, d, ch:ch + 1])
                    ins = nc.scalar.activation(out=T2[:, cs], in_=P.ps[:, bi, cs], func=ACT.Exp, scale=-1.0, bias=NB[:, 1, d, ch:ch + 1])
                tA = P.mark("act", ins)
                P.bank_free[br] = tA
                P.bank_free[bi] = tA
                P.wait_tok("dve", tA)
                T12f = T12[:].rearrange("p a b -> p (a b)")
                nc.vector.tensor_scalar(out=T12f, in0=T12f, scalar1=1.0, scalar2=None, op0=ALU.add)
                tD = P.mark("dve", nc.vector.reciprocal(out=T12f, in_=T12f))
                P.wait_tok("act", tD)
                for oc in range(2):
                    ch = 2 * h + oc
                    cs = slice(oc * SL, (oc + 1) * SL)
                    nc.scalar.activation(out=T3[:, cs], in_=T1[:, cs], func=ACT.Exp, scale=NSP[:, 1, d, ch:ch + 1])
                    ins = nc.scalar.activation(out=T1[:, cs], in_=T1[:, cs], func=ACT.Exp, scale=NSP[:, 0, d, ch:ch + 1])
                tA2 = P.mark("act", ins)
                P.wait_tok("dve", tA2)
                nc.vector.tensor_scalar(out=T3, in0=T3, scalar1=-1.0, scalar2=1.0, op0=ALU.mult, op1=ALU.add)
                nc.vector.tensor_scalar(out=T3, in0=T3, scalar1=1e-30, scalar2=None, op0=ALU.max)
                T2v = T2.rearrange("p (a b) -> p a b", a=2)
                tD2 = P.mark("dve", nc.vector.tensor_tensor(out=T2v, in0=T2v, in1=src_f32, op=ALU.mult))
                P.wait_tok("act", tD2)
                nc.scalar.activation(out=T3, in_=T3, func=ACT.Ln)
                tA3 = P.mark("act", nc.scalar.activation(out=T3, in_=T3, func=ACT.Exp, scale=0.5))
                P.wait_tok("dve", tA3)
                nc.vector.tensor_tensor(out=T2, in0=T2, in1=T3, op=ALU.mult)
                ins = None
                for oc in range(2):
                    cs = slice(oc * SL, (oc + 1) * SL)
                    ins = nc.vector.tensor_tensor_scan(out=out_fn(oc), data0=T1[:, cs], data1=T2[:, cs], initial=hin_fn(oc), op0=ALU.mult, op1=ALU.add)
                st["prev"] = P.mark("dve", ins)
                P.fence("dve", st["prev"])
                return st["prev"]

            def flip_block(src128, waits):
                b = P.bank()
                tp = P.transpose(P.ps[:, b, 0:128], src128, IDENT, [b], waits=waits)
                P.wait_tok("act", tp)
                P.wait_tok("act", st.get("tt"))
                ins = nc.scalar.activation(out=TT, in_=P.ps[:, b, 0:128], func=ACT.Copy)
                ta = P.mark("act", ins)
                P.bank_free[b] = ta
                b2 = P.bank()
                tp2 = P.group(P.ps[:, b2, 0:128], [(TT, JREV)], [b2], waits=[ta])
                st["tt"] = tp2
                return b2, tp2

            for h in range(8):
                tx = [P.load_tile(w_odin[16 + 2 * h + cc]) for cc in range(2)]
                tl = P.load_tile(w_lru[h])
                tg = [P.load_tile(w_odin[2 * h + cc]) for cc in range(2)]
                to = [P.load_tile(w_odout[2 * h])]
                apx = [P.tile_ap(t).rearrange("p (k n) -> p k n", k=16) for t in tx]
                apg = [P.tile_ap(t).rearrange("p (k n) -> p k n", k=16) for t in tg]
                apl = P.tile_ap(tl).rearrange("p (w d k n) -> p w d k n", w=2, d=2, k=2)
                xc_tok = None
                for s in range(NSL):
                    ts = slice(s * SL, (s + 1) * SL)
                    for cc in range(2):
                        ch = 2 * h + cc
                        b = P.bank()
                        tp = P.group(P.ps[:, b, 0:SL], [(apx[cc][:, kc, :], HM[:, kc, ts]) for kc in range(16)], [b], tiles=[tx[cc]])
                        P.wait_tok("dve", tp)
                        xs = P.ps[:, b, 0:SL]
                        xc = XC[:, cc, ts]
                        nc.vector.tensor_scalar(out=xc, in0=xs, scalar1=cw[:, 2, ch:ch + 1], scalar2=cb[:, ch:ch + 1], op0=ALU.mult, op1=ALU.add)
                        nc.vector.tensor_tensor(out=T1[:, 2:SL], in0=xs[:, 0:SL - 2], in1=CVM[:, 0, 2:SL], op=ALU.mult)
                        nc.vector.scalar_tensor_tensor(out=xc[:, 2:SL], in0=T1[:, 2:SL], scalar=cw[:, 0, ch:ch + 1], in1=xc[:, 2:SL], op0=ALU.mult, op1=ALU.add)
                        nc.vector.tensor_tensor(out=T1[:, 1:SL], in0=xs[:, 0:SL - 1], in1=CVM[:, 1, 1:SL], op=ALU.mult)
                        nc.vector.scalar_tensor_tensor(out=xc[:, 1:SL], in0=T1[:, 1:SL], scalar=cw[:, 1, ch:ch + 1], in1=xc[:, 1:SL], op0=ALU.mult, op1=ALU.add)
                        nc.vector.tensor_tensor(out=T1[:, 0:SL - 1], in0=xs[:, 1:SL], in1=CVM[:, 2, 0:SL - 1], op=ALU.mult)
                        ins = nc.vector.scalar_tensor_tensor(out=xc[:, 0:SL - 1], in0=T1[:, 0:SL - 1], scalar=cw[:, 3, ch:ch + 1], in1=xc[:, 0:SL - 1], op0=ALU.mult, op1=ALU.add)
                        P.bank_free[b] = P.mark("dve", ins)
                        ins = nc.vector.tensor_copy(out=XCb[:, cc, ts], in_=xc)
                        xc_tok = P.mark("dve", ins)
                P.release(tx[0]); P.release(tx[1])
                if h == 0:
                    dump(XC[:].rearrange("p a t -> p (a t)"))
                to.append(P.load_tile(w_odout[2 * h + 1]))
                apo = [P.tile_ap(t) for t in to]
                P.barrier()
                for s in range(NSL):
                    ts = slice(s * SL, (s + 1) * SL)
                    for oc in range(2):
                        ch = 2 * h + oc
                        if s == 0:
                            nc.vector.tensor_copy(out=HIN[:, oc:oc + 1], in_=LH0[:, 0, ch:ch + 1])
                        else:
                            nc.vector.tensor_scalar(out=HIN[:, oc:oc + 1], in0=HS[:, oc, s * SL - 1:s * SL], scalar1=CARRY, scalar2=None, op0=ALU.mult)
                    lru_pair(apl, tl, 0, h, XC[:, :, ts], lambda kc: XCb[:, kc, ts], lambda oc: HS[:, oc, ts], lambda oc: HIN[:, oc:oc + 1], xc_tok)
                    for oc in range(2):
                        ch = 2 * h + oc
                        nc.vector.tensor_copy(out=LRUST[:, s, 0, ch:ch + 1], in_=HS[:, oc, (s + 1) * SL - 1:(s + 1) * SL])
                if h == 0:
                    dump(HS[:].rearrange("p a t -> p (a t)"))
                for s in range(NSL - 1, -1, -1):
                    ts = slice(s * SL, (s + 1) * SL)
                    P.barrier()
                    ftok = None
                    for cc in range(2):
                        for hb in range(2):
                            b2, tp2 = flip_block(XC[:, cc, s * SL + hb * 128:s * SL + (hb + 1) * 128], [])
                            P.wait_tok("act", tp2)
                            dst = slice((1 - hb) * 128, (2 - hb) * 128)
                            nc.scalar.activation(out=XF[:, cc, dst], in_=P.ps[:, b2, 0:128], func=ACT.Copy)
                            ins = nc.scalar.activation(out=XFb[:, cc, dst], in_=P.ps[:, b2, 0:128], func=ACT.Copy)
                            ftok = P.mark("act", ins)
                            P.bank_free[b2] = ftok
                    P.wait_tok("dve", ftok)
                    if h == 0 and s == NSL - 2:
                        dump(XF[:].rearrange("p a t -> p (a t)"))
                    for oc in range(2):
                        ch = 2 * h + oc
                        if s == NSL - 1:
                            nc.vector.tensor_copy(out=HIN[:, 2 + oc:3 + oc], in_=LH0[:, 1, ch:ch + 1])
                        else:
                            nc.vector.tensor_scalar(out=HIN[:, 2 + oc:3 + oc], in0=PEB[:, oc:oc + 1], scalar1=CARRY, scalar2=None, op0=ALU.mult)
                    P.wait_tok("dve", P.last_pe)
                    lru_pair(apl, tl, 1, h, XF[:, :, :], lambda kc: XFb[:, kc, :], lambda oc: T4[:, oc, :], lambda oc: HIN[:, 2 + oc:3 + oc], ftok)
                    tk = None
                    for oc in range(2):
                        ch = 2 * h + oc
                        nc.vector.tensor_copy(out=PEB[:, oc:oc + 1], in_=T4[:, oc, SL - 1:SL])
                        tk = P.mark("dve", nc.vector.tensor_copy(out=LRUST[:, s, 1, ch:ch + 1], in_=T4[:, oc, SL - 1:SL]))
                    for oc in range(2):
                        for hb in range(2):
                            b2, tp2 = flip_block(T4[:, oc, hb * 128:(hb + 1) * 128], [tk])
                            P.wait_tok("dve", tp2)
                            dst = slice(s * SL + (1 - hb) * 128, s * SL + (2 - hb) * 128)
                            ins = nc.vector.tensor_tensor(out=HS[:, oc, dst], in0=HS[:, oc, dst], in1=P.ps[:, b2, 0:128], op=ALU.add)
                            P.bank_free[b2] = P.mark("dve", ins)
                P.release(tl)
                P.barrier()
                if h == 0:
                    dump(HS[:].rearrange("p a t -> p (a t)"))
                ytok = None
                T1g = T1[:, 0:SL]
                for s in range(NSL):
                    ts = slice(s * SL, (s + 1) * SL)
                    for cc in range(2):
                        b = P.bank()
                        tp = P.group(P.ps[:, b, 0:SL], [(apg[cc][:, kc, :], HM[:, kc, ts]) for kc in range(16)], [b], tiles=[tg[cc]])
                        gp = P.ps[:, b, 0:SL]
                        P.wait_tok("act", tp)
                        P.wait_tok("act", ytok)
                        ins = nc.scalar.activation(out=T1g, in_=gp, func=ACT.Square)
                        tA = P.mark("act", ins)
                        P.wait_tok("dve", tA)
                        nc.vector.tensor_scalar(out=T1g, in0=T1g, scalar1=0.044715, scalar2=1.0, op0=ALU.mult, op1=ALU.add)
                        ins = nc.vector.tensor_tensor(out=T1g, in0=T1g, in1=gp, op=ALU.mult)
                        tD = P.mark("dve", ins)
                        P.wait_tok("act", tD)
                        ins = nc.scalar.activation(out=T1g, in_=T1g, func=ACT.Exp, scale=-GELU_C)
                        tA2 = P.mark("act", ins)
                        P.wait_tok("dve", tA2)
                        nc.vector.tensor_scalar(out=T1g, in0=T1g, scalar1=1.0, scalar2=None, op0=ALU.add)
                        nc.vector.reciprocal(out=T1g, in_=T1g)
                        nc.vector.tensor_tensor(out=T1g, in0=T1g, in1=gp, op=ALU.mult)
                        ins = nc.vector.tensor_tensor(out=Y[:, cc, ts], in0=T1g, in1=HS[:, cc, ts], op=ALU.mult)
                        ytok = P.mark("dve", ins)
                        P.bank_free[b] = ytok
                P.release(tg[0]); P.release(tg[1])
                for oc in range(NCH):
                    for tb in range(NTB):
                        ts = slice(tb * TB, (tb + 1) * TB)
                        b = P.bank()
                        tp = P.group(P.ps[:, b, :], [(apo[kc][:, oc * 128:(oc + 1) * 128], Y[:, kc, ts]) for kc in range(2)], [b], tiles=to, waits=[ytok])
                        P.wait_tok("dve", tp)
                        ins = nc.vector.scalar_tensor_tensor(out=XT[:, oc, ts], in0=P.ps[:, b, :], scalar=G2[:, oc:oc + 1], in1=XT[:, oc, ts], op0=ALU.mult, op1=ALU.add)
                        P.bank_free[b] = P.mark("dve", ins)
                P.release(to[0]); P.release(to[1])
                P.barrier()

        def gla_part(G2):
            P.barrier()
            AR.reset()
            IDENT = ccol("ident", 128)
            MASK = [ccol("maskf", 128), ccol("maskb", 128)]
            RM = ccol("rm", SL)
            CARRY = pcol("carry", 1)
            gng = vcol("gla_ng", 0, 2)
            gb = vcol("gla_gb", 0, 8)
            GL = AR.bf16(2, NT)
            GW2 = AR.bf16(2, 512)
            NGB = AR.f32(8)
            OS = AR.f32(2, NT)
            GS0 = AR.f32(2, SL)
            mark0 = AR.off
            QT = AR.f32(SL); KT = AR.f32(SL); VT = AR.bf16(2, SL)
            LA = AR.f32(SL); PP = AR.f32(SL); TP = AR.f32(SL); PB = AR.f32(SL)
            E1 = AR.f32(SL); E2 = AR.f32(SL); E3 = AR.f32(SL)
            QTd = AR.bf16(SL); KTd = AR.bf16(SL); KEd = AR.f32(SL)
            AT = AR.bf16(2, 128); KET = AR.bf16(2, 128)
            S = AR.f32(SL); Sb = AR.bf16(4, SL); SST = AR.f32(SL); DEC = AR.f32(4)
            end1 = AR.off
            AR.off = mark0
            GG = AR.bf16(2, NT); SQ = AR.bf16(2, TB); RR = AR.f32(TB); T5 = AR.f32(TB); EG = AR.f32(TB)
            assert AR.off <= ARENA_F32 and end1 <= ARENA_F32
            P.pool_sync()
            dgw = P.dma("pool", GW2[0:16, :, :].rearrange("p a b -> p (a b)"), gw2_in[:, :], "gw2")
            P.wait_dma("pe", dgw)
            nc.vector.tensor_scalar(out=NGB, in0=gb, scalar1=-1.0, scalar2=None, op0=ALU.mult)
            tglr = P.load_tile(w_evin[32])
            apglr = P.tile_ap(tglr).rearrange("p (k n) -> p k n", k=16)
            gl_tok = None
            for d in range(2):
                for tb in range(NTB):
                    ts = slice(tb * TB, (tb + 1) * TB)
                    b = P.bank()
                    tp = P.group(P.ps[0:16, b, :], [(apglr[:, kc, d * 16:(d + 1) * 16], HM[:, kc, ts]) for kc in range(16)], [b], tiles=[tglr])
                    P.wait_tok("act", tp)
                    ins = nc.scalar.activation(out=GL[0:16, d, ts], in_=P.ps[0:16, b, :], func=ACT.Copy)
                    gl_tok = P.mark("act", ins)
                    P.bank_free[b] = gl_tok
            P.release(tglr)
            sst_dma = None
            import os
            STOP = int(os.environ.get("GLA_STOP", "99"))
            for hd in range(4 if STOP > 1 else 0):
                P.barrier()
                tq = P.load_tile(w_evin[8 + hd])
                tk = P.load_tile(w_evin[12 + hd])
                tv = [P.load_tile(w_evin[16 + 2 * hd + i]) for i in range(2)]
                apq = P.tile_ap(tq).rearrange("p (k n) -> p k n", k=16)
                apk = P.tile_ap(tk).rearrange("p (k n) -> p k n", k=16)
                apv = [P.tile_ap(t).rearrange("p (k n) -> p k n", k=16) for t in tv]
                P.sp_sync()
                dgs = P.dma("sp", GS0[:].rearrange("p a b -> p (a b)"), gs0_in[hd], "gs0")
                for d in range(2 if STOP > 2 else 0):
                    order = list(range(NSL)) if d == 0 else list(range(NSL - 1, -1, -1))
                    for si, s in enumerate(order):
                        ts = slice(s * SL, (s + 1) * SL)
                        P.barrier()
                        b = P.bank()
                        tp = P.group(P.ps[:, b, 0:SL], [(apq[:, kc, :], HM[:, kc, ts]) for kc in range(16)], [b], tiles=[tq])
                        P.wait_tok("act", tp)
                        P.bank_free[b] = P.mark("act", nc.scalar.activation(out=QT, in_=P.ps[:, b, 0:SL], func=ACT.Copy, scale=128.0 ** -0.5))
                        b = P.bank()
                        tp = P.group(P.ps[:, b, 0:SL], [(apk[:, kc, :], HM[:, kc, ts]) for kc in range(16)], [b], tiles=[tk])
                        P.wait_tok("act", tp)
                        P.bank_free[b] = P.mark("act", nc.scalar.activation(out=KT, in_=P.ps[:, b, 0:SL], func=ACT.Copy))
                        vt_tok = None
                        for tt in range(2):
                            b = P.bank()
                            tsl = slice(s * SL + tt * 128, s * SL + (tt + 1) * 128)
                            for vc in range(2):
                                tp = P.group(P.ps[:, b, vc * 128:(vc + 1) * 128], [(HM[:, kc, tsl], apv[vc][:, kc, :]) for kc in range(16)],
                                             [b] if vc == 0 else [], tiles=[tv[vc]])
                            P.wait_tok("act", tp)
                            vt_tok = P.mark("act", nc.scalar.activation(out=VT[:, tt, :], in_=P.ps[:, b, 0:SL], func=ACT.Copy))
                            P.bank_free[b] = vt_tok
                        if STOP <= 3:
                            continue
                        b = P.bank()
                        tp = P.group(P.ps[:, b, 0:SL], [(GW2[0:16, d, hd * 128:(hd + 1) * 128], GL[0:16, d, ts])], [b], waits=[gl_tok])
                        P.wait_tok("act", tp)
                        nc.scalar.activation(out=E1, in_=P.ps[:, b, 0:SL], func=ACT.Exp, scale=-1.0, bias=NGB[:, d * 4 + hd:d * 4 + hd + 1])
                        tA = P.mark("act", nc.scalar.activation(out=LA, in_=E1, func=ACT.Ln, bias=1.0))
                        P.bank_free[b] = tA
                        P.wait_tok("dve", tA)
                        tsc = P.mark("dve", nc.vector.tensor_tensor_scan(out=PP, data0=RM, data1=LA, initial=0.0, op0=ALU.mult, op1=ALU.add))
                        P.fence("dve", tsc)
                        TOT = PP[:, 63::64]
                        nc.vector.tensor_tensor(out=TP.rearrange("p (c t) -> p c t", c=4), in0=PP.rearrange("p (c t) -> p c t", c=4),
                                                in1=TOT.unsqueeze(2).to_broadcast([128, 4, 64]), op=ALU.subtract)
                        if d == 0:
                            ins = nc.vector.tensor_copy(out=PB, in_=PP)
                            sc = (-1.0 / 16, 1.0 / 16, 1.0 / 16)
                        else:
                            nc.vector.scalar_tensor_tensor(out=PB, in0=TP, scalar=-1.0, in1=LA, op0=ALU.mult, op1=ALU.add)
                            ins = nc.vector.tensor_tensor(out=TP, in0=PP, in1=LA, op=ALU.subtract)
                            sc = (-1.0 / 16, 1.0 / 16, -1.0 / 16)
                        tD = P.mark("dve", ins)
                        P.wait_tok("act", tD)
                        nc.scalar.activation(out=E1, in_=PB, func=ACT.Exp, scale=sc[0])
                        nc.scalar.activation(out=E2, in_=PB, func=ACT.Exp, scale=sc[1])
                        nc.scalar.activation(out=E3, in_=TP, func=ACT.Exp, scale=sc[2])
                        tA2 = P.mark("act", nc.scalar.activation(out=DEC, in_=TOT, func=ACT.Exp, scale=-1.0 / 16))
                        P.wait_tok("dve", tA2)
                        nc.vector.tensor_tensor(out=QTd, in0=QT, in1=E1, op=ALU.mult)
                        nc.vector.tensor_tensor(out=KTd, in0=KT, in1=E2, op=ALU.mult)
                        tD2 = P.mark("dve", nc.vector.tensor_tensor(out=KEd, in0=KT, in1=E3, op=ALU.mult))
                        if STOP <= 4:
                            continue
                        ba = P.bank()
                        for pp in range(2):
                            tl_ = slice(pp * 128, (pp + 1) * 128)
                            tp = P.group(P.ps[:, ba, tl_], [(KTd[:, tl_], QTd[:, tl_])], [ba] if pp == 0 else [], waits=[tD2])
                        P.wait_tok("dve", tp)
                        for pp in range(2):
                            ins = nc.vector.tensor_tensor(out=AT[:, pp, :], in0=P.ps[:, ba, pp * 128:(pp + 1) * 128], in1=MASK[d], op=ALU.mult)
                        tAT = P.mark("dve", ins)
                        P.bank_free[ba] = tAT
                        if STOP == 5 and os.environ.get("GLA_SUB") == "a":
                            continue
                        for pp in range(2):
                            bt = P.bank()
                            tp = P.transpose(P.ps[:, bt, 0:128], KEd[:, pp * 128:(pp + 1) * 128], IDENT, [bt], waits=[tD2])
                            P.wait_tok("act", tp)
                            tKET = P.mark("act", nc.scalar.activation(out=KET[:, pp, :], in_=P.ps[:, bt, 0:128], func=ACT.Copy))
                            P.bank_free[bt] = tKET
                        if STOP <= 5:
                            continue
                        bk = [P.bank() for _ in range(4)]
                        for n in range(4):
                            pp, half = n // 2, n % 2
                            rows = slice(half * 64, half * 64 + 64)
                            tp = P.group(P.ps[:, bk[n], 0:256], [(KET[rows, pp, :], VT[rows, pp, :])],
                                         [bk[n]], waits=[tKET, vt_tok])
                        P.wait_tok("dve", tp)
                        P.wait_dma("dve", dgs)
                        if si == 0:
                            nc.vector.tensor_copy(out=S, in_=GS0[:, d, :])
                        else:
                            nc.vector.tensor_scalar(out=S, in0=S, scalar1=CARRY, scalar2=None, op0=ALU.mult)
                        corder = [0, 1, 2, 3] if d == 0 else [3, 2, 1, 0]
                        for n in corder:
                            pp, half = n // 2, n % 2
                            nc.vector.tensor_copy(out=Sb[:, n, :], in_=S)
                            ins = nc.vector.scalar_tensor_tensor(out=S, in0=S, scalar=DEC[:, n:n + 1], in1=P.ps[:, bk[n], 0:256], op0=ALU.mult, op1=ALU.add)
                        tS = P.mark("dve", ins)
                        for n in range(4):
                            P.bank_free[bk[n]] = tS
                        if sst_dma is not None:
                            P.wait_dma("dve", sst_dma)
                        tSS = P.mark("dve", nc.vector.tensor_copy(out=SST, in_=S))
                        P.wait_tok("sp", tSS)
                        sst_dma = P.dma("sp", gla_out[(s * 2 + d) * 4 + hd], SST, "sst")
                        if STOP <= 6:
                            continue
                        bo = P.bank()
                        first = True
                        for pp in range(2):
                            for vc in range(2):
                                col0 = (pp * 2 + vc) * 128
                                vs = slice(vc * 128, (vc + 1) * 128)
                                items = [(P.ps[:, bo, col0:col0 + 128], VT[:, pp, vs], AT[:, pp, :])]
                                for half in range(2):
                                    n = pp * 2 + half
                                    items.append((P.ps[:, bo, col0 + half * 64:col0 + (half + 1) * 64], Sb[:, n, vs], QTd[:, n * 64:(n + 1) * 64]))
                                tp = P.group2(items, [bo] if first else [], waits=[tAT, tS, vt_tok])
                                first = False
                        P.wait_tok("dve", tp)
                        for pp in range(2):
                            for vc in range(2):
                                col0 = (pp * 2 + vc) * 128
                                dst = OS[:, vc, s * SL + pp * 128:s * SL + (pp + 1) * 128]
                                if d == 0:
                                    ins = nc.vector.tensor_copy(out=dst, in_=P.ps[:, bo, col0:col0 + 128])
                                else:
                                    ins = nc.vector.tensor_tensor(out=dst, in0=dst, in1=P.ps[:, bo, col0:col0 + 128], op=ALU.add)
                        P.bank_free[bo] = P.mark("dve", ins)
                for t in (tq, tk, tv[0], tv[1]):
                    P.release(t)
                P.barrier()
                tg = [P.load_tile(w_evin[24 + 2 * hd + i]) for i in range(2)]
                to = [P.load_tile(w_evout[8 + 2 * hd + i]) for i in range(2)]
                apg = [P.tile_ap(t).rearrange("p (k n) -> p k n", k=16) for t in tg]
                apo = [P.tile_ap(t) for t in to]
                ogt = None
                for tb in range(NTB):
                    ts = slice(tb * TB, (tb + 1) * TB)
                    for vc in range(2):
                        b = P.bank()
                        tp = P.group(P.ps[:, b, :], [(apg[vc][:, kc, :], HM[:, kc, ts]) for kc in range(16)], [b], tiles=[tg[vc]])
                        P.wait_tok("act", tp)
                        P.wait_tok("act", ogt)
                        tA = P.mark("act", nc.scalar.activation(out=EG, in_=P.ps[:, b, :], func=ACT.Exp, scale=-1.0))
                        P.wait_tok("dve", tA)
                        nc.vector.tensor_scalar(out=EG, in0=EG, scalar1=1.0, scalar2=None, op0=ALU.add)
                        nc.vector.reciprocal(out=EG, in_=EG)
                        ins = nc.vector.tensor_tensor(out=GG[:, vc, ts], in0=EG, in1=P.ps[:, b, :], op=ALU.mult)
                        ogt = P.mark("dve", ins)
                        P.bank_free[b] = ogt
                    P.wait_tok("act", ogt)
                    for vc in range(2):
                        ins = nc.scalar.activation(out=SQ[:, vc, :], in_=OS[:, vc, ts], func=ACT.Square)
                    tSq = P.mark("act", ins)
                    b = P.bank()
                    tp = P.group(P.ps[:, b, :], [(ONES[:, :], SQ[:, vc, :]) for vc in range(2)], [b], waits=[tSq])
                    P.wait_tok("act", tp)
                    nc.scalar.activation(out=RR, in_=P.ps[:, b, :], func=ACT.Ln, scale=1.0 / 256, bias=EPSB[:, 0:1])
                    tR = P.mark("act", nc.scalar.activation(out=RR, in_=RR, func=ACT.Exp, scale=-0.5))
                    P.bank_free[b] = tR
                    P.wait_tok("dve", tR)
                    for vc in range(2):
                        nc.vector.scalar_tensor_tensor(out=T5, in0=OS[:, vc, ts], scalar=gng[:, vc:vc + 1], in1=RR, op0=ALU.mult, op1=ALU.mult)
                        ins = nc.vector.tensor_tensor(out=GG[:, vc, ts], in0=T5, in1=GG[:, vc, ts], op=ALU.mult)
                    ogt = P.mark("dve", ins)
                P.release(tg[0]); P.release(tg[1])
                for oc in range(NCH):
                    for tb in range(NTB):
                        ts = slice(tb * TB, (tb + 1) * TB)
                        b = P.bank()
                        tp = P.group(P.ps[:, b, :], [(apo[kc][:, oc * 128:(oc + 1) * 128], GG[:, kc, ts]) for kc in range(2)], [b], tiles=to, waits=[ogt])
                        P.wait_tok("dve", tp)
                        ins = nc.vector.scalar_tensor_tensor(out=XT[:, oc, ts], in0=P.ps[:, b, :], scalar=G2[:, oc:oc + 1], in1=XT[:, oc, ts], op0=ALU.mult, op1=ALU.add)
                        P.bank_free[b] = P.mark("dve", ins)
                P.release(to[0]); P.release(to[1])
            P.barrier()
            if sst_dma is not None:
                for e in ("dve", "act", "sp"):
                    P.wait_dma(e, sst_dma)

        def s5_part(G2):
            PI = math.pi
            P.barrier()
            AR.reset()
            Bw = AR.bf16(2, 2, 8 * 128)
            Cw = AR.bf16(2, 2, 32 * 32)
            COEF = AR.f32(2, 2, 64)
            S5H0 = AR.f32(2, 64)
            S5ST = AR.f32(NSL * 2, 64)
            UT = AR.bf16(8, NT)
            ut_off = AR.off - 4096
            RT = Arena(P.ring[:].rearrange("p s n -> p (s n)").bitcast(F32))
            RTN = NSLOT * TSZ // 2
            mA = ccol("s5ma", 2)
            mB = ccol("s5mb", 2)
            CARRY = pcol("carry", 1)
            P.sp_sync()
            d0 = P.dma("sp", S5H0[:].rearrange("p a b -> p (a b)"), s5h0_in[:, :], "s5h0")

            def alloc(n):
                v = RT.ap[:, RT.off:RT.off + n]
                RT.off += n
                assert RT.off <= RTN, "ring temp overflow"
                return v

            def coefs(lre, lim, ls, n, want_f):
                DT = alloc(n); ZR = alloc(n); ZI = alloc(n); MAG = alloc(n); KF = alloc(n); KI = alloc(n).bitcast(I32)
                W = alloc(n); M = alloc(n); SN = alloc(n); CS = alloc(n)
                tA = P.mark("act", nc.scalar.activation(out=DT, in_=ls, func=ACT.Exp))
                P.wait_tok("dve", tA)
                nc.vector.tensor_tensor(out=ZR, in0=lre, in1=DT, op=ALU.mult)
                tD = P.mark("dve", nc.vector.tensor_tensor(out=ZI, in0=lim, in1=DT, op=ALU.mult))
                P.wait_tok("act", tD)
                tM = P.mark("act", nc.scalar.activation(out=MAG, in_=ZR, func=ACT.Exp))
                res = {}
                for name, shift, dst in (("sin", 0.0, SN), ("cos", PI / 2, CS)):
                    src = ZI
                    if shift:
                        nc.vector.tensor_scalar(out=DT, in0=ZI, scalar1=shift, scalar2=None, op0=ALU.add)
                        src = DT
                    nc.vector.tensor_scalar(out=KF, in0=src, scalar1=1.0 / (2 * PI), scalar2=None, op0=ALU.mult)
                    nc.vector.tensor_copy(out=KI, in_=KF)
                    nc.vector.tensor_copy(out=KF, in_=KI)
                    nc.vector.scalar_tensor_tensor(out=W, in0=KF, scalar=-2 * PI, in1=src, op0=ALU.mult, op1=ALU.add)
                    nc.vector.tensor_scalar(out=M, in0=W, scalar1=PI, scalar2=None, op0=ALU.is_gt)
                    nc.vector.scalar_tensor_tensor(out=W, in0=M, scalar=-2 * PI, in1=W, op0=ALU.mult, op1=ALU.add)
                    nc.vector.tensor_scalar(out=M, in0=W, scalar1=-PI, scalar2=None, op0=ALU.is_lt)
                    nc.vector.scalar_tensor_tensor(out=W, in0=M, scalar=2 * PI, in1=W, op0=ALU.mult, op1=ALU.add)
                    nc.vector.tensor_scalar(out=W, in0=W, scalar1=PI, scalar2=-PI, op0=ALU.min, op1=ALU.max)
                    tW = P.mark("dve", nc.vector.tensor_copy(out=M, in_=W))
                    P.wait_tok("act", tW)
                    tS = P.mark("act", nc.scalar.activation(out=dst, in_=M, func=ACT.Sin))
                    P.wait_tok("dve", tS)
                P.wait_tok("dve", tM)
                ABR = alloc(n); ABI = alloc(n)
                nc.vector.tensor_tensor(out=ABR, in0=MAG, in1=CS, op=ALU.mult)
                nc.vector.tensor_tensor(out=ABI, in0=MAG, in1=SN, op=ALU.mult)
                res["ab_re"], res["ab_im"] = ABR, ABI
                if want_f:
                    NRE = KF; DEN = W; FRE = alloc(n); FIM = alloc(n)
                    nc.vector.tensor_scalar(out=NRE, in0=ABR, scalar1=-1.0, scalar2=None, op0=ALU.add)
                    nc.vector.tensor_tensor(out=DEN, in0=lre, in1=lre, op=ALU.mult)
                    nc.vector.tensor_tensor(out=M, in0=lim, in1=lim, op=ALU.mult)
                    nc.vector.tensor_tensor(out=DEN, in0=DEN, in1=M, op=ALU.add)
                    nc.vector.reciprocal(out=DEN, in_=DEN)
                    nc.vector.tensor_tensor(out=FRE, in0=NRE, in1=lre, op=ALU.mult)
                    nc.vector.tensor_tensor(out=M, in0=ABI, in1=lim, op=ALU.mult)
                    nc.vector.tensor_tensor(out=FRE, in0=FRE, in1=M, op=ALU.add)
                    nc.vector.tensor_tensor(out=FRE, in0=FRE, in1=DEN, op=ALU.mult)
                    nc.vector.tensor_tensor(out=FIM, in0=ABI, in1=lre, op=ALU.mult)
                    nc.vector.tensor_tensor(out=M, in0=NRE, in1=lim, op=ALU.mult)
                    nc.vector.tensor_tensor(out=FIM, in0=FIM, in1=M, op=ALU.subtract)
                    nc.vector.tensor_tensor(out=FIM, in0=FIM, in1=DEN, op=ALU.mult)
                    res["f_re"], res["f_im"] = FRE, FIM
                return res

            RT.off = 0
            LB = alloc(192)
            CB = alloc(2048)
            dl = P.dma("sp", LB, s5lb_in[:, :], "s5l")
            dc = P.dma("sp", CB, s5cb_in[:, :], "s5l")
            for e in ("act", "dve"):
                P.wait_dma(e, dc)
            LBv = LB.rearrange("p (k n) -> p k n", k=3)
            r = coefs(LBv[:, 0, :], LBv[:, 1, :], LBv[:, 2, :], 64, False)
            for d in range(2):
                ar = r["ab_re"][:, d * 32:(d + 1) * 32]
                ai = r["ab_im"][:, d * 32:(d + 1) * 32]
                nc.vector.tensor_copy(out=COEF[:, d, 0, 0:32], in_=ar)
                nc.vector.tensor_copy(out=COEF[:, d, 0, 32:64], in_=ar)
                nc.vector.tensor_scalar(out=COEF[:, d, 1, 0:32], in0=ai, scalar1=-1.0, scalar2=None, op0=ALU.mult)
                nc.vector.tensor_copy(out=COEF[:, d, 1, 32:64], in_=ai)
            CBv = CB.rearrange("p (c d q n) -> p c d q n", c=2, d=2, q=32)
            for d in range(2):
                for c in range(2):
                    for g2 in range(2):
                        dst = Cw[:, d, c, :].rearrange("p (q m) -> p q m", q=32)[:, :, g2 * 16:(g2 + 1) * 16]
                        nc.vector.tensor_scalar(out=dst, in0=CBv[:, c, d, :, :], scalar1=mB[:, g2:g2 + 1], scalar2=(1.0 if c == 0 else -1.0), op0=ALU.mult, op1=ALU.mult)
            for d in range(2):
              for hh in range(2):
                P.sp_sync()
                RT.off = 0
                LA_ = alloc(768)
                BA = alloc(512)
                dl = P.dma("sp", LA_.rearrange("p (k n) -> p k n", k=3), s5la_in[:, :].rearrange("p (k d h n) -> p k d h n", k=3, d=2, h=2)[:, :, d, hh, :], "s5l")
                dc = P.dma("sp", BA.rearrange("p (c n) -> p c n", c=2), s5ba_in[:, :].rearrange("p (c d h n) -> p c d h n", c=2, d=2, h=2)[:, :, d, hh, :], "s5l")
                for e in ("act", "dve"):
                    P.wait_dma(e, dc)
                LAv = LA_.rearrange("p (k n) -> p k n", k=3)
                r = coefs(LAv[:, 0, :], LAv[:, 1, :], LAv[:, 2, :], 256, True)
                BAv = BA.rearrange("p (c n) -> p c n", c=2)
                BBR = alloc(256); BBI = alloc(256); TM = alloc(256)
                nc.vector.tensor_tensor(out=BBR, in0=r["f_re"], in1=BAv[:, 0, :], op=ALU.mult)
                nc.vector.tensor_tensor(out=TM, in0=r["f_im"], in1=BAv[:, 1, :], op=ALU.mult)
                nc.vector.tensor_tensor(out=BBR, in0=BBR, in1=TM, op=ALU.subtract)
                nc.vector.tensor_tensor(out=BBI, in0=r["f_re"], in1=BAv[:, 1, :], op=ALU.mult)
                nc.vector.tensor_tensor(out=TM, in0=r["f_im"], in1=BAv[:, 0, :], op=ALU.mult)
                nc.vector.tensor_tensor(out=BBI, in0=BBI, in1=TM, op=ALU.add)
                for c, src in ((0, BBR), (1, BBI)):
                    for g2 in range(2):
                        dst = Bw[:, d, c, hh * 512:(hh + 1) * 512].rearrange("p (h m) -> p h m", h=4)[:, :, g2 * 64:(g2 + 1) * 64]
                        nc.vector.tensor_scalar(out=dst, in0=src.rearrange("p (h m) -> p h m", h=4), scalar1=mA[:, g2:g2 + 1], scalar2=None, op0=ALU.mult)
            P.barrier()
            import os
            S5STOP = int(os.environ.get("S5_STOP", "99"))
            tfree = P.mark("dve", nc.vector.memset(P.scr[:, 2:3], 0.0))
            for s_ in range(NSLOT):
                P.slot_free[s_] = tfree
            if S5STOP <= 1:
                dump(COEF[:].rearrange("p a b c -> p (a b c)"))
                dump(XT[:, 0, :])
                dump(ARN[:, 0:2048])
                return
            for ch in range(8):
                t = P.load_tile(w_evin[ch])
                ap = P.tile_ap(t).rearrange("p (k n) -> p k n", k=16)
                for tb in range(NTB):
                    ts = slice(tb * TB, (tb + 1) * TB)
                    b = P.bank()
                    tp = P.group(P.ps[:, b, :], [(ap[:, kc, :], HM[:, kc, ts]) for kc in range(16)], [b], tiles=[t])
                    P.wait_tok("act", tp)
                    P.bank_free[b] = P.mark("act", nc.scalar.activation(out=UT[:, ch, ts], in_=P.ps[:, b, :], func=ACT.Copy))
                P.release(t)
            P.barrier()
            YS = HM[:].rearrange("p c t -> p (c t)").bitcast(F32).rearrange("p (c t) -> p c t", c=8)
            sd = vcol("s5_d", 0, 8)
            for ch in range(8):
                nc.vector.tensor_scalar(out=YS[:, ch, :], in0=UT[:, ch, :], scalar1=sd[:, ch:ch + 1], scalar2=None, op0=ALU.mult)
            P.barrier()
            if S5STOP <= 2:
                return
            RT.off = 0
            Hbuf = [alloc(1024).rearrange("p (t m) -> p t m", t=16) for _ in range(2)]
            Hbb = [alloc(512).bitcast(BF16).rearrange("p (t m) -> p t m", t=16) for _ in range(2)]
            T1 = AR.f32(64); T2 = AR.f32(64); HIN = AR.f32(64)
            Cw3 = alloc(1024).bitcast(BF16).rearrange("p (d c m) -> p d c m", d=2, c=2)
            nc.vector.memset(Cw3[:].rearrange("p d c m -> p (d c m)"), 0.0)
            for d_ in range(2):
                for c_ in range(2):
                    nc.vector.tensor_copy(out=Cw3[:, d_, c_, :].rearrange("p (h m) -> p h m", h=8)[:, :, 32:64],
                                          in_=Cw[:, d_, c_, :].rearrange("p (h r m) -> p h r m", h=8, r=4)[:, :, 3, :])
            Bw3 = alloc(2048).bitcast(BF16).rearrange("p (d c m) -> p d c m", d=2, c=2)
            m96 = ccol("m96", 1)
            tb3 = P.mark("dve", nc.vector.tensor_scalar(out=Bw3[:].rearrange("p d c m -> p (d c m)"), in0=Bw[:].rearrange("p d c m -> p (d c m)"), scalar1=m96, scalar2=None, op0=ALU.mult))
            P.wait_tok("pe", tb3)
            P.wait_dma("dve", d0)
            NBLK = NT // 16
            bfree = [None, None]
            yfree = [None, None]
            hb_tok = [None, None]
            pe_y = [None, None]
            hcopy_tok = [None, None]
            pend = None
            kk = 0
            prev_tsc = None
            for d in range(2):
                A1 = COEF[:, d, 0, :]
                A2 = COEF[:, d, 1, :]
                blocks = list(range(NBLK)) if d == 0 else list(range(NBLK - 1, -1, -1))
                prev = None
                for bi, blk in enumerate(blocks):
                    t0 = blk * 16
                    par = kk % 2
                    kk += 1
                    P.wait_tok("pe", bfree[0])
                    last = None
                    for c in range(2):
                        for q in range(32):
                            rg = q % 4
                            col = c * 128 + (q // 4) * 16
                            if rg < 3:
                                rows = slice(32 * rg, 32 * rg + 32)
                                wsrc = Bw[rows, d, c, (q // 4) * 128:(q // 4 + 1) * 128]
                            else:
                                rows = slice(64, 128)
                                wsrc = Bw3[rows, d, c, (q // 4) * 128:(q // 4 + 1) * 128]
                            last = nc.tensor.matmul(P.ps[:, rg, col:col + 16], wsrc, UT[rows, q // 4, t0:t0 + 16], start=True, stop=True)
                    tE = P.mark("pe", last)
                    P.last_pe = tE
                    H = Hbuf[par]
                    P.wait_tok("act", tE)
                    P.wait_tok("act", prev_tsc)
                    tB = None
                    for rg in range(4):
                        o_ap = H[:].rearrange("p t (c q r) -> p t c q r", c=2, r=4)[:, :, :, :, rg]
                        i_ap = P.ps[:, rg, 0:256].rearrange("p (c q t) -> p t c q", c=2, q=8)
                        tB = nc.scalar.activation(out=o_ap, in_=i_ap, func=ACT.Copy)
                    tB = P.mark("act", tB)
                    bfree[0] = tB
                    P.wait_tok("dve", tB)
                    steps = list(range(16)) if d == 0 else list(range(15, -1, -1))
                    slot_first = (t0 % SL == 0) if d == 0 else ((t0 + 16) % SL == 0)
                    slot_last = ((t0 + 16) % SL == 0) if d == 0 else (t0 % SL == 0)
                    if slot_first:
                        if bi == 0:
                            nc.vector.tensor_copy(out=HIN, in_=S5H0[:, d, :])
                        else:
                            nc.vector.tensor_scalar(out=HIN, in0=prev, scalar1=CARRY, scalar2=None, op0=ALU.mult)
                        prev = HIN
                    ins = None
                    for t in steps:
                        nc.vector.tensor_tensor(out=T1, in0=A1, in1=prev, op=ALU.mult)
                        nc.vector.tensor_tensor(out=T2[:, 0:32], in0=A2[:, 0:32], in1=prev[:, 32:64], op=ALU.mult)
                        nc.vector.tensor_tensor(out=T2[:, 32:64], in0=A2[:, 32:64], in1=prev[:, 0:32], op=ALU.mult)
                        nc.vector.tensor_tensor(out=T1, in0=T1, in1=T2, op=ALU.add)
                        ins = nc.vector.tensor_tensor(out=H[:, t, :], in0=T1, in1=H[:, t, :], op=ALU.add)
                        prev = H[:, t, :]
                    tSc = P.mark("dve", ins)
                    prev_tsc = tSc
                    if slot_last:
                        s_idx = t0 // SL
                        nc.vector.tensor_copy(out=S5ST[:, s_idx * 2 + d, :], in_=prev)
                    if pend is not None:
                        pb, pt0 = pend
                        P.wait_tok("dve", pe_y[pb])
                        dst = YS[:, :, pt0:pt0 + 16]
                        ins = nc.vector.tensor_tensor(out=dst, in0=dst, in1=P.ps[:, 4 + pb, 0:128].rearrange("p (c t) -> p c t", c=8), op=ALU.add)
                        yfree[pb] = P.mark("dve", ins)
                    if S5STOP <= 3:
                        continue
                    P.wait_tok("act", tSc)
                    P.wait_tok("act", pe_y[par])
                    hcopy_tok[par] = P.mark("act", nc.scalar.activation(out=Hbb[par][:].rearrange("p t m -> p (t m)"), in_=H[:].rearrange("p t m -> p (t m)"), func=ACT.Copy))
                    P.wait_tok("pe", hcopy_tok[par])
                    P.wait_tok("pe", yfree[par])
                    last = None
                    for ch in range(8):
                        for rg in (3, 0, 1, 2):
                            q = 4 * ch + rg
                            for c in range(2):
                                if rg < 3:
                                    o_ap = P.ps[32 * rg:32 * rg + 32, 4 + par, ch * 16:ch * 16 + 16]
                                    w_ap = Cw[:, d, c, q * 32:(q + 1) * 32]
                                else:
                                    o_ap = P.ps[64:128, 4 + par, ch * 16:ch * 16 + 16]
                                    w_ap = Cw3[:, d, c, ch * 64:(ch + 1) * 64]
                                last = nc.tensor.matmul(o_ap, w_ap, Hbb[par][:, :, c * 32 + q], start=(c == 0), stop=(c == 1))
                    pe_y[par] = P.mark("pe", last)
                    P.last_pe = pe_y[par]
                    pend = (par, t0)
            if S5STOP <= 3:
                P.barrier()
                for b in range(8):
                    P.bank_free[b] = None
                return
            pb, pt0 = pend
            P.wait_tok("dve", pe_y[pb])
            dst = YS[:, :, pt0:pt0 + 16]
            nc.vector.tensor_tensor(out=dst, in0=dst, in1=P.ps[:, 4 + pb, 0:128].rearrange("p (c t) -> p c t", c=8), op=ALU.add)
            P.barrier()
            for b in range(8):
                P.bank_free[b] = None
            YA = UT
            RT.off = 0
            G1 = alloc(NT)
            for ch in range(8):
                y = YS[:, ch, :]
                tA = P.mark("act", nc.scalar.activation(out=G1, in_=y, func=ACT.Square))
                P.wait_tok("dve", tA)
                nc.vector.tensor_scalar(out=G1, in0=G1, scalar1=0.044715, scalar2=1.0, op0=ALU.mult, op1=ALU.add)
                tD = P.mark("dve", nc.vector.tensor_tensor(out=G1, in0=G1, in1=y, op=ALU.mult))
                P.wait_tok("act", tD)
                tA2 = P.mark("act", nc.scalar.activation(out=G1, in_=G1, func=ACT.Exp, scale=-GELU_C))
                P.wait_tok("dve", tA2)
                nc.vector.tensor_scalar(out=G1, in0=G1, scalar1=1.0, scalar2=None, op0=ALU.add)
                nc.vector.reciprocal(out=G1, in_=G1)
                tD2 = P.mark("dve", nc.vector.tensor_tensor(out=YA[:, ch, :], in0=G1, in1=y, op=ALU.mult))
                P.wait_tok("act", tD2)
            P.barrier()
            tfree = P.mark("dve", nc.vector.memset(P.scr[:, 2:3], 0.0))
            for s_ in range(NSLOT):
                P.slot_free[s_] = tfree
            YG = HM[:, 0:8, :]
            NGLB = S5H0[:, 0, 0:8]
            nc.vector.tensor_scalar(out=NGLB, in0=vcol("glu_b", 0, 8), scalar1=-1.0, scalar2=None, op0=ALU.mult)
            ygt = None
            P.barrier()
            for oc in range(8):
                t = P.load_tile(w_glu[oc], n=1024)
                ap = P.tile_ap(t)[:, 0:1024].rearrange("p (k n) -> p k n", k=8)
                for tb in range(NTB):
                    ts = slice(tb * TB, (tb + 1) * TB)
                    b = P.bank()
                    tp = P.group(P.ps[:, b, :], [(ap[:, kc, :], YA[:, kc, ts]) for kc in range(8)], [b], tiles=[t])
                    P.wait_tok("act", tp)
                    tA = P.mark("act", nc.scalar.activation(out=P.ps[:, b, :], in_=P.ps[:, b, :], func=ACT.Exp, scale=-1.0, bias=NGLB[:, oc:oc + 1]))
                    P.wait_tok("dve", tA)
                    nc.vector.tensor_scalar(out=P.ps[:, b, :], in0=P.ps[:, b, :], scalar1=1.0, scalar2=None, op0=ALU.add)
                    nc.vector.reciprocal(out=P.ps[:, b, :], in_=P.ps[:, b, :])
                    ygt = P.mark("dve", nc.vector.tensor_tensor(out=YG[:, oc, ts], in0=P.ps[:, b, :], in1=YA[:, oc, ts], op=ALU.mult))
                    P.bank_free[b] = ygt
                P.release(t)
            for half in range(2):
                tiles = [P.load_tile(w_evout[half * 4 + j]) for j in range(4)]
                aps = [P.tile_ap(t) for t in tiles]
                for oc in range(NCH):
                    for tb in range(NTB):
                        ts = slice(tb * TB, (tb + 1) * TB)
                        b = P.bank()
                        tp = P.group(P.ps[:, b, :], [(aps[j][:, oc * 128:(oc + 1) * 128], YG[:, half * 4 + j, ts]) for j in range(4)], [b], tiles=tiles, waits=[ygt])
                        P.wait_tok("dve", tp)
                        ins = nc.vector.scalar_tensor_tensor(out=XT[:, oc, ts], in0=P.ps[:, b, :], scalar=G2[:, oc:oc + 1], in1=XT[:, oc, ts], op0=ALU.mult, op1=ALU.add)
                        P.bank_free[b] = P.mark("dve", ins)
                for t in tiles:
                    P.release(t)
            P.barrier()
            P.wait("sp", P.ms["dve"], P.msn["dve"])
            ds = P.dma("sp", s5_out[:, :], S5ST[:].rearrange("p a b -> p (a b)"), "s5o")
            for e in ("dve", "act", "pe"):
                P.wait_dma(e, ds)

        P.wait_dma("dve", ldv)
        lru_prep()
        for l in range(2):
            mvt = ada(l)
            hm = norm(MV[:, l, 0, :], MV[:, l, 1, :], mvt)
            ffn(2 * l, MV[:, l, 2, :], hm)
            hm = norm(MV[:, l, 3, :], MV[:, l, 4, :], mvt)
            if l == 0 and mix_gla:
                gla_part(MV[:, l, 5, :])
            if l == 0 and mix_s5:
                s5_part(MV[:, l, 5, :])
            if l == 1 and mix_odd:
                odd_mixer(l, MV[:, l, 5, :])
            hm = norm(MV[:, l, 6, :], MV[:, l, 7, :], mvt)
            ffn(2 * l + 1, MV[:, l, 8, :], hm)
        fin = norm(vcol("final_g"), None, None, final=True)
        P.barrier()
        for e in ("sp",):
            P.wait_tok(e, fin)
            P.wait("sp", P.ms["dve"], P.msn["dve"])
            P.wait("sp", P.ms["act"], P.msn["act"])
        o1 = P.dma("sp", yT_out[:, :], XT[:].rearrange("p c t -> p (c t)"), "out")
        o2 = P.dma("sp", lru_out[:, :], LRUST[:].rearrange("p s d c -> p (s d c)"), "out")
        P.wait_dma("sp", o2)
    return nc


def _offsets(items):
    off = {}
    o = 0
    for n, w in items:
        off[n] = o
        o += w
    return off, o


VOFF, NVEC = _offsets([("norm_g", 96), ("final_g", 16), ("ada_b", 288), ("conv_w", 64), ("conv_b", 16),
                       ("lru_ba", 32), ("lru_bx", 32), ("lru_lam", 32), ("gla_gb", 8), ("gla_ng", 2), ("s5_d", 8), ("glu_b", 8)])
COFF, NCST = _offsets([("ident", 128), ("jrev", 128), ("maskf", 128), ("maskb", 128), ("rm", SL), ("s5ma", 2), ("s5mb", 2), ("m96", 1)])
POFF, NPC = _offsets([("cvm", 3 * SL), ("carry", 1), ("lh0", 2 * NCH)])


def pvec(v):
    v = np.asarray(v, np.float32).reshape(-1, 128)
    return np.ascontiguousarray(v.T)


def tile_cols(W, c0, ncols=128):
    K = W.shape[0]
    t = W[:, c0:c0 + ncols].reshape(K // 128, 128, ncols).transpose(1, 0, 2)
    return t.reshape(128, -1)


def prep_shared(inp):
    sh = {}
    ada_w = inp["ada_w"]
    wa = np.empty((2 * 144, 128, TSZ), np.float32)
    for l in range(2):
        wa[l * 144:(l + 1) * 144] = ada_w[l].reshape(16, 128, 144, 128).transpose(2, 1, 0, 3).reshape(144, 128, TSZ)
    sh["w_ada"] = wa
    wi = np.empty((4 * 88, 128, TSZ), np.float32)
    wo = np.empty((4 * 44, 128, TSZ), np.float32)
    for l in range(2):
        for w in range(2):
            f = 2 * l + w
            Wi = inp["ffn_w_in"][l, w].reshape(16, 128, 88, 128).transpose(2, 1, 0, 3).reshape(88, 128, TSZ)
            wi[f * 88:(f + 1) * 88:2] = Wi[0:44]
            wi[f * 88 + 1:(f + 1) * 88:2] = Wi[44:88]
            wo[f * 44:(f + 1) * 44] = inp["ffn_w_out"][l, w].reshape(44, 128, TSZ)
    sh["w_in"] = wi
    sh["w_out"] = wo
    sh["w_odin"] = np.ascontiguousarray(inp["od_w_in"][0].reshape(16, 128, 32, 128).transpose(2, 1, 0, 3).reshape(32, 128, TSZ))
    sh["w_odout"] = np.ascontiguousarray(inp["od_w_out"][0].reshape(16, 128, TSZ))
    wl = np.stack([inp["lru_wa"][0], inp["lru_wx"][0]], 0)
    wl = wl.reshape(2, 2, 8, 2, 128, 256).transpose(2, 4, 0, 1, 3, 5)
    sh["w_lru"] = np.ascontiguousarray(wl).reshape(8, 128, TSZ)
    vec = np.zeros((128, NVEC), np.float32)

    def put(name, arr):
        a = pvec(np.asarray(arr).reshape(-1))
        vec[:, VOFF[name]:VOFF[name] + a.shape[1]] = a
    put("norm_g", inp["norm_g"])
    put("final_g", inp["final_norm_g"])
    put("ada_b", inp["ada_b"])
    put("conv_w", inp["lru_conv_w"][0])
    put("conv_b", inp["lru_conv_b"][0])
    put("lru_ba", inp["lru_ba"][0])
    put("lru_bx", inp["lru_bx"][0])
    put("lru_lam", inp["lru_lam"][0])
    put("gla_gb", inp["gla_gate_b"][0])
    put("gla_ng", inp["gla_norm_g"][0])
    evin = np.zeros((2048, 33 * 128), np.float32)
    evin[:, :4128] = inp["ev_w_in"][0]
    sh["w_evin"] = np.ascontiguousarray(evin.reshape(16, 128, 33, 128).transpose(2, 1, 0, 3).reshape(33, 128, TSZ))
    sh["w_evout"] = np.ascontiguousarray(inp["ev_w_out"][0].reshape(16, 128, TSZ))
    sh["gw2"] = np.ascontiguousarray(inp["gla_gate_w2"][0].transpose(1, 0, 2).reshape(16, 1024))
    put("s5_d", inp["s5_d"][0])
    put("glu_b", inp["s5_glu_b"][0])
    sh["w_glu"] = np.ascontiguousarray(inp["s5_glu_w"][0].reshape(8, 128, 8, 128).transpose(2, 1, 0, 3).reshape(8, 128, 1024))
    lre, lim = inp["s5_lam_re"][0], inp["s5_lam_im"][0]
    ls = np.broadcast_to(inp["s5_log_step"][0][:, :, None], (2, 64, 64))
    def layB(a):
        return a.reshape(2, 32, 2, 64).transpose(2, 3, 0, 1).reshape(128, 64)
    sh["s5lb"] = np.ascontiguousarray(np.stack([layB(lre), layB(lim), layB(ls)], 1).reshape(128, 192)).astype(np.float32)
    def layCB(a):
        return a.reshape(2, 32, 2, 16, 64).transpose(2, 4, 0, 1, 3).reshape(128, 2, 32, 16)
    sh["s5cb"] = np.ascontiguousarray(np.stack([layCB(inp["s5_c_re"][0]), layCB(inp["s5_c_im"][0])], 1).reshape(128, 2048))
    def layA(a):
        t = a.reshape(2, 8, 8, 64).transpose(2, 0, 1, 3)
        return np.broadcast_to(t[:, None], (8, 16, 2, 8, 64)).reshape(128, 1024)
    sh["s5la"] = np.ascontiguousarray(np.stack([layA(lre), layA(lim), layA(ls)], 1).reshape(128, 3072)).astype(np.float32)
    def layBA(a):
        return a.reshape(2, 8, 8, 64, 16).transpose(2, 4, 0, 1, 3).reshape(128, 1024)
    sh["s5ba"] = np.ascontiguousarray(np.stack([layBA(inp["s5_b_re"][0]), layBA(inp["s5_b_im"][0])], 1).reshape(128, 2048))
    sh["vecs"] = vec
    cst = np.zeros((128, NCST), np.float32)
    cst[:, COFF["ident"]:COFF["ident"] + 128] = np.eye(128, dtype=np.float32)
    cst[:, COFF["jrev"]:COFF["jrev"] + 128] = np.eye(128, dtype=np.float32)[::-1]
    jj = np.arange(128)[:, None]
    ii = np.arange(128)[None, :]
    same = (jj // 64) == (ii // 64)
    cst[:, COFF["maskf"]:COFF["maskf"] + 128] = (same & (jj <= ii)).astype(np.float32)
    cst[:, COFF["maskb"]:COFF["maskb"] + 128] = (same & (jj >= ii)).astype(np.float32)
    cst[:, COFF["rm"]:COFF["rm"] + SL] = (np.arange(SL) % 64 != 0).astype(np.float32)[None, :]
    pidx = np.arange(128)
    for g2 in range(2):
        cst[:, COFF["s5ma"] + g2] = ((pidx // 16) % 2 == g2).astype(np.float32)
        cst[:, COFF["s5mb"] + g2] = ((pidx // 64) == g2).astype(np.float32)
    cst[:, COFF["m96"]] = (pidx >= 96).astype(np.float32)
    sh["consts"] = cst
    return sh


PROMPT_ASSIGN = [[0, 1, 2], [3, 4, 5], [6, 7, 8], [9, 10, 11], [12, 13], [14, 15]]


def prep_core(inp, core):
    m = {}
    x = np.zeros((NT, D), np.float32)
    pc = np.zeros((128, NPC), np.float32)
    t = np.arange(SL)
    if core < 2:
        x[:] = inp["x_sample"][core]
        cond = inp["c"][core]
        seg = 64
        pc[:, POFF["carry"]] = 1.0
        lh0 = np.stack([pvec(inp["state_lru"][core, 0, d]) for d in range(2)], 1)
        pc[:, POFF["lh0"]:POFF["lh0"] + 32] = lh0.reshape(128, 32)
    else:
        for s, b in enumerate(PROMPT_ASSIGN[core - 2]):
            x[SL * s:SL * (s + 1)] = inp["x_prompt"][b]
        cond = inp["c_ctx"]
        seg = 256
    cvm = np.stack([(t % seg >= 2), (t % seg >= 1), (t % seg <= seg - 2)], 0).astype(np.float32)
    pc[:, POFF["cvm"]:POFF["cvm"] + 3 * SL] = cvm.reshape(1, -1)
    m["pcore"] = pc
    gs0 = np.zeros((4, 128, 2, SL), np.float32)
    if core < 2:
        gs0[:] = inp["state_gla"][core, 0].transpose(1, 2, 0, 3)
    m["gs0"] = gs0.reshape(4, 128, 2 * SL)
    h0 = np.zeros((128, 2, 2, 32), np.float32)
    if core < 2:
        for c, nm in enumerate(("state_s5_re", "state_s5_im")):
            h0[:, :, c, :] = inp[nm][core, 0].reshape(2, 32, 2, 64).transpose(2, 3, 0, 1).reshape(128, 2, 32)
    m["s5h0"] = h0.reshape(128, 128)
    m["xT"] = np.ascontiguousarray(x.T.reshape(NCH, 128, NT).transpose(1, 0, 2)).reshape(128, NCH * NT)
    m["cond"] = pvec(cond)
    return m


def assemble(inp, results):
    yp = np.zeros((16, 256, D), np.float32)
    ys = np.zeros((2, 1024, D), np.float32)
    st_re = np.zeros((16, 1, 2, 64, 64), np.float32)
    st_im = np.zeros((16, 1, 2, 64, 64), np.float32)
    st_gla = np.zeros((16, 1, 2, 4, 128, 256), np.float32)
    st_lru = np.zeros((16, 1, 2, D), np.float32)
    for core in range(N_CORES):
        yT = results[core]["yT"].reshape(128, NCH, NT)
        y = yT.transpose(2, 1, 0).reshape(NT, D)
        if core < 2:
            ys[core] = y
        else:
            ls = results[core]["lru_st"].reshape(128, NSL, 2, NCH)
            gs = results[core]["gla_st"].reshape(NSL, 2, 4, 128, SL)
            s5 = results[core]["s5_st"].reshape(2, 64, NSL, 2, 2, 32)
            for s, b in enumerate(PROMPT_ASSIGN[core - 2]):
                yp[b] = y[SL * s:SL * (s + 1)]
                st_lru[b, 0] = ls[:, s].transpose(1, 2, 0).reshape(2, D)
                st_gla[b, 0] = gs[s]
                st_re[b, 0] = s5[:, :, s, :, 0, :].transpose(2, 3, 0, 1).reshape(2, 64, 64)
                st_im[b, 0] = s5[:, :, s, :, 1, :].transpose(2, 3, 0, 1).reshape(2, 64, 64)
    return yp, ys, st_re, st_im, st_gla, st_lru


def kernel(**inputs):
    inp = {k: np.asarray(v) for k, v in inputs.items()}
    shared = prep_shared(inp)
    in_maps = []
    for core in range(N_CORES):
        m = dict(shared)
        m.update(prep_core(inp, core))
        in_maps.append(m)
    nc = build_program()
    res = run_bass_kernel_spmd(nc, in_maps, core_ids=list(range(N_CORES)))
    return assemble(inp, res.results)
```

```python
import contextlib
import math
import numpy as np
import concourse.bass as bass
import concourse.mybir as mybir
from concourse.bass_utils import run_bass_kernel_spmd

F32 = mybir.dt.float32
BF16 = mybir.dt.bfloat16
I32 = mybir.dt.int32
ACT = mybir.ActivationFunctionType
ALU = mybir.AluOpType

D = 2048
NCH = 16
NT = 1024
TB = 512
NTB = 2
SL = 256
NSL = 4
DFF = 5632
NSLOT = 10
TSZ = 2048
EPS = 1e-6
N_CORES = 8
ARENA_F32 = 11520
GELU_C = 1.5957691216057308


class Tok:
    __slots__ = ("eng", "val")

    def __init__(self, eng, val):
        self.eng = eng
        self.val = val


class Prog:
    def __init__(self, nc, es):
        self.nc = nc
        self.es = es
        self.engs = {"pe": nc.tensor, "act": nc.scalar, "dve": nc.vector, "pool": nc.gpsimd, "sp": nc.sync}
        self.ms = {k: es.enter_context(nc.semaphore("ms_" + k)) for k in ("pe", "act", "dve")}
        self.msn = {k: 0 for k in self.ms}
        self.waited = {}
        self.dsems = {}
        self.ps = es.enter_context(nc.psum_tensor("ps", [128, 8, 512], F32))
        self.bank_free = [None] * 8
        self.bank_rr = 0
        self.ring = es.enter_context(nc.sbuf_tensor("ring", [128, NSLOT, TSZ], BF16))
        self.slot_ld = [es.enter_context(nc.semaphore("ld%d" % s)) for s in range(NSLOT)]
        self.slot_ldn = [0] * NSLOT
        self.slot_free = [None] * NSLOT
        self.slot_rr = 0
        self.slot_open = [False] * NSLOT
        self.last_pe = None
        self.scr = es.enter_context(nc.sbuf_tensor("scr", [128, 4], F32))

    def wait(self, eng, sem, val):
        key = (eng, id(sem))
        if self.waited.get(key, 0) >= val:
            return
        self.waited[key] = val
        self.engs[eng].wait_ge(sem, val)

    def wait_tok(self, eng, tok):
        if tok is None or tok.eng == eng:
            return
        self.wait(eng, self.ms[tok.eng], tok.val)

    def fence(self, eng, tok):
        self.wait(eng, self.ms[tok.eng], tok.val)

    def mark(self, eng, ins):
        ins.then_inc(self.ms[eng], 1)
        self.msn[eng] += 1
        return Tok(eng, self.msn[eng])

    def dma(self, eng, out, in_, name):
        if name not in self.dsems:
            self.dsems[name] = [self.es.enter_context(self.nc.semaphore("d_" + name)), 0]
        s = self.dsems[name]
        self.engs[eng].dma_start(out=out, in_=in_).then_inc(s[0], 16)
        s[1] += 16
        return (s[0], s[1])

    def wait_dma(self, eng, d):
        self.wait(eng, d[0], d[1])

    def barrier(self):
        nc = self.nc
        ta = self.mark("act", nc.scalar.activation(out=self.scr[:, 0:1], in_=self.scr[:, 0:1], func=ACT.Copy))
        td = self.mark("dve", nc.vector.memset(self.scr[:, 1:2], 0.0))
        for e in ("pe", "act", "dve"):
            self.wait_tok(e, ta)
            self.wait_tok(e, td)
            self.wait_tok(e, self.last_pe)

    def sp_sync(self):
        self.barrier()
        self.wait("sp", self.ms["act"], self.msn["act"])
        self.wait("sp", self.ms["dve"], self.msn["dve"])
        self.wait_tok("sp", self.last_pe)

    def pool_sync(self):
        self.barrier()
        self.wait("pool", self.ms["act"], self.msn["act"])
        self.wait("pool", self.ms["dve"], self.msn["dve"])
        self.wait_tok("pool", self.last_pe)

    def bank(self):
        b = self.bank_rr
        self.bank_rr = (b + 1) % 8
        return b

    def bank_pair(self):
        if self.bank_rr % 2:
            self.bank_rr = (self.bank_rr + 1) % 8
        b = self.bank_rr
        self.bank_rr = (b + 2) % 8
        return b, b + 1

    def load_tile(self, src_ap, n=TSZ):
        s = self.slot_rr
        self.slot_rr = (s + 1) % NSLOT
        assert not self.slot_open[s], "ring slot %d still open" % s
        self.slot_open[s] = True
        self.wait_tok("pool", self.slot_free[s])
        self.nc.gpsimd.dma_start(out=self.ring[:, s, 0:n], in_=src_ap).then_inc(self.slot_ld[s], 16)
        self.slot_ldn[s] += 16
        return (s, self.slot_ldn[s])

    def release(self, t):
        self.slot_open[t[0]] = False

    def tile_ap(self, t):
        return self.ring[:, t[0], :]

    def transpose(self, out_ap, in_ap, ident, banks, waits=()):
        for b in banks:
            self.wait_tok("pe", self.bank_free[b])
        for w in waits:
            self.wait_tok("pe", w)
        ins = self.nc.tensor.transpose(out=out_ap, in_=in_ap, identity=ident)
        tok = self.mark("pe", ins)
        self.last_pe = tok
        return tok

    def group2(self, items, banks, tiles=(), waits=()):
        for b in banks:
            self.wait_tok("pe", self.bank_free[b])
        for t in tiles:
            self.wait("pe", self.slot_ld[t[0]], t[1])
        for w in waits:
            self.wait_tok("pe", w)
        n = len(items)
        ins = None
        for i, (o, l, r) in enumerate(items):
            ins = self.nc.tensor.matmul(o, l, r, start=(i == 0), stop=(i == n - 1))
        tok = self.mark("pe", ins)
        self.last_pe = tok
        for t in tiles:
            self.slot_free[t[0]] = tok
        return tok

    def group(self, out_ap, pairs, banks, tiles=(), waits=(), mark=True):
        for b in banks:
            self.wait_tok("pe", self.bank_free[b])
        for t in tiles:
            self.wait("pe", self.slot_ld[t[0]], t[1])
        for w in waits:
            self.wait_tok("pe", w)
        n = len(pairs)
        ins = None
        for i, (l, r) in enumerate(pairs):
            ins = self.nc.tensor.matmul(out_ap, l, r, start=(i == 0), stop=(i == n - 1))
        if not mark:
            return None
        tok = self.mark("pe", ins)
        self.last_pe = tok
        for t in tiles:
            self.slot_free[t[0]] = tok
        return tok


class Arena:
    def __init__(self, ap):
        self.ap = ap
        self.off = 0

    def reset(self):
        self.off = 0

    def f32(self, *shape):
        n = int(np.prod(shape))
        v = self.ap[:, self.off:self.off + n]
        self.off += n
        assert self.off <= ARENA_F32, ("arena overflow", self.off)
        if len(shape) == 2:
            return v.rearrange("p (a b) -> p a b", a=shape[0])
        if len(shape) == 3:
            return v.rearrange("p (a b c) -> p a b c", a=shape[0], b=shape[1])
        return v

    def bf16(self, *shape):
        n = int(np.prod(shape))
        assert n % 2 == 0
        v = self.ap[:, self.off:self.off + n // 2].bitcast(BF16)
        self.off += n // 2
        assert self.off <= ARENA_F32, ("arena overflow", self.off)
        if len(shape) == 2:
            return v.rearrange("p (a b) -> p a b", a=shape[0])
        if len(shape) == 3:
            return v.rearrange("p (a b c) -> p a b c", a=shape[0], b=shape[1])
        return v


def build_program(mix_even=True, mix_odd=True, debug=False, mix_gla=None, mix_s5=None):
    if mix_gla is None:
        mix_gla = mix_even
    if mix_s5 is None:
        mix_s5 = mix_even
    nc = bass.Bass("TRN2", target_bir_lowering=False)
    es = contextlib.ExitStack()
    dbg_list = []

    def din(name, shape):
        return nc.dram_tensor(name, list(shape), F32, kind="ExternalInput").ap()

    def dout(name, shape):
        return nc.dram_tensor(name, list(shape), F32, kind="ExternalOutput").ap()

    xT_in = din("xT", [128, NCH * NT])
    cond_in = din("cond", [128, NCH])
    vecs_in = din("vecs", [128, NVEC])
    cst_in = din("consts", [128, NCST])
    pc_in = din("pcore", [128, NPC])
    w_ada = din("w_ada", [2 * 144, 128, TSZ])
    w_in = din("w_in", [4 * 88, 128, TSZ])
    w_out = din("w_out", [4 * 44, 128, TSZ])
    w_odin = din("w_odin", [32, 128, TSZ])
    w_odout = din("w_odout", [16, 128, TSZ])
    w_lru = din("w_lru", [8, 128, TSZ])
    w_evin = din("w_evin", [33, 128, TSZ])
    w_evout = din("w_evout", [16, 128, TSZ])
    gw2_in = din("gw2", [16, 1024])
    gs0_in = din("gs0", [4, 128, 2 * SL])
    w_glu = din("w_glu", [8, 128, 1024])
    s5lb_in = din("s5lb", [128, 192])
    s5cb_in = din("s5cb", [128, 2048])
    s5la_in = din("s5la", [128, 3072])
    s5ba_in = din("s5ba", [128, 2048])
    s5h0_in = din("s5h0", [128, 128])
    yT_out = dout("yT", [128, NCH * NT])
    lru_out = dout("lru_st", [128, NSL * 2 * NCH])
    gla_out = dout("gla_st", [NSL * 2 * 4, 128, SL])
    s5_out = dout("s5_st", [128, NSL * 2 * 64])
    dbg_out = dout("dbg", [8, 128, 2048]) if debug else None

    with es:
        P = Prog(nc, es)
        sb = lambda name, shape, dt=F32: es.enter_context(nc.sbuf_tensor(name, list(shape), dt))
        XT = sb("XT", [128, NCH, NT])
        HM = sb("HM", [128, NCH, NT], BF16)
        ARN = sb("ARN", [128, ARENA_F32])
        AR = Arena(ARN[:])
        VEC = sb("VEC", [128, NVEC])
        CST = sb("CST", [128, NCST])
        PC = sb("PC", [128, NPC])
        MOD = sb("MOD", [128, 2, 144])
        MV = sb("MV", [128, 2, 9, 16])
        CA = sb("CA", [128, NCH], BF16)
        CAf = sb("CAf", [128, NCH])
        ONES = sb("ONES", [128, 128], BF16)
        EPSB = sb("EPSB", [128, 1])
        LRUST = sb("LRUST", [128, NSL, 2, NCH])
        NSP = sb("NSP", [128, 2, 2, NCH])

        state = {"xt_w": None, "hm_r": None}

        def vcol(name, j=0, w=16):
            o = VOFF[name] + j * w
            return VEC[:, o:o + w]

        def ccol(name, w):
            o = COFF[name]
            return CST[:, o:o + w]

        def pcol(name, w):
            o = POFF[name]
            return PC[:, o:o + w]

        def dump(ap2d):
            if not debug:
                return
            i = len(dbg_list)
            dbg_list.append(i)
            P.barrier()
            P.wait("sp", P.ms["dve"], P.msn["dve"])
            P.wait("sp", P.ms["act"], P.msn["act"])
            P.wait_tok("sp", P.last_pe)
            d = P.dma("sp", dbg_out[i, :, 0:ap2d.shape[1]], ap2d, "dbg")
            for e in ("pe", "act", "dve"):
                P.wait_dma(e, d)

        ld = []
        for q in range(4):
            ld.append(P.dma("sp", XT[:, 4 * q:4 * q + 4, :], xT_in[:, 4 * q * NT:(4 * q + 4) * NT].rearrange("p (c t) -> p c t", c=4), "inx%d" % q))
        ldv = P.dma("sp", VEC[:], vecs_in[:, :], "inv")
        ldk = P.dma("sp", CST[:], cst_in[:, :], "ink")
        ldp = P.dma("sp", PC[:], pc_in[:, :], "inp")
        ldc = P.dma("sp", CAf[:], cond_in[:, :], "inc")
        for e in ("act", "dve", "pool", "pe"):
            for dd in (ldc, ldv, ldk, ldp):
                P.wait_dma(e, dd)
        ones_ready = es.enter_context(nc.semaphore("ones"))
        nc.gpsimd.memset(EPSB[:], EPS)
        nc.gpsimd.memset(P.scr[:], 0.0)
        nc.gpsimd.memset(ONES[:], 1.0).then_inc(ones_ready, 1)
        for e in ("pe", "act", "dve"):
            P.wait(e, ones_ready, 1)
        ins = nc.scalar.activation(out=CA[:], in_=CAf[:], func=ACT.Silu)
        t_ca = P.mark("act", ins)

        def ada(l):
            b = P.bank()
            tok = None
            for i in range(144):
                t = P.load_tile(w_ada[l * 144 + i])
                ap = P.tile_ap(t).rearrange("p (k n) -> p k n", k=16)
                tok = P.group(P.ps[:, b, i:i + 1], [(ap[:, kc, :], CA[:, kc:kc + 1]) for kc in range(16)],
                              [b] if i == 0 else [], tiles=[t], waits=[t_ca])
                P.release(t)
            P.wait_tok("dve", tok)
            ab = VOFF["ada_b"] + l * 144
            nc.vector.tensor_tensor(out=MOD[:, l, :], in0=P.ps[:, b, 0:144], in1=VEC[:, ab:ab + 144], op=ALU.add)
            for s in range(3):
                sh = MOD[:, l, (3 * s) * 16:(3 * s) * 16 + 16]
                sc = MOD[:, l, (3 * s + 1) * 16:(3 * s + 1) * 16 + 16]
                g = MOD[:, l, (3 * s + 2) * 16:(3 * s + 2) * 16 + 16]
                ng = vcol("norm_g", l * 3 + s)
                nc.vector.scalar_tensor_tensor(out=MV[:, l, 3 * s, :], in0=sc, scalar=1.0, in1=ng, op0=ALU.add, op1=ALU.mult)
                nc.vector.tensor_copy(out=MV[:, l, 3 * s + 1, :], in_=sh)
                ins = nc.vector.tensor_scalar(out=MV[:, l, 3 * s + 2, :], in0=g, scalar1=(1.0 if s == 1 else 0.5), scalar2=None, op0=ALU.mult)
            tk = P.mark("dve", ins)
            P.bank_free[b] = tk
            return tk

        def norm(A, sh, mv_tok, final=False):
            AR.reset()
            RS = AR.f32(TB)
            TMP = AR.f32(TB)
            P.barrier()
            prev = None
            for tb in range(NTB):
                ts = slice(tb * TB, (tb + 1) * TB)
                P.wait_tok("act", prev)
                for q in range(4):
                    P.wait_dma("act", ld[q])
                    P.wait_dma("dve", ld[q])
                for c in range(NCH):
                    ins = nc.scalar.activation(out=HM[:, c, ts], in_=XT[:, c, ts], func=ACT.Square)
                tS = P.mark("act", ins)
                b = P.bank()
                tP = P.group(P.ps[:, b, :], [(ONES[:, :], HM[:, c, ts]) for c in range(NCH)], [b], waits=[tS])
                P.wait_tok("act", tP)
                ins = nc.scalar.activation(out=RS, in_=P.ps[:, b, :], func=ACT.Sqrt, scale=1.0 / D, bias=EPSB[:, 0:1])
                tR = P.mark("act", ins)
                P.bank_free[b] = tR
                P.wait_tok("dve", tR)
                P.wait_tok("dve", mv_tok)
                nc.vector.reciprocal(out=RS, in_=RS)
                for c in range(NCH):
                    if final:
                        ins = nc.vector.scalar_tensor_tensor(out=XT[:, c, ts], in0=XT[:, c, ts], scalar=A[:, c:c + 1], in1=RS, op0=ALU.mult, op1=ALU.mult)
                    else:
                        nc.vector.scalar_tensor_tensor(out=TMP, in0=XT[:, c, ts], scalar=A[:, c:c + 1], in1=RS, op0=ALU.mult, op1=ALU.mult)
                        ins = nc.vector.tensor_scalar(out=HM[:, c, ts], in0=TMP, scalar1=sh[:, c:c + 1], scalar2=None, op0=ALU.add)
                prev = P.mark("dve", ins)
            P.barrier()
            return prev

        def ffn(f, G, hm_tok):
            AR.reset()
            PN = 8
            H = AR.bf16(PN, NT)
            lastB = None
            for p0 in range(0, 44, PN):
                npan = min(PN, 44 - p0)
                tokD = None
                for j in range(npan):
                    hc = p0 + j
                    ta = P.load_tile(w_in[f * 88 + 2 * hc])
                    tb_ = P.load_tile(w_in[f * 88 + 2 * hc + 1])
                    apa = P.tile_ap(ta).rearrange("p (k n) -> p k n", k=16)
                    apb = P.tile_ap(tb_).rearrange("p (k n) -> p k n", k=16)
                    for tb in range(NTB):
                        ts = slice(tb * TB, (tb + 1) * TB)
                        ba, bb = P.bank_pair()
                        P.group(P.ps[:, ba, :], [(apa[:, kc, :], HM[:, kc, ts]) for kc in range(16)], [ba], tiles=[ta], waits=[hm_tok])
                        tpb = P.group(P.ps[:, bb, :], [(apb[:, kc, :], HM[:, kc, ts]) for kc in range(16)], [bb], tiles=[tb_])
                        P.wait_tok("act", tpb)
                        P.wait_tok("act", lastB)
                        ins = nc.scalar.activation(out=H[:, j, ts], in_=P.ps[:, ba, :], func=ACT.Silu)
                        tA = P.mark("act", ins)
                        P.wait_tok("dve", tA)
                        ins = nc.vector.tensor_tensor(out=H[:, j, ts], in0=H[:, j, ts], in1=P.ps[:, bb, :], op=ALU.mult)
                        tokD = P.mark("dve", ins)
                        P.bank_free[ba] = tokD
                        P.bank_free[bb] = tokD
                    P.release(ta)
                    P.release(tb_)
                tiles = [P.load_tile(w_out[f * 44 + p0 + j]) for j in range(npan)]
                aps = [P.tile_ap(t) for t in tiles]
                for oc in range(NCH):
                    for tb in range(NTB):
                        ts = slice(tb * TB, (tb + 1) * TB)
                        b = P.bank()
                        pairs = [(aps[j][:, oc * 128:(oc + 1) * 128], H[:, j, ts]) for j in range(npan)]
                        lastB = P.group(P.ps[:, b, :], pairs, [b], tiles=tiles, waits=[tokD])
                        P.wait_tok("dve", lastB)
                        ins = nc.vector.scalar_tensor_tensor(out=XT[:, oc, ts], in0=P.ps[:, b, :], scalar=G[:, oc:oc + 1], in1=XT[:, oc, ts], op0=ALU.mult, op1=ALU.add)
                        P.bank_free[b] = P.mark("dve", ins)
                for t in tiles:
                    P.release(t)

        def lru_prep():
            lam = VEC[:, VOFF["lru_lam"]:VOFF["lru_lam"] + 32]
            t = NSP[:, 0, :, :].rearrange("p d c -> p (d c)")
            t2 = NSP[:, 1, :, :].rearrange("p d c -> p (d c)")
            P.wait_dma("act", ldv)
            nc.scalar.activation(out=t, in_=lam, func=ACT.Exp, scale=-1.0)
            ins = nc.scalar.activation(out=t, in_=t, func=ACT.Ln, bias=1.0)
            tk = P.mark("act", ins)
            P.wait_tok("dve", tk)
            nc.vector.tensor_scalar(out=t2, in0=t, scalar1=-16.0, scalar2=None, op0=ALU.mult)
            nc.vector.tensor_scalar(out=t, in0=t, scalar1=-8.0, scalar2=None, op0=ALU.mult)

        def odd_mixer(l, G2):
            P.barrier()
            AR.reset()
            XC = AR.f32(2, NT)
            HS = AR.f32(2, NT)
            XCb = AR.bf16(2, NT)
            Y = AR.bf16(2, NT)
            T12 = AR.f32(2, 2 * SL)
            T1 = T12[:, 0, :]
            T2 = T12[:, 1, :]
            T3 = AR.f32(2 * SL)
            T4 = AR.f32(2, SL)
            XF = AR.f32(2, SL)
            XFb = AR.bf16(2, SL)
            TT = AR.f32(128)
            HIN = AR.f32(4)
            PEB = AR.f32(2)
            IDENT = ccol("ident", 128)
            JREV = ccol("jrev", 128)
            CVM = pcol("cvm", 3 * SL).rearrange("p (j t) -> p j t", j=3)
            CARRY = pcol("carry", 1)
            LH0 = pcol("lh0", 2 * NCH).rearrange("p (d c) -> p d c", d=2)
            cw = VEC[:, VOFF["conv_w"]:VOFF["conv_w"] + 64].rearrange("p (j c) -> p j c", j=4)
            cb = vcol("conv_b")
            nba = VEC[:, VOFF["lru_ba"]:VOFF["lru_ba"] + 32].rearrange("p (d c) -> p d c", d=2)
            nbx = VEC[:, VOFF["lru_bx"]:VOFF["lru_bx"] + 32].rearrange("p (d c) -> p d c", d=2)
            NB = AR.f32(2, 2, NCH)
            nc.vector.tensor_scalar(out=NB[:, 0, :, :], in0=nba, scalar1=-1.0, scalar2=None, op0=ALU.mult)
            nc.vector.tensor_scalar(out=NB[:, 1, :, :], in0=nbx, scalar1=-1.0, scalar2=None, op0=ALU.mult)
            P.barrier()
            st = {"prev": None}

            def lru_pair(apl, tl, d, h, src_f32, src_bf, out_fn, hin_fn, wtok):
                br, bi = P.bank_pair()
                tpi = None
                for oc in range(2):
                    ocs = slice(oc * 128, (oc + 1) * 128)
                    cs = slice(oc * SL, (oc + 1) * SL)
                    P.group(P.ps[:, br, cs], [(apl[:, 0, d, kc, ocs], src_bf(kc)) for kc in range(2)], [br] if oc == 0 else [], tiles=[tl], waits=[wtok])
                    tpi = P.group(P.ps[:, bi, cs], [(apl[:, 1, d, kc, ocs], src_bf(kc)) for kc in range(2)], [bi] if oc == 0 else [], tiles=[tl])
                P.wait_tok("act", tpi)
                P.wait_tok("act", st["prev"])
                for oc in range(2):
                    ch = 2 * h + oc
                    cs = slice(oc * SL, (oc + 1) * SL)
                    nc.scalar.activation(out=T1[:, cs], in_=P.ps[:, br, cs], func=ACT.Exp, scale=-1.0, bias=NB[:, 0, d, ch:ch + 1])
                    ins = nc.scalar.activation(out=T2[:, cs], in_=P.ps[:, bi, cs], func=ACT.Exp, scale=-1.0, bias=NB[:, 1, d, ch:ch + 1])
                tA = P.mark("act", ins)
                P.bank_free[br] = tA
                P.bank_free[bi] = tA
                P.wait_tok("dve", tA)
                T12f = T12[:].rearrange("p a b -> p (a b)")
                nc.vector.tensor_scalar(out=T12f, in0=T12f, scalar1=1.0, scalar2=None, op0=ALU.add)
                tD = P.mark("dve", nc.vector.reciprocal(out=T12f, in_=T12f))
                P.wait_tok("act", tD)
                for oc in range(2):
                    ch = 2 * h + oc
                    cs = slice(oc * SL, (oc + 1) * SL)
                    nc.scalar.activation(out=T3[:, cs], in_=T1[:, cs], func=ACT.Exp, scale=NSP[:, 1, d, ch:ch + 1])
                    ins = nc.scalar.activation(out=T1[:, cs], in_=T1[:, cs], func=ACT.Exp, scale=NSP[:, 0, d, ch:ch + 1])
                tA2 = P.mark("act", ins)
                P.wait_tok("dve", tA2)
                nc.vector.tensor_scalar(out=T3, in0=T3, scalar1=-1.0, scalar2=1.0, op0=ALU.mult, op1=ALU.add)
                nc.vector.tensor_scalar(out=T3, in0=T3, scalar1=1e-30, scalar2=None, op0=ALU.max)
                T2v = T2.rearrange("p (a b) -> p a b", a=2)
                tD2 = P.mark("dve", nc.vector.tensor_tensor(out=T2v, in0=T2v, in1=src_f32, op=ALU.mult))
                P.wait_tok("act", tD2)
                nc.scalar.activation(out=T3, in_=T3, func=ACT.Ln)
                tA3 = P.mark("act", nc.scalar.activation(out=T3, in_=T3, func=ACT.Exp, scale=0.5))
                P.wait_tok("dve", tA3)
                nc.vector.tensor_tensor(out=T2, in0=T2, in1=T3, op=ALU.mult)
                ins = None
                for oc in range(2):
                    cs = slice(oc * SL, (oc + 1) * SL)
                    ins = nc.vector.tensor_tensor_scan(out=out_fn(oc), data0=T1[:, cs], data1=T2[:, cs], initial=hin_fn(oc), op0=ALU.mult, op1=ALU.add)
                st["prev"] = P.mark("dve", ins)
                P.fence("dve", st["prev"])
                return st["prev"]

            def flip_block(src128, waits):
                b = P.bank()
                tp = P.transpose(P.ps[:, b, 0:128], src128, IDENT, [b], waits=waits)
                P.wait_tok("act", tp)
                P.wait_tok("act", st.get("tt"))
                ins = nc.scalar.activation(out=TT, in_=P.ps[:, b, 0:128], func=ACT.Copy)
                ta = P.mark("act", ins)
                P.bank_free[b] = ta
                b2 = P.bank()
                tp2 = P.group(P.ps[:, b2, 0:128], [(TT, JREV)], [b2], waits=[ta])
                st["tt"] = tp2
                return b2, tp2

            for h in range(8):
                tx = [P.load_tile(w_odin[16 + 2 * h + cc]) for cc in range(2)]
                tl = P.load_tile(w_lru[h])
                tg = [P.load_tile(w_odin[2 * h + cc]) for cc in range(2)]
                to = [P.load_tile(w_odout[2 * h])]
                apx = [P.tile_ap(t).rearrange("p (k n) -> p k n", k=16) for t in tx]
                apg = [P.tile_ap(t).rearrange("p (k n) -> p k n", k=16) for t in tg]
                apl = P.tile_ap(tl).rearrange("p (w d k n) -> p w d k n", w=2, d=2, k=2)
                xc_tok = None
                for s in range(NSL):
                    ts = slice(s * SL, (s + 1) * SL)
                    for cc in range(2):
                        ch = 2 * h + cc
                        b = P.bank()
                        tp = P.group(P.ps[:, b, 0:SL], [(apx[cc][:, kc, :], HM[:, kc, ts]) for kc in range(16)], [b], tiles=[tx[cc]])
                        P.wait_tok("dve", tp)
                        xs = P.ps[:, b, 0:SL]
                        xc = XC[:, cc, ts]
                        nc.vector.tensor_scalar(out=xc, in0=xs, scalar1=cw[:, 2, ch:ch + 1], scalar2=cb[:, ch:ch + 1], op0=ALU.mult, op1=ALU.add)
                        nc.vector.tensor_tensor(out=T1[:, 2:SL], in0=xs[:, 0:SL - 2], in1=CVM[:, 0, 2:SL], op=ALU.mult)
                        nc.vector.scalar_tensor_tensor(out=xc[:, 2:SL], in0=T1[:, 2:SL], scalar=cw[:, 0, ch:ch + 1], in1=xc[:, 2:SL], op0=ALU.mult, op1=ALU.add)
                        nc.vector.tensor_tensor(out=T1[:, 1:SL], in0=xs[:, 0:SL - 1], in1=CVM[:, 1, 1:SL], op=ALU.mult)
                        nc.vector.scalar_tensor_tensor(out=xc[:, 1:SL], in0=T1[:, 1:SL], scalar=cw[:, 1, ch:ch + 1], in1=xc[:, 1:SL], op0=ALU.mult, op1=ALU.add)
                        nc.vector.tensor_tensor(out=T1[:, 0:SL - 1], in0=xs[:, 1:SL], in1=CVM[:, 2, 0:SL - 1], op=ALU.mult)
                        ins = nc.vector.scalar_tensor_tensor(out=xc[:, 0:SL - 1], in0=T1[:, 0:SL - 1], scalar=cw[:, 3, ch:ch + 1], in1=xc[:, 0:SL - 1], op0=ALU.mult, op1=ALU.add)
                        P.bank_free[b] = P.mark("dve", ins)
                        ins = nc.vector.tensor_copy(out=XCb[:, cc, ts], in_=xc)
                        xc_tok = P.mark("dve", ins)
                P.release(tx[0]); P.release(tx[1])
                if h == 0:
                    dump(XC[:].rearrange("p a t -> p (a t)"))
                to.append(P.load_tile(w_odout[2 * h + 1]))
                apo = [P.tile_ap(t) for t in to]
                P.barrier()
                for s in range(NSL):
                    ts = slice(s * SL, (s + 1) * SL)
                    for oc in range(2):
                        ch = 2 * h + oc
                        if s == 0:
                            nc.vector.tensor_copy(out=HIN[:, oc:oc + 1], in_=LH0[:, 0, ch:ch + 1])
                        else:
                            nc.vector.tensor_scalar(out=HIN[:, oc:oc + 1], in0=HS[:, oc, s * SL - 1:s * SL], scalar1=CARRY, scalar2=None, op0=ALU.mult)
                    lru_pair(apl, tl, 0, h, XC[:, :, ts], lambda kc: XCb[:, kc, ts], lambda oc: HS[:, oc, ts], lambda oc: HIN[:, oc:oc + 1], xc_tok)
                    for oc in range(2):
                        ch = 2 * h + oc
                        nc.vector.tensor_copy(out=LRUST[:, s, 0, ch:ch + 1], in_=HS[:, oc, (s + 1) * SL - 1:(s + 1) * SL])
                if h == 0:
                    dump(HS[:].rearrange("p a t -> p (a t)"))
                for s in range(NSL - 1, -1, -1):
                    ts = slice(s * SL, (s + 1) * SL)
                    P.barrier()
                    ftok = None
                    for cc in range(2):
                        for hb in range(2):
                            b2, tp2 = flip_block(XC[:, cc, s * SL + hb * 128:s * SL + (hb + 1) * 128], [])
                            P.wait_tok("act", tp2)
                            dst = slice((1 - hb) * 128, (2 - hb) * 128)
                            nc.scalar.activation(out=XF[:, cc, dst], in_=P.ps[:, b2, 0:128], func=ACT.Copy)
                            ins = nc.scalar.activation(out=XFb[:, cc, dst], in_=P.ps[:, b2, 0:128], func=ACT.Copy)
                            ftok = P.mark("act", ins)
                            P.bank_free[b2] = ftok
                    P.wait_tok("dve", ftok)
                    if h == 0 and s == NSL - 2:
                        dump(XF[:].rearrange("p a t -> p (a t)"))
                    for oc in range(2):
                        ch = 2 * h + oc
                        if s == NSL - 1:
                            nc.vector.tensor_copy(out=HIN[:, 2 + oc:3 + oc], in_=LH0[:, 1, ch:ch + 1])
                        else:
                            nc.vector.tensor_scalar(out=HIN[:, 2 + oc:3 + oc], in0=PEB[:, oc:oc + 1], scalar1=CARRY, scalar2=None, op0=ALU.mult)
                    P.wait_tok("dve", P.last_pe)
                    lru_pair(apl, tl, 1, h, XF[:, :, :], lambda kc: XFb[:, kc, :], lambda oc: T4[:, oc, :], lambda oc: HIN[:, 2 + oc:3 + oc], ftok)
                    tk = None
                    for oc in range(2):
                        ch = 2 * h + oc
                        nc.vector.tensor_copy(out=PEB[:, oc:oc + 1], in_=T4[:, oc, SL - 1:SL])
                        tk = P.mark("dve", nc.vector.tensor_copy(out=LRUST[:, s, 1, ch:ch + 1], in_=T4[:, oc, SL - 1:SL]))
                    for oc in range(2):
                        for hb in range(2):
                            b2, tp2 = flip_block(T4[:, oc, hb * 128:(hb + 1) * 128], [tk])
                            P.wait_tok("dve", tp2)
                            dst = slice(s * SL + (1 - hb) * 128, s * SL + (2 - hb) * 128)
                            ins = nc.vector.tensor_tensor(out=HS[:, oc, dst], in0=HS[:, oc, dst], in1=P.ps[:, b2, 0:128], op=ALU.add)
                            P.bank_free[b2] = P.mark("dve", ins)
                P.release(tl)
                P.barrier()
                if h == 0:
                    dump(HS[:].rearrange("p a t -> p (a t)"))
                ytok = None
                T1g = T1[:, 0:SL]
                for s in range(NSL):
                    ts = slice(s * SL, (s + 1) * SL)
                    for cc in range(2):
                        b = P.bank()
                        tp = P.group(P.ps[:, b, 0:SL], [(apg[cc][:, kc, :], HM[:, kc, ts]) for kc in range(16)], [b], tiles=[tg[cc]])
                        gp = P.ps[:, b, 0:SL]
                        P.wait_tok("act", tp)
                        P.wait_tok("act", ytok)
                        ins = nc.scalar.activation(out=T1g, in_=gp, func=ACT.Square)
                        tA = P.mark("act", ins)
                        P.wait_tok("dve", tA)
                        nc.vector.tensor_scalar(out=T1g, in0=T1g, scalar1=0.044715, scalar2=1.0, op0=ALU.mult, op1=ALU.add)
                        ins = nc.vector.tensor_tensor(out=T1g, in0=T1g, in1=gp, op=ALU.mult)
                        tD = P.mark("dve", ins)
                        P.wait_tok("act", tD)
                        ins = nc.scalar.activation(out=T1g, in_=T1g, func=ACT.Exp, scale=-GELU_C)
                        tA2 = P.mark("act", ins)
                        P.wait_tok("dve", tA2)
                        nc.vector.tensor_scalar(out=T1g, in0=T1g, scalar1=1.0, scalar2=None, op0=ALU.add)
                        nc.vector.reciprocal(out=T1g, in_=T1g)
                        nc.vector.tensor_tensor(out=T1g, in0=T1g, in1=gp, op=ALU.mult)
                        ins = nc.vector.tensor_tensor(out=Y[:, cc, ts], in0=T1g, in1=HS[:, cc, ts], op=ALU.mult)
                        ytok = P.mark("dve", ins)
                        P.bank_free[b] = ytok
                P.release(tg[0]); P.release(tg[1])
                for oc in range(NCH):
                    for tb in range(NTB):
                        ts = slice(tb * TB, (tb + 1) * TB)
                        b = P.bank()
                        tp = P.group(P.ps[:, b, :], [(apo[kc][:, oc * 128:(oc + 1) * 128], Y[:, kc, ts]) for kc in range(2)], [b], tiles=to, waits=[ytok])
                        P.wait_tok("dve", tp)
                        ins = nc.vector.scalar_tensor_tensor(out=XT[:, oc, ts], in0=P.ps[:, b, :], scalar=G2[:, oc:oc + 1], in1=XT[:, oc, ts], op0=ALU.mult, op1=ALU.add)
                        P.bank_free[b] = P.mark("dve", ins)
                P.release(to[0]); P.release(to[1])
                P.barrier()

        def gla_part(G2):
            P.barrier()
            AR.reset()
            IDENT = ccol("ident", 128)
            MASK = [ccol("maskf", 128), ccol("maskb", 128)]
            RM = ccol("rm", SL)
            CARRY = pcol("carry", 1)
            gng = vcol("gla_ng", 0, 2)
            gb = vcol("gla_gb", 0, 8)
            GL = AR.bf16(2, NT)
            GW2 = AR.bf16(2, 512)
            NGB = AR.f32(8)
            OS = AR.f32(2, NT)
            GS0 = AR.f32(2, SL)
            mark0 = AR.off
            QT = AR.f32(SL); KT = AR.f32(SL); VT = AR.bf16(2, SL)
            LA = AR.f32(SL); PP = AR.f32(SL); TP = AR.f32(SL); PB = AR.f32(SL)
            E1 = AR.f32(SL); E2 = AR.f32(SL); E3 = AR.f32(SL)
            QTd = AR.bf16(SL); KTd = AR.bf16(SL); KEd = AR.f32(SL)
            AT = AR.bf16(2, 128); KET = AR.bf16(2, 128)
            S = AR.f32(SL); Sb = AR.bf16(4, SL); SST = AR.f32(SL); DEC = AR.f32(4)
            end1 = AR.off
            AR.off = mark0
            GG = AR.bf16(2, NT); SQ = AR.bf16(2, TB); RR = AR.f32(TB); T5 = AR.f32(TB); EG = AR.f32(TB)
            assert AR.off <= ARENA_F32 and end1 <= ARENA_F32
            P.pool_sync()
            dgw = P.dma("pool", GW2[0:16, :, :].rearrange("p a b -> p (a b)"), gw2_in[:, :], "gw2")
            P.wait_dma("pe", dgw)
            nc.vector.tensor_scalar(out=NGB, in0=gb, scalar1=-1.0, scalar2=None, op0=ALU.mult)
            tglr = P.load_tile(w_evin[32])
            apglr = P.tile_ap(tglr).rearrange("p (k n) -> p k n", k=16)
            gl_tok = None
            for d in range(2):
                for tb in range(NTB):
                    ts = slice(tb * TB, (tb + 1) * TB)
                    b = P.bank()
                    tp = P.group(P.ps[0:16, b, :], [(apglr[:, kc, d * 16:(d + 1) * 16], HM[:, kc, ts]) for kc in range(16)], [b], tiles=[tglr])
                    P.wait_tok("act", tp)
                    ins = nc.scalar.activation(out=GL[0:16, d, ts], in_=P.ps[0:16, b, :], func=ACT.Copy)
                    gl_tok = P.mark("act", ins)
                    P.bank_free[b] = gl_tok
            P.release(tglr)
            sst_dma = None
            import os
            STOP = int(os.environ.get("GLA_STOP", "99"))
            for hd in range(4 if STOP > 1 else 0):
                P.barrier()
                tq = P.load_tile(w_evin[8 + hd])
                tk = P.load_tile(w_evin[12 + hd])
                tv = [P.load_tile(w_evin[16 + 2 * hd + i]) for i in range(2)]
                apq = P.tile_ap(tq).rearrange("p (k n) -> p k n", k=16)
                apk = P.tile_ap(tk).rearrange("p (k n) -> p k n", k=16)
                apv = [P.tile_ap(t).rearrange("p (k n) -> p k n", k=16) for t in tv]
                P.sp_sync()
                dgs = P.dma("sp", GS0[:].rearrange("p a b -> p (a b)"), gs0_in[hd], "gs0")
                for d in range(2 if STOP > 2 else 0):
                    order = list(range(NSL)) if d == 0 else list(range(NSL - 1, -1, -1))
                    for si, s in enumerate(order):
                        ts = slice(s * SL, (s + 1) * SL)
                        P.barrier()
                        b = P.bank()
                        tp = P.group(P.ps[:, b, 0:SL], [(apq[:, kc, :], HM[:, kc, ts]) for kc in range(16)], [b], tiles=[tq])
                        P.wait_tok("act", tp)
                        P.bank_free[b] = P.mark("act", nc.scalar.activation(out=QT, in_=P.ps[:, b, 0:SL], func=ACT.Copy, scale=128.0 ** -0.5))
                        b = P.bank()
                        tp = P.group(P.ps[:, b, 0:SL], [(apk[:, kc, :], HM[:, kc, ts]) for kc in range(16)], [b], tiles=[tk])
                        P.wait_tok("act", tp)
                        P.bank_free[b] = P.mark("act", nc.scalar.activation(out=KT, in_=P.ps[:, b, 0:SL], func=ACT.Copy))
                        vt_tok = None
                        for tt in range(2):
                            b = P.bank()
                            tsl = slice(s * SL + tt * 128, s * SL + (tt + 1) * 128)
                            for vc in range(2):
                                tp = P.group(P.ps[:, b, vc * 128:(vc + 1) * 128], [(HM[:, kc, tsl], apv[vc][:, kc, :]) for kc in range(16)],
                                             [b] if vc == 0 else [], tiles=[tv[vc]])
                            P.wait_tok("act", tp)
                            vt_tok = P.mark("act", nc.scalar.activation(out=VT[:, tt, :], in_=P.ps[:, b, 0:SL], func=ACT.Copy))
                            P.bank_free[b] = vt_tok
                        if STOP <= 3:
                            continue
                        b = P.bank()
                        tp = P.group(P.ps[:, b, 0:SL], [(GW2[0:16, d, hd * 128:(hd + 1) * 128], GL[0:16, d, ts])], [b], waits=[gl_tok])
                        P.wait_tok("act", tp)
                        nc.scalar.activation(out=E1, in_=P.ps[:, b, 0:SL], func=ACT.Exp, scale=-1.0, bias=NGB[:, d * 4 + hd:d * 4 + hd + 1])
                        tA = P.mark("act", nc.scalar.activation(out=LA, in_=E1, func=ACT.Ln, bias=1.0))
                        P.bank_free[b] = tA
                        P.wait_tok("dve", tA)
                        tsc = P.mark("dve", nc.vector.tensor_tensor_scan(out=PP, data0=RM, data1=LA, initial=0.0, op0=ALU.mult, op1=ALU.add))
                        P.fence("dve", tsc)
                        TOT = PP[:, 63::64]
                        nc.vector.tensor_tensor(out=TP.rearrange("p (c t) -> p c t", c=4), in0=PP.rearrange("p (c t) -> p c t", c=4),
                                                in1=TOT.unsqueeze(2).to_broadcast([128, 4, 64]), op=ALU.subtract)
                        if d == 0:
                            ins = nc.vector.tensor_copy(out=PB, in_=PP)
                            sc = (-1.0 / 16, 1.0 / 16, 1.0 / 16)
                        else:
                            nc.vector.scalar_tensor_tensor(out=PB, in0=TP, scalar=-1.0, in1=LA, op0=ALU.mult, op1=ALU.add)
                            ins = nc.vector.tensor_tensor(out=TP, in0=PP, in1=LA, op=ALU.subtract)
                            sc = (-1.0 / 16, 1.0 / 16, -1.0 / 16)
                        tD = P.mark("dve", ins)
                        P.wait_tok("act", tD)
                        nc.scalar.activation(out=E1, in_=PB, func=ACT.Exp, scale=sc[0])
                        nc.scalar.activation(out=E2, in_=PB, func=ACT.Exp, scale=sc[1])
                        nc.scalar.activation(out=E3, in_=TP, func=ACT.Exp, scale=sc[2])
                        tA2 = P.mark("act", nc.scalar.activation(out=DEC, in_=TOT, func=ACT.Exp, scale=-1.0 / 16))
                        P.wait_tok("dve", tA2)
                        nc.vector.tensor_tensor(out=QTd, in0=QT, in1=E1, op=ALU.mult)
                        nc.vector.tensor_tensor(out=KTd, in0=KT, in1=E2, op=ALU.mult)
                        tD2 = P.mark("dve", nc.vector.tensor_tensor(out=KEd, in0=KT, in1=E3, op=ALU.mult))
                        if STOP <= 4:
                            continue
                        ba = P.bank()
                        for pp in range(2):
                            tl_ = slice(pp * 128, (pp + 1) * 128)
                            tp = P.group(P.ps[:, ba, tl_], [(KTd[:, tl_], QTd[:, tl_])], [ba] if pp == 0 else [], waits=[tD2])
                        P.wait_tok("dve", tp)
                        for pp in range(2):
                            ins = nc.vector.tensor_tensor(out=AT[:, pp, :], in0=P.ps[:, ba, pp * 128:(pp + 1) * 128], in1=MASK[d], op=ALU.mult)
                        tAT = P.mark("dve", ins)
                        P.bank_free[ba] = tAT
                        if STOP == 5 and os.environ.get("GLA_SUB") == "a":
                            continue
                        for pp in range(2):
                            bt = P.bank()
                            tp = P.transpose(P.ps[:, bt, 0:128], KEd[:, pp * 128:(pp + 1) * 128], IDENT, [bt], waits=[tD2])
                            P.wait_tok("act", tp)
                            tKET = P.mark("act", nc.scalar.activation(out=KET[:, pp, :], in_=P.ps[:, bt, 0:128], func=ACT.Copy))
                            P.bank_free[bt] = tKET
                        if STOP <= 5:
                            continue
                        bk = [P.bank() for _ in range(4)]
                        for n in range(4):
                            pp, half = n // 2, n % 2
                            rows = slice(half * 64, half * 64 + 64)
                            tp = P.group(P.ps[:, bk[n], 0:256], [(KET[rows, pp, :], VT[rows, pp, :])],
                                         [bk[n]], waits=[tKET, vt_tok])
                        P.wait_tok("dve", tp)
                        P.wait_dma("dve", dgs)
                        if si == 0:
                            nc.vector.tensor_copy(out=S, in_=GS0[:, d, :])
                        else:
                            nc.vector.tensor_scalar(out=S, in0=S, scalar1=CARRY, scalar2=None, op0=ALU.mult)
                        corder = [0, 1, 2, 3] if d == 0 else [3, 2, 1, 0]
                        for n in corder:
                            pp, half = n // 2, n % 2
                            nc.vector.tensor_copy(out=Sb[:, n, :], in_=S)
                            ins = nc.vector.scalar_tensor_tensor(out=S, in0=S, scalar=DEC[:, n:n + 1], in1=P.ps[:, bk[n], 0:256], op0=ALU.mult, op1=ALU.add)
                        tS = P.mark("dve", ins)
                        for n in range(4):
                            P.bank_free[bk[n]] = tS
                        if sst_dma is not None:
                            P.wait_dma("dve", sst_dma)
                        tSS = P.mark("dve", nc.vector.tensor_copy(out=SST, in_=S))
                        P.wait_tok("sp", tSS)
                        sst_dma = P.dma("sp", gla_out[(s * 2 + d) * 4 + hd], SST, "sst")
                        if STOP <= 6:
                            continue
                        bo = P.bank()
                        first = True
                        for pp in range(2):
                            for vc in range(2):
                                col0 = (pp * 2 + vc) * 128
                                vs = slice(vc * 128, (vc + 1) * 128)
                                items = [(P.ps[:, bo, col0:col0 + 128], VT[:, pp, vs], AT[:, pp, :])]
                                for half in range(2):
                                    n = pp * 2 + half
                                    items.append((P.ps[:, bo, col0 + half * 64:col0 + (half + 1) * 64], Sb[:, n, vs], QTd[:, n * 64:(n + 1) * 64]))
                                tp = P.group2(items, [bo] if first else [], waits=[tAT, tS, vt_tok])
                                first = False
                        P.wait_tok("dve", tp)
                        for pp in range(2):
                            for vc in range(2):
                                col0 = (pp * 2 + vc) * 128
                                dst = OS[:, vc, s * SL + pp * 128:s * SL + (pp + 1) * 128]
                                if d == 0:
                                    ins = nc.vector.tensor_copy(out=dst, in_=P.ps[:, bo, col0:col0 + 128])
                                else:
                                    ins = nc.vector.tensor_tensor(out=dst, in0=dst, in1=P.ps[:, bo, col0:col0 + 128], op=ALU.add)
                        P.bank_free[bo] = P.mark("dve", ins)
                for t in (tq, tk, tv[0], tv[1]):
                    P.release(t)
                P.barrier()
                tg = [P.load_tile(w_evin[24 + 2 * hd + i]) for i in range(2)]
                to = [P.load_tile(w_evout[8 + 2 * hd + i]) for i in range(2)]
                apg = [P.tile_ap(t).rearrange("p (k n) -> p k n", k=16) for t in tg]
                apo = [P.tile_ap(t) for t in to]
                ogt = None
                for tb in range(NTB):
                    ts = slice(tb * TB, (tb + 1) * TB)
                    for vc in range(2):
                        b = P.bank()
                        tp = P.group(P.ps[:, b, :], [(apg[vc][:, kc, :], HM[:, kc, ts]) for kc in range(16)], [b], tiles=[tg[vc]])
                        P.wait_tok("act", tp)
                        P.wait_tok("act", ogt)
                        tA = P.mark("act", nc.scalar.activation(out=EG, in_=P.ps[:, b, :], func=ACT.Exp, scale=-1.0))
                        P.wait_tok("dve", tA)
                        nc.vector.tensor_scalar(out=EG, in0=EG, scalar1=1.0, scalar2=None, op0=ALU.add)
                        nc.vector.reciprocal(out=EG, in_=EG)
                        ins = nc.vector.tensor_tensor(out=GG[:, vc, ts], in0=EG, in1=P.ps[:, b, :], op=ALU.mult)
                        ogt = P.mark("dve", ins)
                        P.bank_free[b] = ogt
                    P.wait_tok("act", ogt)
                    for vc in range(2):
                        ins = nc.scalar.activation(out=SQ[:, vc, :], in_=OS[:, vc, ts], func=ACT.Square)
                    tSq = P.mark("act", ins)
                    b = P.bank()
                    tp = P.group(P.ps[:, b, :], [(ONES[:, :], SQ[:, vc, :]) for vc in range(2)], [b], waits=[tSq])
                    P.wait_tok("act", tp)
                    nc.scalar.activation(out=RR, in_=P.ps[:, b, :], func=ACT.Ln, scale=1.0 / 256, bias=EPSB[:, 0:1])
                    tR = P.mark("act", nc.scalar.activation(out=RR, in_=RR, func=ACT.Exp, scale=-0.5))
                    P.bank_free[b] = tR
                    P.wait_tok("dve", tR)
                    for vc in range(2):
                        nc.vector.scalar_tensor_tensor(out=T5, in0=OS[:, vc, ts], scalar=gng[:, vc:vc + 1], in1=RR, op0=ALU.mult, op1=ALU.mult)
                        ins = nc.vector.tensor_tensor(out=GG[:, vc, ts], in0=T5, in1=GG[:, vc, ts], op=ALU.mult)
                    ogt = P.mark("dve", ins)
                P.release(tg[0]); P.release(tg[1])
                for oc in range(NCH):
                    for tb in range(NTB):
                        ts = slice(tb * TB, (tb + 1) * TB)
                        b = P.bank()
                        tp = P.group(P.ps[:, b, :], [(apo[kc][:, oc * 128:(oc + 1) * 128], GG[:, kc, ts]) for kc in range(2)], [b], tiles=to, waits=[ogt])
                        P.wait_tok("dve", tp)
                        ins = nc.vector.scalar_tensor_tensor(out=XT[:, oc, ts], in0=P.ps[:, b, :], scalar=G2[:, oc:oc + 1], in1=XT[:, oc, ts], op0=ALU.mult, op1=ALU.add)
                        P.bank_free[b] = P.mark("dve", ins)
                P.release(to[0]); P.release(to[1])
            P.barrier()
            if sst_dma is not None:
                for e in ("dve", "act", "sp"):
                    P.wait_dma(e, sst_dma)

        def s5_part(G2):
            PI = math.pi
            P.barrier()
            AR.reset()
            Bw = AR.bf16(2, 2, 8 * 128)
            Cw = AR.bf16(2, 2, 32 * 32)
            COEF = AR.f32(2, 2, 64)
            S5H0 = AR.f32(2, 64)
            S5ST = AR.f32(NSL * 2, 64)
            UT = AR.bf16(8, NT)
            ut_off = AR.off - 4096
            RT = Arena(P.ring[:].rearrange("p s n -> p (s n)").bitcast(F32))
            RTN = NSLOT * TSZ // 2
            mA = ccol("s5ma", 2)
            mB = ccol("s5mb", 2)
            CARRY = pcol("carry", 1)
            P.sp_sync()
            d0 = P.dma("sp", S5H0[:].rearrange("p a b -> p (a b)"), s5h0_in[:, :], "s5h0")

            def alloc(n):
                v = RT.ap[:, RT.off:RT.off + n]
                RT.off += n
                assert RT.off <= RTN, "ring temp overflow"
                return v

            def coefs(lre, lim, ls, n, want_f):
                DT = alloc(n); ZR = alloc(n); ZI = alloc(n); MAG = alloc(n); KF = alloc(n); KI = alloc(n).bitcast(I32)
                W = alloc(n); M = alloc(n); SN = alloc(n); CS = alloc(n)
                tA = P.mark("act", nc.scalar.activation(out=DT, in_=ls, func=ACT.Exp))
                P.wait_tok("dve", tA)
                nc.vector.tensor_tensor(out=ZR, in0=lre, in1=DT, op=ALU.mult)
                tD = P.mark("dve", nc.vector.tensor_tensor(out=ZI, in0=lim, in1=DT, op=ALU.mult))
                P.wait_tok("act", tD)
                tM = P.mark("act", nc.scalar.activation(out=MAG, in_=ZR, func=ACT.Exp))
                res = {}
                for name, shift, dst in (("sin", 0.0, SN), ("cos", PI / 2, CS)):
                    src = ZI
                    if shift:
                        nc.vector.tensor_scalar(out=DT, in0=ZI, scalar1=shift, scalar2=None, op0=ALU.add)
                        src = DT
                    nc.vector.tensor_scalar(out=KF, in0=src, scalar1=1.0 / (2 * PI), scalar2=None, op0=ALU.mult)
                    nc.vector.tensor_copy(out=KI, in_=KF)
                    nc.vector.tensor_copy(out=KF, in_=KI)
                    nc.vector.scalar_tensor_tensor(out=W, in0=KF, scalar=-2 * PI, in1=src, op0=ALU.mult, op1=ALU.add)
                    nc.vector.tensor_scalar(out=M, in0=W, scalar1=PI, scalar2=None, op0=ALU.is_gt)
                    nc.vector.scalar_tensor_tensor(out=W, in0=M, scalar=-2 * PI, in1=W, op0=ALU.mult, op1=ALU.add)
                    nc.vector.tensor_scalar(out=M, in0=W, scalar1=-PI, scalar2=None, op0=ALU.is_lt)
                    nc.vector.scalar_tensor_tensor(out=W, in0=M, scalar=2 * PI, in1=W, op0=ALU.mult, op1=ALU.add)
                    nc.vector.tensor_scalar(out=W, in0=W, scalar1=PI, scalar2=-PI, op0=ALU.min, op1=ALU.max)
                    tW = P.mark("dve", nc.vector.tensor_copy(out=M, in_=W))
                    P.wait_tok("act", tW)
                    tS = P.mark("act", nc.scalar.activation(out=dst, in_=M, func=ACT.Sin))
                    P.wait_tok("dve", tS)
                P.wait_tok("dve", tM)
                ABR = alloc(n); ABI = alloc(n)
                nc.vector.tensor_tensor(out=ABR, in0=MAG, in1=CS, op=ALU.mult)
                nc.vector.tensor_tensor(out=ABI, in0=MAG, in1=SN, op=ALU.mult)
                res["ab_re"], res["ab_im"] = ABR, ABI
                if want_f:
                    NRE = KF; DEN = W; FRE = alloc(n); FIM = alloc(n)
                    nc.vector.tensor_scalar(out=NRE, in0=ABR, scalar1=-1.0, scalar2=None, op0=ALU.add)
                    nc.vector.tensor_tensor(out=DEN, in0=lre, in1=lre, op=ALU.mult)
                    nc.vector.tensor_tensor(out=M, in0=lim, in1=lim, op=ALU.mult)
                    nc.vector.tensor_tensor(out=DEN, in0=DEN, in1=M, op=ALU.add)
                    nc.vector.reciprocal(out=DEN, in_=DEN)
                    nc.vector.tensor_tensor(out=FRE, in0=NRE, in1=lre, op=ALU.mult)
                    nc.vector.tensor_tensor(out=M, in0=ABI, in1=lim, op=ALU.mult)
                    nc.vector.tensor_tensor(out=FRE, in0=FRE, in1=M, op=ALU.add)
                    nc.vector.tensor_tensor(out=FRE, in0=FRE, in1=DEN, op=ALU.mult)
                    nc.vector.tensor_tensor(out=FIM, in0=ABI, in1=lre, op=ALU.mult)
                    nc.vector.tensor_tensor(out=M, in0=NRE, in1=lim, op=ALU.mult)
                    nc.vector.tensor_tensor(out=FIM, in0=FIM, in1=M, op=ALU.subtract)
                    nc.vector.tensor_tensor(out=FIM, in0=FIM, in1=DEN, op=ALU.mult)
                    res["f_re"], res["f_im"] = FRE, FIM
                return res

            RT.off = 0
            LB = alloc(192)
            CB = alloc(2048)
            dl = P.dma("sp", LB, s5lb_in[:, :], "s5l")
            dc = P.dma("sp", CB, s5cb_in[:, :], "s5l")
            for e in ("act", "dve"):
                P.wait_dma(e, dc)
            LBv = LB.rearrange("p (k n) -> p k n", k=3)
            r = coefs(LBv[:, 0, :], LBv[:, 1, :], LBv[:, 2, :], 64, False)
            for d in range(2):
                ar = r["ab_re"][:, d * 32:(d + 1) * 32]
                ai = r["ab_im"][:, d * 32:(d + 1) * 32]
                nc.vector.tensor_copy(out=COEF[:, d, 0, 0:32], in_=ar)
                nc.vector.tensor_copy(out=COEF[:, d, 0, 32:64], in_=ar)
                nc.vector.tensor_scalar(out=COEF[:, d, 1, 0:32], in0=ai, scalar1=-1.0, scalar2=None, op0=ALU.mult)
                nc.vector.tensor_copy(out=COEF[:, d, 1, 32:64], in_=ai)
            CBv = CB.rearrange("p (c d q n) -> p c d q n", c=2, d=2, q=32)
            for d in range(2):
                for c in range(2):
                    for g2 in range(2):
                        dst = Cw[:, d, c, :].rearrange("p (q m) -> p q m", q=32)[:, :, g2 * 16:(g2 + 1) * 16]
                        nc.vector.tensor_scalar(out=dst, in0=CBv[:, c, d, :, :], scalar1=mB[:, g2:g2 + 1], scalar2=(1.0 if c == 0 else -1.0), op0=ALU.mult, op1=ALU.mult)
            for d in range(2):
              for hh in range(2):
                P.sp_sync()
                RT.off = 0
                LA_ = alloc(768)
                BA = alloc(512)
                dl = P.dma("sp", LA_.rearrange("p (k n) -> p k n", k=3), s5la_in[:, :].rearrange("p (k d h n) -> p k d h n", k=3, d=2, h=2)[:, :, d, hh, :], "s5l")
                dc = P.dma("sp", BA.rearrange("p (c n) -> p c n", c=2), s5ba_in[:, :].rearrange("p (c d h n) -> p c d h n", c=2, d=2, h=2)[:, :, d, hh, :], "s5l")
                for e in ("act", "dve"):
                    P.wait_dma(e, dc)
                LAv = LA_.rearrange("p (k n) -> p k n", k=3)
                r = coefs(LAv[:, 0, :], LAv[:, 1, :], LAv[:, 2, :], 256, True)
                BAv = BA.rearrange("p (c n) -> p c n", c=2)
                BBR = alloc(256); BBI = alloc(256); TM = alloc(256)
                nc.vector.tensor_tensor(out=BBR, in0=r["f_re"], in1=BAv[:, 0, :], op=ALU.mult)
                nc.vector.tensor_tensor(out=TM, in0=r["f_im"], in1=BAv[:, 1, :], op=ALU.mult)
                nc.vector.tensor_tensor(out=BBR, in0=BBR, in1=TM, op=ALU.subtract)
                nc.vector.tensor_tensor(out=BBI, in0=r["f_re"], in1=BAv[:, 1, :], op=ALU.mult)
                nc.vector.tensor_tensor(out=TM, in0=r["f_im"], in1=BAv[:, 0, :], op=ALU.mult)
                nc.vector.tensor_tensor(out=BBI, in0=BBI, in1=TM, op=ALU.add)
                for c, src in ((0, BBR), (1, BBI)):
                    for g2 in range(2):
                        dst = Bw[:, d, c, hh * 512:(hh + 1) * 512].rearrange("p (h m) -> p h m", h=4)[:, :, g2 * 64:(g2 + 1) * 64]
                        nc.vector.tensor_scalar(out=dst, in0=src.rearrange("p (h m) -> p h m", h=4), scalar1=mA[:, g2:g2 + 1], scalar2=None, op0=ALU.mult)
            P.barrier()
            import os
            S5STOP = int(os.environ.get("S5_STOP", "99"))
            tfree = P.mark("dve", nc.vector.memset(P.scr[:, 2:3], 0.0))
            for s_ in range(NSLOT):
                P.slot_free[s_] = tfree
            if S5STOP <= 1:
                dump(COEF[:].rearrange("p a b c -> p (a b c)"))
                dump(XT[:, 0, :])
                dump(ARN[:, 0:2048])
                return
            for ch in range(8):
                t = P.load_tile(w_evin[ch])
                ap = P.tile_ap(t).rearrange("p (k n) -> p k n", k=16)
                for tb in range(NTB):
                    ts = slice(tb * TB, (tb + 1) * TB)
                    b = P.bank()
                    tp = P.group(P.ps[:, b, :], [(ap[:, kc, :], HM[:, kc, ts]) for kc in range(16)], [b], tiles=[t])
                    P.wait_tok("act", tp)
                    P.bank_free[b] = P.mark("act", nc.scalar.activation(out=UT[:, ch, ts], in_=P.ps[:, b, :], func=ACT.Copy))
                P.release(t)
            P.barrier()
            YS = HM[:].rearrange("p c t -> p (c t)").bitcast(F32).rearrange("p (c t) -> p c t", c=8)
            sd = vcol("s5_d", 0, 8)
            for ch in range(8):
                nc.vector.tensor_scalar(out=YS[:, ch, :], in0=UT[:, ch, :], scalar1=sd[:, ch:ch + 1], scalar2=None, op0=ALU.mult)
            P.barrier()
            if S5STOP <= 2:
                return
            RT.off = 0
            Hbuf = [alloc(1024).rearrange("p (t m) -> p t m", t=16) for _ in range(2)]
            Hbb = [alloc(512).bitcast(BF16).rearrange("p (t m) -> p t m", t=16) for _ in range(2)]
            T1 = AR.f32(64); T2 = AR.f32(64); HIN = AR.f32(64)
            Cw3 = alloc(1024).bitcast(BF16).rearrange("p (d c m) -> p d c m", d=2, c=2)
            nc.vector.memset(Cw3[:].rearrange("p d c m -> p (d c m)"), 0.0)
            for d_ in range(2):
                for c_ in range(2):
                    nc.vector.tensor_copy(out=Cw3[:, d_, c_, :].rearrange("p (h m) -> p h m", h=8)[:, :, 32:64],
                                          in_=Cw[:, d_, c_, :].rearrange("p (h r m) -> p h r m", h=8, r=4)[:, :, 3, :])
            Bw3 = alloc(2048).bitcast(BF16).rearrange("p (d c m) -> p d c m", d=2, c=2)
            m96 = ccol("m96", 1)
            tb3 = P.mark("dve", nc.vector.tensor_scalar(out=Bw3[:].rearrange("p d c m -> p (d c m)"), in0=Bw[:].rearrange("p d c m -> p (d c m)"), scalar1=m96, scalar2=None, op0=ALU.mult))
            P.wait_tok("pe", tb3)
            P.wait_dma("dve", d0)
            NBLK = NT // 16
            ST = AR.f32(64)
            seq = []
            for d in range(2):
                blocks = list(range(NBLK)) if d == 0 else list(range(NBLK - 1, -1, -1))
                for bi, blk in enumerate(blocks):
                    seq.append((d, bi, blk))
            NS = len(seq)
            tE = [None] * NS
            tB = [None] * NS
            yfree = [None, None]
            pe_y = [None, None]
            hb_done = [None, None]

            def emit_expand(i):
                d, bi, blk = seq[i]
                t0 = blk * 16
                P.wait_tok("pe", tB[i - 1] if i > 0 else None)
                last = None
                for c in range(2):
                    for q in range(32):
                        rg = q % 4
                        col = c * 128 + (q // 4) * 16
                        if rg < 3:
                            rows = slice(32 * rg, 32 * rg + 32)
                            wsrc = Bw[rows, d, c, (q // 4) * 128:(q // 4 + 1) * 128]
                        else:
                            rows = slice(64, 128)
                            wsrc = Bw3[rows, d, c, (q // 4) * 128:(q // 4 + 1) * 128]
                        last = nc.tensor.matmul(P.ps[:, rg, col:col + 16], wsrc, UT[rows, q // 4, t0:t0 + 16], start=True, stop=True)
                tE[i] = P.mark("pe", last)
                P.last_pe = tE[i]

            def emit_bcopy(i):
                par = i % 2
                H = Hbuf[par]
                P.wait_tok("act", tE[i])
                ins = None
                for rg in range(4):
                    o_ap = H[:].rearrange("p t (c q r) -> p t c q r", c=2, r=4)[:, :, :, :, rg]
                    i_ap = P.ps[:, rg, 0:256].rearrange("p (c q t) -> p t c q", c=2, q=8)
                    ins = nc.scalar.activation(out=o_ap, in_=i_ap, func=ACT.Copy)
                tB[i] = P.mark("act", ins)

            emit_expand(0)
            emit_bcopy(0)
            pend = None
            for i in range(NS):
                d, bi, blk = seq[i]
                par = i % 2
                t0 = blk * 16
                A1 = COEF[:, d, 0, :]
                A2 = COEF[:, d, 1, :]
                H = Hbuf[par]
                if i + 1 < NS:
                    emit_expand(i + 1)
                    emit_bcopy(i + 1)
                P.wait_tok("dve", tB[i])
                steps = list(range(16)) if d == 0 else list(range(15, -1, -1))
                slot_first = (t0 % SL == 0) if d == 0 else ((t0 + 16) % SL == 0)
                slot_last = ((t0 + 16) % SL == 0) if d == 0 else (t0 % SL == 0)
                if slot_first:
                    if bi == 0:
                        nc.vector.tensor_copy(out=HIN, in_=S5H0[:, d, :])
                    else:
                        nc.vector.tensor_scalar(out=HIN, in0=ST, scalar1=CARRY, scalar2=None, op0=ALU.mult)
                    prev = HIN
                else:
                    prev = ST
                for t in steps:
                    nc.vector.tensor_tensor(out=T1, in0=A1, in1=prev, op=ALU.mult)
                    nc.vector.tensor_tensor(out=T2[:, 0:32], in0=A2[:, 0:32], in1=prev[:, 32:64], op=ALU.mult)
                    nc.vector.tensor_tensor(out=T2[:, 32:64], in0=A2[:, 32:64], in1=prev[:, 0:32], op=ALU.mult)
                    nc.vector.tensor_tensor(out=T1, in0=T1, in1=T2, op=ALU.add)
                    nc.vector.tensor_tensor(out=H[:, t, :], in0=T1, in1=H[:, t, :], op=ALU.add)
                    prev = H[:, t, :]
                if slot_last:
                    s_idx = t0 // SL
                    nc.vector.tensor_copy(out=S5ST[:, s_idx * 2 + d, :], in_=prev)
                tSc = P.mark("dve", nc.vector.tensor_copy(out=ST, in_=prev))
                if pend is not None:
                    pb, pt0 = pend
                    P.wait_tok("dve", pe_y[pb])
                    dst = YS[:, :, pt0:pt0 + 16]
                    ins = nc.vector.tensor_tensor(out=dst, in0=dst, in1=P.ps[:, 4 + pb, 0:128].rearrange("p (c t) -> p c t", c=8), op=ALU.add)
                    yfree[pb] = P.mark("dve", ins)
                if S5STOP <= 3:
                    continue
                P.wait_tok("act", tSc)
                P.wait_tok("act", pe_y[par])
                hb_done[par] = P.mark("act", nc.scalar.activation(out=Hbb[par][:].rearrange("p t m -> p (t m)"), in_=H[:].rearrange("p t m -> p (t m)"), func=ACT.Copy))
                P.wait_tok("pe", hb_done[par])
                P.wait_tok("pe", yfree[par])
                last = None
                for ch in range(8):
                    for rg in (3, 0, 1, 2):
                        q = 4 * ch + rg
                        for c in range(2):
                            if rg < 3:
                                o_ap = P.ps[32 * rg:32 * rg + 32, 4 + par, ch * 16:ch * 16 + 16]
                                w_ap = Cw[:, d, c, q * 32:(q + 1) * 32]
                            else:
                                o_ap = P.ps[64:128, 4 + par, ch * 16:ch * 16 + 16]
                                w_ap = Cw3[:, d, c, ch * 64:(ch + 1) * 64]
                            last = nc.tensor.matmul(o_ap, w_ap, Hbb[par][:, :, c * 32 + q], start=(c == 0), stop=(c == 1))
                pe_y[par] = P.mark("pe", last)
                P.last_pe = pe_y[par]
                pend = (par, t0)
            if S5STOP <= 3:
                P.barrier()
                for b in range(8):
                    P.bank_free[b] = None
                return
            pb, pt0 = pend
            P.wait_tok("dve", pe_y[pb])
            dst = YS[:, :, pt0:pt0 + 16]
            nc.vector.tensor_tensor(out=dst, in0=dst, in1=P.ps[:, 4 + pb, 0:128].rearrange("p (c t) -> p c t", c=8), op=ALU.add)
            P.barrier()
            for b in range(8):
                P.bank_free[b] = None
            YA = UT
            RT.off = 0
            G1 = alloc(NT)
            for ch in range(8):
                y = YS[:, ch, :]
                tA = P.mark("act", nc.scalar.activation(out=G1, in_=y, func=ACT.Square))
                P.wait_tok("dve", tA)
                nc.vector.tensor_scalar(out=G1, in0=G1, scalar1=0.044715, scalar2=1.0, op0=ALU.mult, op1=ALU.add)
                tD = P.mark("dve", nc.vector.tensor_tensor(out=G1, in0=G1, in1=y, op=ALU.mult))
                P.wait_tok("act", tD)
                tA2 = P.mark("act", nc.scalar.activation(out=G1, in_=G1, func=ACT.Exp, scale=-GELU_C))
                P.wait_tok("dve", tA2)
                nc.vector.tensor_scalar(out=G1, in0=G1, scalar1=1.0, scalar2=None, op0=ALU.add)
                nc.vector.reciprocal(out=G1, in_=G1)
                tD2 = P.mark("dve", nc.vector.tensor_tensor(out=YA[:, ch, :], in0=G1, in1=y, op=ALU.mult))
                P.wait_tok("act", tD2)
            P.barrier()
            tfree = P.mark("dve", nc.vector.memset(P.scr[:, 2:3], 0.0))
            for s_ in range(NSLOT):
                P.slot_free[s_] = tfree
            YG = HM[:, 0:8, :]
            NGLB = S5H0[:, 0, 0:8]
            nc.vector.tensor_scalar(out=NGLB, in0=vcol("glu_b", 0, 8), scalar1=-1.0, scalar2=None, op0=ALU.mult)
            ygt = None
            P.barrier()
            for oc in range(8):
                t = P.load_tile(w_glu[oc], n=1024)
                ap = P.tile_ap(t)[:, 0:1024].rearrange("p (k n) -> p k n", k=8)
                for tb in range(NTB):
                    ts = slice(tb * TB, (tb + 1) * TB)
                    b = P.bank()
                    tp = P.group(P.ps[:, b, :], [(ap[:, kc, :], YA[:, kc, ts]) for kc in range(8)], [b], tiles=[t])
                    P.wait_tok("act", tp)
                    tA = P.mark("act", nc.scalar.activation(out=P.ps[:, b, :], in_=P.ps[:, b, :], func=ACT.Exp, scale=-1.0, bias=NGLB[:, oc:oc + 1]))
                    P.wait_tok("dve", tA)
                    nc.vector.tensor_scalar(out=P.ps[:, b, :], in0=P.ps[:, b, :], scalar1=1.0, scalar2=None, op0=ALU.add)
                    nc.vector.reciprocal(out=P.ps[:, b, :], in_=P.ps[:, b, :])
                    ygt = P.mark("dve", nc.vector.tensor_tensor(out=YG[:, oc, ts], in0=P.ps[:, b, :], in1=YA[:, oc, ts], op=ALU.mult))
                    P.bank_free[b] = ygt
                P.release(t)
            for half in range(2):
                tiles = [P.load_tile(w_evout[half * 4 + j]) for j in range(4)]
                aps = [P.tile_ap(t) for t in tiles]
                for oc in range(NCH):
                    for tb in range(NTB):
                        ts = slice(tb * TB, (tb + 1) * TB)
                        b = P.bank()
                        tp = P.group(P.ps[:, b, :], [(aps[j][:, oc * 128:(oc + 1) * 128], YG[:, half * 4 + j, ts]) for j in range(4)], [b], tiles=tiles, waits=[ygt])
                        P.wait_tok("dve", tp)
                        ins = nc.vector.scalar_tensor_tensor(out=XT[:, oc, ts], in0=P.ps[:, b, :], scalar=G2[:, oc:oc + 1], in1=XT[:, oc, ts], op0=ALU.mult, op1=ALU.add)
                        P.bank_free[b] = P.mark("dve", ins)
                for t in tiles:
                    P.release(t)
            P.barrier()
            P.wait("sp", P.ms["dve"], P.msn["dve"])
            ds = P.dma("sp", s5_out[:, :], S5ST[:].rearrange("p a b -> p (a b)"), "s5o")
            for e in ("dve", "act", "pe"):
                P.wait_dma(e, ds)

        P.wait_dma("dve", ldv)
        lru_prep()
        for l in range(2):
            mvt = ada(l)
            hm = norm(MV[:, l, 0, :], MV[:, l, 1, :], mvt)
            ffn(2 * l, MV[:, l, 2, :], hm)
            hm = norm(MV[:, l, 3, :], MV[:, l, 4, :], mvt)
            if l == 0 and mix_gla:
                gla_part(MV[:, l, 5, :])
            if l == 0 and mix_s5:
                s5_part(MV[:, l, 5, :])
            if l == 1 and mix_odd:
                odd_mixer(l, MV[:, l, 5, :])
            hm = norm(MV[:, l, 6, :], MV[:, l, 7, :], mvt)
            ffn(2 * l + 1, MV[:, l, 8, :], hm)
        fin = norm(vcol("final_g"), None, None, final=True)
        P.barrier()
        for e in ("sp",):
            P.wait_tok(e, fin)
            P.wait("sp", P.ms["dve"], P.msn["dve"])
            P.wait("sp", P.ms["act"], P.msn["act"])
        o1 = P.dma("sp", yT_out[:, :], XT[:].rearrange("p c t -> p (c t)"), "out")
        o2 = P.dma("sp", lru_out[:, :], LRUST[:].rearrange("p s d c -> p (s d c)"), "out")
        P.wait_dma("sp", o2)
    return nc


def _offsets(items):
    off = {}
    o = 0
    for n, w in items:
        off[n] = o
        o += w
    return off, o


VOFF, NVEC = _offsets([("norm_g", 96), ("final_g", 16), ("ada_b", 288), ("conv_w", 64), ("conv_b", 16),
                       ("lru_ba", 32), ("lru_bx", 32), ("lru_lam", 32), ("gla_gb", 8), ("gla_ng", 2), ("s5_d", 8), ("glu_b", 8)])
COFF, NCST = _offsets([("ident", 128), ("jrev", 128), ("maskf", 128), ("maskb", 128), ("rm", SL), ("s5ma", 2), ("s5mb", 2), ("m96", 1)])
POFF, NPC = _offsets([("cvm", 3 * SL), ("carry", 1), ("lh0", 2 * NCH)])


def pvec(v):
    v = np.asarray(v, np.float32).reshape(-1, 128)
    return np.ascontiguousarray(v.T)


def tile_cols(W, c0, ncols=128):
    K = W.shape[0]
    t = W[:, c0:c0 + ncols].reshape(K // 128, 128, ncols).transpose(1, 0, 2)
    return t.reshape(128, -1)


def prep_shared(inp):
    sh = {}
    ada_w = inp["ada_w"]
    wa = np.empty((2 * 144, 128, TSZ), np.float32)
    for l in range(2):
        wa[l * 144:(l + 1) * 144] = ada_w[l].reshape(16, 128, 144, 128).transpose(2, 1, 0, 3).reshape(144, 128, TSZ)
    sh["w_ada"] = wa
    wi = np.empty((4 * 88, 128, TSZ), np.float32)
    wo = np.empty((4 * 44, 128, TSZ), np.float32)
    for l in range(2):
        for w in range(2):
            f = 2 * l + w
            Wi = inp["ffn_w_in"][l, w].reshape(16, 128, 88, 128).transpose(2, 1, 0, 3).reshape(88, 128, TSZ)
            wi[f * 88:(f + 1) * 88:2] = Wi[0:44]
            wi[f * 88 + 1:(f + 1) * 88:2] = Wi[44:88]
            wo[f * 44:(f + 1) * 44] = inp["ffn_w_out"][l, w].reshape(44, 128, TSZ)
    sh["w_in"] = wi
    sh["w_out"] = wo
    sh["w_odin"] = np.ascontiguousarray(inp["od_w_in"][0].reshape(16, 128, 32, 128).transpose(2, 1, 0, 3).reshape(32, 128, TSZ))
    sh["w_odout"] = np.ascontiguousarray(inp["od_w_out"][0].reshape(16, 128, TSZ))
    wl = np.stack([inp["lru_wa"][0], inp["lru_wx"][0]], 0)
    wl = wl.reshape(2, 2, 8, 2, 128, 256).transpose(2, 4, 0, 1, 3, 5)
    sh["w_lru"] = np.ascontiguousarray(wl).reshape(8, 128, TSZ)
    vec = np.zeros((128, NVEC), np.float32)

    def put(name, arr):
        a = pvec(np.asarray(arr).reshape(-1))
        vec[:, VOFF[name]:VOFF[name] + a.shape[1]] = a
    put("norm_g", inp["norm_g"])
    put("final_g", inp["final_norm_g"])
    put("ada_b", inp["ada_b"])
    put("conv_w", inp["lru_conv_w"][0])
    put("conv_b", inp["lru_conv_b"][0])
    put("lru_ba", inp["lru_ba"][0])
    put("lru_bx", inp["lru_bx"][0])
    put("lru_lam", inp["lru_lam"][0])
    put("gla_gb", inp["gla_gate_b"][0])
    put("gla_ng", inp["gla_norm_g"][0])
    evin = np.zeros((2048, 33 * 128), np.float32)
    evin[:, :4128] = inp["ev_w_in"][0]
    sh["w_evin"] = np.ascontiguousarray(evin.reshape(16, 128, 33, 128).transpose(2, 1, 0, 3).reshape(33, 128, TSZ))
    sh["w_evout"] = np.ascontiguousarray(inp["ev_w_out"][0].reshape(16, 128, TSZ))
    sh["gw2"] = np.ascontiguousarray(inp["gla_gate_w2"][0].transpose(1, 0, 2).reshape(16, 1024))
    put("s5_d", inp["s5_d"][0])
    put("glu_b", inp["s5_glu_b"][0])
    sh["w_glu"] = np.ascontiguousarray(inp["s5_glu_w"][0].reshape(8, 128, 8, 128).transpose(2, 1, 0, 3).reshape(8, 128, 1024))
    lre, lim = inp["s5_lam_re"][0], inp["s5_lam_im"][0]
    ls = np.broadcast_to(inp["s5_log_step"][0][:, :, None], (2, 64, 64))
    def layB(a):
        return a.reshape(2, 32, 2, 64).transpose(2, 3, 0, 1).reshape(128, 64)
    sh["s5lb"] = np.ascontiguousarray(np.stack([layB(lre), layB(lim), layB(ls)], 1).reshape(128, 192)).astype(np.float32)
    def layCB(a):
        return a.reshape(2, 32, 2, 16, 64).transpose(2, 4, 0, 1, 3).reshape(128, 2, 32, 16)
    sh["s5cb"] = np.ascontiguousarray(np.stack([layCB(inp["s5_c_re"][0]), layCB(inp["s5_c_im"][0])], 1).reshape(128, 2048))
    def layA(a):
        t = a.reshape(2, 8, 8, 64).transpose(2, 0, 1, 3)
        return np.broadcast_to(t[:, None], (8, 16, 2, 8, 64)).reshape(128, 1024)
    sh["s5la"] = np.ascontiguousarray(np.stack([layA(lre), layA(lim), layA(ls)], 1).reshape(128, 3072)).astype(np.float32)
    def layBA(a):
        return a.reshape(2, 8, 8, 64, 16).transpose(2, 4, 0, 1, 3).reshape(128, 1024)
    sh["s5ba"] = np.ascontiguousarray(np.stack([layBA(inp["s5_b_re"][0]), layBA(inp["s5_b_im"][0])], 1).reshape(128, 2048))
    sh["vecs"] = vec
    cst = np.zeros((128, NCST), np.float32)
    cst[:, COFF["ident"]:COFF["ident"] + 128] = np.eye(128, dtype=np.float32)
    cst[:, COFF["jrev"]:COFF["jrev"] + 128] = np.eye(128, dtype=np.float32)[::-1]
    jj = np.arange(128)[:, None]
    ii = np.arange(128)[None, :]
    same = (jj // 64) == (ii // 64)
    cst[:, COFF["maskf"]:COFF["maskf"] + 128] = (same & (jj <= ii)).astype(np.float32)
    cst[:, COFF["maskb"]:COFF["maskb"] + 128] = (same & (jj >= ii)).astype(np.float32)
    cst[:, COFF["rm"]:COFF["rm"] + SL] = (np.arange(SL) % 64 != 0).astype(np.float32)[None, :]
    pidx = np.arange(128)
    for g2 in range(2):
        cst[:, COFF["s5ma"] + g2] = ((pidx // 16) % 2 == g2).astype(np.float32)
        cst[:, COFF["s5mb"] + g2] = ((pidx // 64) == g2).astype(np.float32)
    cst[:, COFF["m96"]] = (pidx >= 96).astype(np.float32)
    sh["consts"] = cst
    return sh


PROMPT_ASSIGN = [[0, 1, 2], [3, 4, 5], [6, 7, 8], [9, 10, 11], [12, 13], [14, 15]]


def prep_core(inp, core):
    m = {}
    x = np.zeros((NT, D), np.float32)
    pc = np.zeros((128, NPC), np.float32)
    t = np.arange(SL)
    if core < 2:
        x[:] = inp["x_sample"][core]
        cond = inp["c"][core]
        seg = 64
        pc[:, POFF["carry"]] = 1.0
        lh0 = np.stack([pvec(inp["state_lru"][core, 0, d]) for d in range(2)], 1)
        pc[:, POFF["lh0"]:POFF["lh0"] + 32] = lh0.reshape(128, 32)
    else:
        for s, b in enumerate(PROMPT_ASSIGN[core - 2]):
            x[SL * s:SL * (s + 1)] = inp["x_prompt"][b]
        cond = inp["c_ctx"]
        seg = 256
    cvm = np.stack([(t % seg >= 2), (t % seg >= 1), (t % seg <= seg - 2)], 0).astype(np.float32)
    pc[:, POFF["cvm"]:POFF["cvm"] + 3 * SL] = cvm.reshape(1, -1)
    m["pcore"] = pc
    gs0 = np.zeros((4, 128, 2, SL), np.float32)
    if core < 2:
        gs0[:] = inp["state_gla"][core, 0].transpose(1, 2, 0, 3)
    m["gs0"] = gs0.reshape(4, 128, 2 * SL)
    h0 = np.zeros((128, 2, 2, 32), np.float32)
    if core < 2:
        for c, nm in enumerate(("state_s5_re", "state_s5_im")):
            h0[:, :, c, :] = inp[nm][core, 0].reshape(2, 32, 2, 64).transpose(2, 3, 0, 1).reshape(128, 2, 32)
    m["s5h0"] = h0.reshape(128, 128)
    m["xT"] = np.ascontiguousarray(x.T.reshape(NCH, 128, NT).transpose(1, 0, 2)).reshape(128, NCH * NT)
    m["cond"] = pvec(cond)
    return m


def assemble(inp, results):
    yp = np.zeros((16, 256, D), np.float32)
    ys = np.zeros((2, 1024, D), np.float32)
    st_re = np.zeros((16, 1, 2, 64, 64), np.float32)
    st_im = np.zeros((16, 1, 2, 64, 64), np.float32)
    st_gla = np.zeros((16, 1, 2, 4, 128, 256), np.float32)
    st_lru = np.zeros((16, 1, 2, D), np.float32)
    for core in range(N_CORES):
        yT = results[core]["yT"].reshape(128, NCH, NT)
        y = yT.transpose(2, 1, 0).reshape(NT, D)
        if core < 2:
            ys[core] = y
        else:
            ls = results[core]["lru_st"].reshape(128, NSL, 2, NCH)
            gs = results[core]["gla_st"].reshape(NSL, 2, 4, 128, SL)
            s5 = results[core]["s5_st"].reshape(2, 64, NSL, 2, 2, 32)
            for s, b in enumerate(PROMPT_ASSIGN[core - 2]):
                yp[b] = y[SL * s:SL * (s + 1)]
                st_lru[b, 0] = ls[:, s].transpose(1, 2, 0).reshape(2, D)
                st_gla[b, 0] = gs[s]
                st_re[b, 0] = s5[:, :, s, :, 0, :].transpose(2, 3, 0, 1).reshape(2, 64, 64)
                st_im[b, 0] = s5[:, :, s, :, 1, :].transpose(2, 3, 0, 1).reshape(2, 64, 64)
    return yp, ys, st_re, st_im, st_gla, st_lru


def kernel(**inputs):
    inp = {k: np.asarray(v) for k, v in inputs.items()}
    shared = prep_shared(inp)
    in_maps = []
    for core in range(N_CORES):
        m = dict(shared)
        m.update(prep_core(inp, core))
        in_maps.append(m)
    nc = build_program()
    res = run_bass_kernel_spmd(nc, in_maps, core_ids=list(range(N_CORES)))
    return assemble(inp, res.results)
```

```python
import contextlib
import math
import numpy as np
import concourse.bass as bass
import concourse.mybir as mybir
from concourse.bass_utils import run_bass_kernel_spmd

F32 = mybir.dt.float32
BF16 = mybir.dt.bfloat16
I32 = mybir.dt.int32
ACT = mybir.ActivationFunctionType
ALU = mybir.AluOpType

D = 2048
NCH = 16
NT = 1024
TB = 512
NTB = 2
SL = 256
NSL = 4
DFF = 5632
NSLOT = 10
TSZ = 2048
EPS = 1e-6
N_CORES = 8
ARENA_F32 = 13056
GELU_C = 1.5957691216057308


class Tok:
    __slots__ = ("eng", "val")

    def __init__(self, eng, val):
        self.eng = eng
        self.val = val


class Prog:
    def __init__(self, nc, es):
        self.nc = nc
        self.es = es
        self.engs = {"pe": nc.tensor, "act": nc.scalar, "dve": nc.vector, "pool": nc.gpsimd, "sp": nc.sync}
        self.ms = {k: es.enter_context(nc.semaphore("ms_" + k)) for k in ("pe", "act", "dve")}
        self.msn = {k: 0 for k in self.ms}
        self.waited = {}
        self.dsems = {}
        self.ps = es.enter_context(nc.psum_tensor("ps", [128, 8, 512], F32))
        self.bank_free = [None] * 8
        self.bank_rr = 0
        self.ring = es.enter_context(nc.sbuf_tensor("ring", [128, NSLOT, TSZ], BF16))
        self.slot_ld = [es.enter_context(nc.semaphore("ld%d" % s)) for s in range(NSLOT)]
        self.slot_ldn = [0] * NSLOT
        self.slot_free = [None] * NSLOT
        self.slot_rr = 0
        self.slot_open = [False] * NSLOT
        self.last_pe = None
        self.scr = es.enter_context(nc.sbuf_tensor("scr", [128, 4], F32))

    def wait(self, eng, sem, val):
        key = (eng, id(sem))
        if self.waited.get(key, 0) >= val:
            return
        self.waited[key] = val
        self.engs[eng].wait_ge(sem, val)

    def wait_tok(self, eng, tok):
        if tok is None or tok.eng == eng:
            return
        self.wait(eng, self.ms[tok.eng], tok.val)

    def fence(self, eng, tok):
        self.wait(eng, self.ms[tok.eng], tok.val)

    def mark(self, eng, ins):
        ins.then_inc(self.ms[eng], 1)
        self.msn[eng] += 1
        return Tok(eng, self.msn[eng])

    def dma(self, eng, out, in_, name):
        if name not in self.dsems:
            self.dsems[name] = [self.es.enter_context(self.nc.semaphore("d_" + name)), 0]
        s = self.dsems[name]
        self.engs[eng].dma_start(out=out, in_=in_).then_inc(s[0], 16)
        s[1] += 16
        return (s[0], s[1])

    def wait_dma(self, eng, d):
        self.wait(eng, d[0], d[1])

    def barrier(self):
        nc = self.nc
        ta = self.mark("act", nc.scalar.activation(out=self.scr[:, 0:1], in_=self.scr[:, 0:1], func=ACT.Copy))
        td = self.mark("dve", nc.vector.memset(self.scr[:, 1:2], 0.0))
        for e in ("pe", "act", "dve"):
            self.wait_tok(e, ta)
            self.wait_tok(e, td)
            self.wait_tok(e, self.last_pe)

    def sp_sync(self):
        self.barrier()
        self.wait("sp", self.ms["act"], self.msn["act"])
        self.wait("sp", self.ms["dve"], self.msn["dve"])
        self.wait_tok("sp", self.last_pe)

    def pool_sync(self):
        self.barrier()
        self.wait("pool", self.ms["act"], self.msn["act"])
        self.wait("pool", self.ms["dve"], self.msn["dve"])
        self.wait_tok("pool", self.last_pe)

    def bank(self):
        b = self.bank_rr
        self.bank_rr = (b + 1) % 8
        return b

    def bank_pair(self):
        if self.bank_rr % 2:
            self.bank_rr = (self.bank_rr + 1) % 8
        b = self.bank_rr
        self.bank_rr = (b + 2) % 8
        return b, b + 1

    def load_tile(self, src_ap, n=TSZ):
        s = self.slot_rr
        self.slot_rr = (s + 1) % NSLOT
        assert not self.slot_open[s], "ring slot %d still open" % s
        self.slot_open[s] = True
        self.wait_tok("pool", self.slot_free[s])
        self.nc.gpsimd.dma_start(out=self.ring[:, s, 0:n], in_=src_ap).then_inc(self.slot_ld[s], 16)
        self.slot_ldn[s] += 16
        return (s, self.slot_ldn[s])

    def release(self, t):
        self.slot_open[t[0]] = False

    def tile_ap(self, t):
        return self.ring[:, t[0], :]

    def transpose(self, out_ap, in_ap, ident, banks, waits=()):
        for b in banks:
            self.wait_tok("pe", self.bank_free[b])
        for w in waits:
            self.wait_tok("pe", w)
        ins = self.nc.tensor.transpose(out=out_ap, in_=in_ap, identity=ident)
        tok = self.mark("pe", ins)
        self.last_pe = tok
        return tok

    def group2(self, items, banks, tiles=(), waits=()):
        for b in banks:
            self.wait_tok("pe", self.bank_free[b])
        for t in tiles:
            self.wait("pe", self.slot_ld[t[0]], t[1])
        for w in waits:
            self.wait_tok("pe", w)
        n = len(items)
        ins = None
        for i, (o, l, r) in enumerate(items):
            ins = self.nc.tensor.matmul(o, l, r, start=(i == 0), stop=(i == n - 1))
        tok = self.mark("pe", ins)
        self.last_pe = tok
        for t in tiles:
            self.slot_free[t[0]] = tok
        return tok

    def group(self, out_ap, pairs, banks, tiles=(), waits=(), mark=True):
        for b in banks:
            self.wait_tok("pe", self.bank_free[b])
        for t in tiles:
            self.wait("pe", self.slot_ld[t[0]], t[1])
        for w in waits:
            self.wait_tok("pe", w)
        n = len(pairs)
        ins = None
        for i, (l, r) in enumerate(pairs):
            ins = self.nc.tensor.matmul(out_ap, l, r, start=(i == 0), stop=(i == n - 1))
        if not mark:
            return None
        tok = self.mark("pe", ins)
        self.last_pe = tok
        for t in tiles:
            self.slot_free[t[0]] = tok
        return tok


class Arena:
    def __init__(self, ap):
        self.ap = ap
        self.off = 0

    def reset(self):
        self.off = 0

    def f32(self, *shape):
        n = int(np.prod(shape))
        v = self.ap[:, self.off:self.off + n]
        self.off += n
        assert self.off <= ARENA_F32, ("arena overflow", self.off)
        if len(shape) == 2:
            return v.rearrange("p (a b) -> p a b", a=shape[0])
        if len(shape) == 3:
            return v.rearrange("p (a b c) -> p a b c", a=shape[0], b=shape[1])
        return v

    def bf16(self, *shape):
        n = int(np.prod(shape))
        assert n % 2 == 0
        v = self.ap[:, self.off:self.off + n // 2].bitcast(BF16)
        self.off += n // 2
        assert self.off <= ARENA_F32, ("arena overflow", self.off)
        if len(shape) == 2:
            return v.rearrange("p (a b) -> p a b", a=shape[0])
        if len(shape) == 3:
            return v.rearrange("p (a b c) -> p a b c", a=shape[0], b=shape[1])
        return v


def build_program(mix_even=True, mix_odd=True, debug=False, mix_gla=None, mix_s5=None):
    if mix_gla is None:
        mix_gla = mix_even
    if mix_s5 is None:
        mix_s5 = mix_even
    nc = bass.Bass("TRN2", target_bir_lowering=False)
    es = contextlib.ExitStack()
    dbg_list = []

    def din(name, shape):
        return nc.dram_tensor(name, list(shape), F32, kind="ExternalInput").ap()

    def dout(name, shape):
        return nc.dram_tensor(name, list(shape), F32, kind="ExternalOutput").ap()

    xT_in = din("xT", [128, NCH * NT])
    cond_in = din("cond", [128, NCH])
    vecs_in = din("vecs", [128, NVEC])
    cst_in = din("consts", [128, NCST])
    pc_in = din("pcore", [128, NPC])
    w_ada = din("w_ada", [2 * 144, 128, TSZ])
    w_in = din("w_in", [4 * 88, 128, TSZ])
    w_out = din("w_out", [4 * 44, 128, TSZ])
    w_odin = din("w_odin", [32, 128, TSZ])
    w_odout = din("w_odout", [16, 128, TSZ])
    w_lru = din("w_lru", [8, 128, TSZ])
    w_evin = din("w_evin", [33, 128, TSZ])
    w_evout = din("w_evout", [16, 128, TSZ])
    gw2_in = din("gw2", [16, 1024])
    gs0_in = din("gs0", [4, 128, 2 * SL])
    w_glu = din("w_glu", [8, 128, 1024])
    s5lb_in = din("s5lb", [128, 192])
    s5cb_in = din("s5cb", [128, 2048])
    s5la_in = din("s5la", [128, 3072])
    s5ba_in = din("s5ba", [128, 2048])
    s5h0_in = din("s5h0", [128, 128])
    yT_out = dout("yT", [128, NCH * NT])
    lru_out = dout("lru_st", [128, NSL * 2 * NCH])
    gla_out = dout("gla_st", [NSL * 2 * 4, 128, SL])
    s5_out = dout("s5_st", [128, NSL * 2 * 64])
    dbg_out = dout("dbg", [8, 128, 2048]) if debug else None

    with es:
        P = Prog(nc, es)
        sb = lambda name, shape, dt=F32: es.enter_context(nc.sbuf_tensor(name, list(shape), dt))
        XT = sb("XT", [128, NCH, NT])
        HM = sb("HM", [128, NCH, NT], BF16)
        ARN = sb("ARN", [128, ARENA_F32])
        AR = Arena(ARN[:])
        VEC = sb("VEC", [128, NVEC])
        CST = sb("CST", [128, NCST])
        PC = sb("PC", [128, NPC])
        MOD = sb("MOD", [128, 2, 144])
        MV = sb("MV", [128, 2, 9, 16])
        CA = sb("CA", [128, NCH], BF16)
        CAf = sb("CAf", [128, NCH])
        ONES = sb("ONES", [128, 128], BF16)
        EPSB = sb("EPSB", [128, 1])
        LRUST = sb("LRUST", [128, NSL, 2, NCH])
        NSP = sb("NSP", [128, 2, 2, NCH])

        state = {"xt_w": None, "hm_r": None}

        def vcol(name, j=0, w=16):
            o = VOFF[name] + j * w
            return VEC[:, o:o + w]

        def ccol(name, w):
            o = COFF[name]
            return CST[:, o:o + w]

        def pcol(name, w):
            o = POFF[name]
            return PC[:, o:o + w]

        def dump(ap2d):
            if not debug:
                return
            i = len(dbg_list)
            dbg_list.append(i)
            P.barrier()
            P.wait("sp", P.ms["dve"], P.msn["dve"])
            P.wait("sp", P.ms["act"], P.msn["act"])
            P.wait_tok("sp", P.last_pe)
            d = P.dma("sp", dbg_out[i, :, 0:ap2d.shape[1]], ap2d, "dbg")
            for e in ("pe", "act", "dve"):
                P.wait_dma(e, d)

        ld = []
        for q in range(4):
            ld.append(P.dma("sp", XT[:, 4 * q:4 * q + 4, :], xT_in[:, 4 * q * NT:(4 * q + 4) * NT].rearrange("p (c t) -> p c t", c=4), "inx%d" % q))
        ldv = P.dma("sp", VEC[:], vecs_in[:, :], "inv")
        ldk = P.dma("sp", CST[:], cst_in[:, :], "ink")
        ldp = P.dma("sp", PC[:], pc_in[:, :], "inp")
        ldc = P.dma("sp", CAf[:], cond_in[:, :], "inc")
        for e in ("act", "dve", "pool", "pe"):
            for dd in (ldc, ldv, ldk, ldp):
                P.wait_dma(e, dd)
        ones_ready = es.enter_context(nc.semaphore("ones"))
        nc.gpsimd.memset(EPSB[:], EPS)
        nc.gpsimd.memset(P.scr[:], 0.0)
        nc.gpsimd.memset(ONES[:], 1.0).then_inc(ones_ready, 1)
        for e in ("pe", "act", "dve"):
            P.wait(e, ones_ready, 1)
        ins = nc.scalar.activation(out=CA[:], in_=CAf[:], func=ACT.Silu)
        t_ca = P.mark("act", ins)

        def ada(l):
            b = P.bank()
            tok = None
            for i in range(144):
                t = P.load_tile(w_ada[l * 144 + i])
                ap = P.tile_ap(t).rearrange("p (k n) -> p k n", k=16)
                tok = P.group(P.ps[:, b, i:i + 1], [(ap[:, kc, :], CA[:, kc:kc + 1]) for kc in range(16)],
                              [b] if i == 0 else [], tiles=[t], waits=[t_ca])
                P.release(t)
            P.wait_tok("dve", tok)
            ab = VOFF["ada_b"] + l * 144
            nc.vector.tensor_tensor(out=MOD[:, l, :], in0=P.ps[:, b, 0:144], in1=VEC[:, ab:ab + 144], op=ALU.add)
            for s in range(3):
                sh = MOD[:, l, (3 * s) * 16:(3 * s) * 16 + 16]
                sc = MOD[:, l, (3 * s + 1) * 16:(3 * s + 1) * 16 + 16]
                g = MOD[:, l, (3 * s + 2) * 16:(3 * s + 2) * 16 + 16]
                ng = vcol("norm_g", l * 3 + s)
                nc.vector.scalar_tensor_tensor(out=MV[:, l, 3 * s, :], in0=sc, scalar=1.0, in1=ng, op0=ALU.add, op1=ALU.mult)
                nc.vector.tensor_copy(out=MV[:, l, 3 * s + 1, :], in_=sh)
                ins = nc.vector.tensor_scalar(out=MV[:, l, 3 * s + 2, :], in0=g, scalar1=(1.0 if s == 1 else 0.5), scalar2=None, op0=ALU.mult)
            tk = P.mark("dve", ins)
            P.bank_free[b] = tk
            return tk

        def norm(A, sh, mv_tok, final=False):
            AR.reset()
            RS = AR.f32(TB)
            TMP = AR.f32(TB)
            P.barrier()
            prev = None
            for tb in range(NTB):
                ts = slice(tb * TB, (tb + 1) * TB)
                P.wait_tok("act", prev)
                for q in range(4):
                    P.wait_dma("act", ld[q])
                    P.wait_dma("dve", ld[q])
                for c in range(NCH):
                    ins = nc.scalar.activation(out=HM[:, c, ts], in_=XT[:, c, ts], func=ACT.Square)
                tS = P.mark("act", ins)
                b = P.bank()
                tP = P.group(P.ps[:, b, :], [(ONES[:, :], HM[:, c, ts]) for c in range(NCH)], [b], waits=[tS])
                P.wait_tok("act", tP)
                ins = nc.scalar.activation(out=RS, in_=P.ps[:, b, :], func=ACT.Sqrt, scale=1.0 / D, bias=EPSB[:, 0:1])
                tR = P.mark("act", ins)
                P.bank_free[b] = tR
                P.wait_tok("dve", tR)
                P.wait_tok("dve", mv_tok)
                nc.vector.reciprocal(out=RS, in_=RS)
                for c in range(NCH):
                    if final:
                        ins = nc.vector.scalar_tensor_tensor(out=XT[:, c, ts], in0=XT[:, c, ts], scalar=A[:, c:c + 1], in1=RS, op0=ALU.mult, op1=ALU.mult)
                    else:
                        nc.vector.scalar_tensor_tensor(out=TMP, in0=XT[:, c, ts], scalar=A[:, c:c + 1], in1=RS, op0=ALU.mult, op1=ALU.mult)
                        ins = nc.vector.tensor_scalar(out=HM[:, c, ts], in0=TMP, scalar1=sh[:, c:c + 1], scalar2=None, op0=ALU.add)
                prev = P.mark("dve", ins)
            P.barrier()
            return prev

        def ffn(f, G, hm_tok):
            AR.reset()
            PN = 8
            H = AR.bf16(PN, NT)
            lastB = None
            for p0 in range(0, 44, PN):
                npan = min(PN, 44 - p0)
                tokD = None
                for j in range(npan):
                    hc = p0 + j
                    ta = P.load_tile(w_in[f * 88 + 2 * hc])
                    tb_ = P.load_tile(w_in[f * 88 + 2 * hc + 1])
                    apa = P.tile_ap(ta).rearrange("p (k n) -> p k n", k=16)
                    apb = P.tile_ap(tb_).rearrange("p (k n) -> p k n", k=16)
                    for tb in range(NTB):
                        ts = slice(tb * TB, (tb + 1) * TB)
                        ba, bb = P.bank_pair()
                        P.group(P.ps[:, ba, :], [(apa[:, kc, :], HM[:, kc, ts]) for kc in range(16)], [ba], tiles=[ta], waits=[hm_tok])
                        tpb = P.group(P.ps[:, bb, :], [(apb[:, kc, :], HM[:, kc, ts]) for kc in range(16)], [bb], tiles=[tb_])
                        P.wait_tok("act", tpb)
                        P.wait_tok("act", lastB)
                        ins = nc.scalar.activation(out=H[:, j, ts], in_=P.ps[:, ba, :], func=ACT.Silu)
                        tA = P.mark("act", ins)
                        P.wait_tok("dve", tA)
                        ins = nc.vector.tensor_tensor(out=H[:, j, ts], in0=H[:, j, ts], in1=P.ps[:, bb, :], op=ALU.mult)
                        tokD = P.mark("dve", ins)
                        P.bank_free[ba] = tokD
                        P.bank_free[bb] = tokD
                    P.release(ta)
                    P.release(tb_)
                tiles = [P.load_tile(w_out[f * 44 + p0 + j]) for j in range(npan)]
                aps = [P.tile_ap(t) for t in tiles]
                for oc in range(NCH):
                    for tb in range(NTB):
                        ts = slice(tb * TB, (tb + 1) * TB)
                        b = P.bank()
                        pairs = [(aps[j][:, oc * 128:(oc + 1) * 128], H[:, j, ts]) for j in range(npan)]
                        lastB = P.group(P.ps[:, b, :], pairs, [b], tiles=tiles, waits=[tokD])
                        P.wait_tok("dve", lastB)
                        ins = nc.vector.scalar_tensor_tensor(out=XT[:, oc, ts], in0=P.ps[:, b, :], scalar=G[:, oc:oc + 1], in1=XT[:, oc, ts], op0=ALU.mult, op1=ALU.add)
                        P.bank_free[b] = P.mark("dve", ins)
                for t in tiles:
                    P.release(t)

        def lru_prep():
            lam = VEC[:, VOFF["lru_lam"]:VOFF["lru_lam"] + 32]
            t = NSP[:, 0, :, :].rearrange("p d c -> p (d c)")
            t2 = NSP[:, 1, :, :].rearrange("p d c -> p (d c)")
            P.wait_dma("act", ldv)
            nc.scalar.activation(out=t, in_=lam, func=ACT.Exp, scale=-1.0)
            ins = nc.scalar.activation(out=t, in_=t, func=ACT.Ln, bias=1.0)
            tk = P.mark("act", ins)
            P.wait_tok("dve", tk)
            nc.vector.tensor_scalar(out=t2, in0=t, scalar1=-16.0, scalar2=None, op0=ALU.mult)
            nc.vector.tensor_scalar(out=t, in0=t, scalar1=-8.0, scalar2=None, op0=ALU.mult)

        def odd_mixer(l, G2):
            P.barrier()
            AR.reset()
            XC = AR.f32(2, NT)
            HS = AR.f32(2, NT)
            XCb = AR.bf16(2, NT)
            Y = AR.bf16(2, NT)
            T12 = AR.f32(2, 2 * SL)
            T1 = T12[:, 0, :]
            T2 = T12[:, 1, :]
            T3 = AR.f32(2 * SL)
            T12b = AR.f32(2, 2 * SL)
            T3b = AR.f32(2 * SL)
            HSB = AR.f32(2, NT)
            T4 = AR.f32(2, SL)
            XF = AR.f32(2, SL)
            XFb = AR.bf16(2, SL)
            TT = AR.f32(128)
            HIN = AR.f32(4)
            PEB = AR.f32(2)
            IDENT = ccol("ident", 128)
            JREV = ccol("jrev", 128)
            CVM = pcol("cvm", 3 * SL).rearrange("p (j t) -> p j t", j=3)
            CARRY = pcol("carry", 1)
            LH0 = pcol("lh0", 2 * NCH).rearrange("p (d c) -> p d c", d=2)
            cw = VEC[:, VOFF["conv_w"]:VOFF["conv_w"] + 64].rearrange("p (j c) -> p j c", j=4)
            cb = vcol("conv_b")
            nba = VEC[:, VOFF["lru_ba"]:VOFF["lru_ba"] + 32].rearrange("p (d c) -> p d c", d=2)
            nbx = VEC[:, VOFF["lru_bx"]:VOFF["lru_bx"] + 32].rearrange("p (d c) -> p d c", d=2)
            NB = AR.f32(2, 2, NCH)
            nc.vector.tensor_scalar(out=NB[:, 0, :, :], in0=nba, scalar1=-1.0, scalar2=None, op0=ALU.mult)
            nc.vector.tensor_scalar(out=NB[:, 1, :, :], in0=nbx, scalar1=-1.0, scalar2=None, op0=ALU.mult)
            P.barrier()
            st = {"prev": None}

            def lru_pair(TS, key, apl, tl, d, h, src_f32, src_bf, out_fn, hin_fn, wtok, pre_scan=None):
                T12_, T3_ = TS
                T1_ = T12_[:, 0, :]
                T2_ = T12_[:, 1, :]
                br, bi = P.bank_pair()
                tpi = None
                for oc in range(2):
                    ocs = slice(oc * 128, (oc + 1) * 128)
                    cs = slice(oc * SL, (oc + 1) * SL)
                    P.group(P.ps[:, br, cs], [(apl[:, 0, d, kc, ocs], src_bf(kc)) for kc in range(2)], [br] if oc == 0 else [], tiles=[tl], waits=[wtok])
                    tpi = P.group(P.ps[:, bi, cs], [(apl[:, 1, d, kc, ocs], src_bf(kc)) for kc in range(2)], [bi] if oc == 0 else [], tiles=[tl])
                yield
                P.wait_tok("act", tpi)
                P.wait_tok("act", st.get(key))
                for oc in range(2):
                    ch = 2 * h + oc
                    cs = slice(oc * SL, (oc + 1) * SL)
                    nc.scalar.activation(out=T1_[:, cs], in_=P.ps[:, br, cs], func=ACT.Exp, scale=-1.0, bias=NB[:, 0, d, ch:ch + 1])
                    ins = nc.scalar.activation(out=T2_[:, cs], in_=P.ps[:, bi, cs], func=ACT.Exp, scale=-1.0, bias=NB[:, 1, d, ch:ch + 1])
                tA = P.mark("act", ins)
                P.bank_free[br] = tA
                P.bank_free[bi] = tA
                yield
                P.wait_tok("dve", tA)
                T12f = T12_[:].rearrange("p a b -> p (a b)")
                nc.vector.tensor_scalar(out=T12f, in0=T12f, scalar1=1.0, scalar2=None, op0=ALU.add)
                tD = P.mark("dve", nc.vector.reciprocal(out=T12f, in_=T12f))
                yield
                P.wait_tok("act", tD)
                for oc in range(2):
                    ch = 2 * h + oc
                    cs = slice(oc * SL, (oc + 1) * SL)
                    nc.scalar.activation(out=T3_[:, cs], in_=T1_[:, cs], func=ACT.Exp, scale=NSP[:, 1, d, ch:ch + 1])
                    ins = nc.scalar.activation(out=T1_[:, cs], in_=T1_[:, cs], func=ACT.Exp, scale=NSP[:, 0, d, ch:ch + 1])
                tA2 = P.mark("act", ins)
                yield
                P.wait_tok("dve", tA2)
                nc.vector.tensor_scalar(out=T3_, in0=T3_, scalar1=-1.0, scalar2=1.0, op0=ALU.mult, op1=ALU.add)
                nc.vector.tensor_scalar(out=T3_, in0=T3_, scalar1=1e-30, scalar2=None, op0=ALU.max)
                T2v = T2_.rearrange("p (a b) -> p a b", a=2)
                tD2 = P.mark("dve", nc.vector.tensor_tensor(out=T2v, in0=T2v, in1=src_f32, op=ALU.mult))
                yield
                P.wait_tok("act", tD2)
                nc.scalar.activation(out=T3_, in_=T3_, func=ACT.Ln)
                tA3 = P.mark("act", nc.scalar.activation(out=T3_, in_=T3_, func=ACT.Exp, scale=0.5))
                yield
                P.wait_tok("dve", tA3)
                P.wait_tok("dve", pre_scan)
                nc.vector.tensor_tensor(out=T2_, in0=T2_, in1=T3_, op=ALU.mult)
                ins = None
                for oc in range(2):
                    cs = slice(oc * SL, (oc + 1) * SL)
                    ins = nc.vector.tensor_tensor_scan(out=out_fn(oc), data0=T1_[:, cs], data1=T2_[:, cs], initial=hin_fn(oc), op0=ALU.mult, op1=ALU.add)
                st[key] = P.mark("dve", ins)
                P.fence("dve", st[key])

            def flip_block(src128, waits):
                b = P.bank()
                tp = P.transpose(P.ps[:, b, 0:128], src128, IDENT, [b], waits=waits)
                P.wait_tok("act", tp)
                P.wait_tok("act", st.get("tt"))
                ins = nc.scalar.activation(out=TT, in_=P.ps[:, b, 0:128], func=ACT.Copy)
                ta = P.mark("act", ins)
                P.bank_free[b] = ta
                b2 = P.bank()
                tp2 = P.group(P.ps[:, b2, 0:128], [(TT, JREV)], [b2], waits=[ta])
                st["tt"] = tp2
                return b2, tp2

            for h in range(8):
                tx = [P.load_tile(w_odin[16 + 2 * h + cc]) for cc in range(2)]
                tl = P.load_tile(w_lru[h])
                tg = [P.load_tile(w_odin[2 * h + cc]) for cc in range(2)]
                to = [P.load_tile(w_odout[2 * h])]
                apx = [P.tile_ap(t).rearrange("p (k n) -> p k n", k=16) for t in tx]
                apg = [P.tile_ap(t).rearrange("p (k n) -> p k n", k=16) for t in tg]
                apl = P.tile_ap(tl).rearrange("p (w d k n) -> p w d k n", w=2, d=2, k=2)
                xc_tok = None
                for s in range(NSL):
                    ts = slice(s * SL, (s + 1) * SL)
                    for cc in range(2):
                        ch = 2 * h + cc
                        b = P.bank()
                        tp = P.group(P.ps[:, b, 0:SL], [(apx[cc][:, kc, :], HM[:, kc, ts]) for kc in range(16)], [b], tiles=[tx[cc]])
                        P.wait_tok("dve", tp)
                        xs = P.ps[:, b, 0:SL]
                        xc = XC[:, cc, ts]
                        nc.vector.tensor_scalar(out=xc, in0=xs, scalar1=cw[:, 2, ch:ch + 1], scalar2=cb[:, ch:ch + 1], op0=ALU.mult, op1=ALU.add)
                        nc.vector.tensor_tensor(out=T1[:, 2:SL], in0=xs[:, 0:SL - 2], in1=CVM[:, 0, 2:SL], op=ALU.mult)
                        nc.vector.scalar_tensor_tensor(out=xc[:, 2:SL], in0=T1[:, 2:SL], scalar=cw[:, 0, ch:ch + 1], in1=xc[:, 2:SL], op0=ALU.mult, op1=ALU.add)
                        nc.vector.tensor_tensor(out=T1[:, 1:SL], in0=xs[:, 0:SL - 1], in1=CVM[:, 1, 1:SL], op=ALU.mult)
                        nc.vector.scalar_tensor_tensor(out=xc[:, 1:SL], in0=T1[:, 1:SL], scalar=cw[:, 1, ch:ch + 1], in1=xc[:, 1:SL], op0=ALU.mult, op1=ALU.add)
                        nc.vector.tensor_tensor(out=T1[:, 0:SL - 1], in0=xs[:, 1:SL], in1=CVM[:, 2, 0:SL - 1], op=ALU.mult)
                        ins = nc.vector.scalar_tensor_tensor(out=xc[:, 0:SL - 1], in0=T1[:, 0:SL - 1], scalar=cw[:, 3, ch:ch + 1], in1=xc[:, 0:SL - 1], op0=ALU.mult, op1=ALU.add)
                        P.bank_free[b] = P.mark("dve", ins)
                        ins = nc.vector.tensor_copy(out=XCb[:, cc, ts], in_=xc)
                        xc_tok = P.mark("dve", ins)
                P.release(tx[0]); P.release(tx[1])
                if h == 0:
                    dump(XC[:].rearrange("p a t -> p (a t)"))
                to.append(P.load_tile(w_odout[2 * h + 1]))
                apo = [P.tile_ap(t) for t in to]
                P.barrier()
                def fwd_task():
                    for s in range(NSL):
                        ts = slice(s * SL, (s + 1) * SL)
                        for oc in range(2):
                            ch = 2 * h + oc
                            if s == 0:
                                nc.vector.tensor_copy(out=HIN[:, oc:oc + 1], in_=LH0[:, 0, ch:ch + 1])
                            else:
                                nc.vector.tensor_scalar(out=HIN[:, oc:oc + 1], in0=HS[:, oc, s * SL - 1:s * SL], scalar1=CARRY, scalar2=None, op0=ALU.mult)
                        yield from lru_pair((T12, T3), "pf", apl, tl, 0, h, XC[:, :, ts], lambda kc: XCb[:, kc, ts], lambda oc: HS[:, oc, ts], lambda oc: HIN[:, oc:oc + 1], xc_tok)
                        for oc in range(2):
                            ch = 2 * h + oc
                            nc.vector.tensor_copy(out=LRUST[:, s, 0, ch:ch + 1], in_=HS[:, oc, (s + 1) * SL - 1:(s + 1) * SL])
                        yield

                def bwd_task():
                    unflip_tok = None
                    for s in range(NSL - 1, -1, -1):
                        ftok = None
                        for cc in range(2):
                            for hb in range(2):
                                b2, tp2 = flip_block(XC[:, cc, s * SL + hb * 128:s * SL + (hb + 1) * 128], [xc_tok])
                                yield
                                P.wait_tok("act", tp2)
                                P.wait_tok("act", st.get("pb"))
                                dst = slice((1 - hb) * 128, (2 - hb) * 128)
                                nc.scalar.activation(out=XF[:, cc, dst], in_=P.ps[:, b2, 0:128], func=ACT.Copy)
                                ins = nc.scalar.activation(out=XFb[:, cc, dst], in_=P.ps[:, b2, 0:128], func=ACT.Copy)
                                ftok = P.mark("act", ins)
                                P.bank_free[b2] = ftok
                        yield
                        P.wait_tok("dve", ftok)
                        for oc in range(2):
                            ch = 2 * h + oc
                            if s == NSL - 1:
                                nc.vector.tensor_copy(out=HIN[:, 2 + oc:3 + oc], in_=LH0[:, 1, ch:ch + 1])
                            else:
                                nc.vector.tensor_scalar(out=HIN[:, 2 + oc:3 + oc], in0=PEB[:, oc:oc + 1], scalar1=CARRY, scalar2=None, op0=ALU.mult)
                        yield from lru_pair((T12b, T3b), "pb", apl, tl, 1, h, XF[:, :, :], lambda kc: XFb[:, kc, :], lambda oc: T4[:, oc, :], lambda oc: HIN[:, 2 + oc:3 + oc], ftok, pre_scan=unflip_tok)
                        tk = None
                        for oc in range(2):
                            ch = 2 * h + oc
                            nc.vector.tensor_copy(out=PEB[:, oc:oc + 1], in_=T4[:, oc, SL - 1:SL])
                            tk = P.mark("dve", nc.vector.tensor_copy(out=LRUST[:, s, 1, ch:ch + 1], in_=T4[:, oc, SL - 1:SL]))
                        for oc in range(2):
                            for hb in range(2):
                                b2, tp2 = flip_block(T4[:, oc, hb * 128:(hb + 1) * 128], [tk])
                                unflip_tok = tp2
                                yield
                                P.wait_tok("act", tp2)
                                dst = slice(s * SL + (1 - hb) * 128, s * SL + (2 - hb) * 128)
                                P.bank_free[b2] = P.mark("act", nc.scalar.activation(out=HSB[:, oc, dst], in_=P.ps[:, b2, 0:128], func=ACT.Copy))
                        yield

                st["pf"] = None
                st["pb"] = None
                tasks = [fwd_task(), bwd_task()]
                while tasks:
                    for tsk in list(tasks):
                        try:
                            next(tsk)
                        except StopIteration:
                            tasks.remove(tsk)
                P.release(tl)
                P.barrier()
                if h == 0:
                    dump(HS[:].rearrange("p a t -> p (a t)"))
                ytok = None
                T1g = T1[:, 0:SL]
                for s in range(NSL):
                    ts = slice(s * SL, (s + 1) * SL)
                    for cc in range(2):
                        b = P.bank()
                        tp = P.group(P.ps[:, b, 0:SL], [(apg[cc][:, kc, :], HM[:, kc, ts]) for kc in range(16)], [b], tiles=[tg[cc]])
                        gp = P.ps[:, b, 0:SL]
                        P.wait_tok("act", tp)
                        P.wait_tok("act", ytok)
                        ins = nc.scalar.activation(out=T1g, in_=gp, func=ACT.Square)
                        tA = P.mark("act", ins)
                        P.wait_tok("dve", tA)
                        nc.vector.tensor_scalar(out=T1g, in0=T1g, scalar1=0.044715, scalar2=1.0, op0=ALU.mult, op1=ALU.add)
                        ins = nc.vector.tensor_tensor(out=T1g, in0=T1g, in1=gp, op=ALU.mult)
                        tD = P.mark("dve", ins)
                        P.wait_tok("act", tD)
                        ins = nc.scalar.activation(out=T1g, in_=T1g, func=ACT.Exp, scale=-GELU_C)
                        tA2 = P.mark("act", ins)
                        P.wait_tok("dve", tA2)
                        nc.vector.tensor_scalar(out=T1g, in0=T1g, scalar1=1.0, scalar2=None, op0=ALU.add)
                        nc.vector.reciprocal(out=T1g, in_=T1g)
                        nc.vector.tensor_tensor(out=T1g, in0=T1g, in1=gp, op=ALU.mult)
                        nc.vector.tensor_tensor(out=HS[:, cc, ts], in0=HS[:, cc, ts], in1=HSB[:, cc, ts], op=ALU.add)
                        ins = nc.vector.tensor_tensor(out=Y[:, cc, ts], in0=T1g, in1=HS[:, cc, ts], op=ALU.mult)
                        ytok = P.mark("dve", ins)
                        P.bank_free[b] = ytok
                P.release(tg[0]); P.release(tg[1])
                for oc in range(NCH):
                    for tb in range(NTB):
                        ts = slice(tb * TB, (tb + 1) * TB)
                        b = P.bank()
                        tp = P.group(P.ps[:, b, :], [(apo[kc][:, oc * 128:(oc + 1) * 128], Y[:, kc, ts]) for kc in range(2)], [b], tiles=to, waits=[ytok])
                        P.wait_tok("dve", tp)
                        ins = nc.vector.scalar_tensor_tensor(out=XT[:, oc, ts], in0=P.ps[:, b, :], scalar=G2[:, oc:oc + 1], in1=XT[:, oc, ts], op0=ALU.mult, op1=ALU.add)
                        P.bank_free[b] = P.mark("dve", ins)
                P.release(to[0]); P.release(to[1])
                P.barrier()

        def gla_part(G2):
            P.barrier()
            AR.reset()
            IDENT = ccol("ident", 128)
            MASK = [ccol("maskf", 128), ccol("maskb", 128)]
            RM = ccol("rm", SL)
            CARRY = pcol("carry", 1)
            gng = vcol("gla_ng", 0, 2)
            gb = vcol("gla_gb", 0, 8)
            GL = AR.bf16(2, NT)
            GW2 = AR.bf16(2, 512)
            NGB = AR.f32(8)
            OS = AR.f32(2, NT)
            GS0 = AR.f32(2, SL)
            mark0 = AR.off
            QT = AR.f32(SL); KT = AR.f32(SL); VT = AR.bf16(2, SL)
            LA = AR.f32(SL); PP = AR.f32(SL); TP = AR.f32(SL); PB = AR.f32(SL)
            E1 = AR.f32(SL); E2 = AR.f32(SL); E3 = AR.f32(SL)
            QTd = AR.bf16(SL); KTd = AR.bf16(SL); KEd = AR.f32(SL)
            AT = AR.bf16(2, 128); KET = AR.bf16(2, 128)
            S = AR.f32(SL); Sb = AR.bf16(4, SL); SST = AR.f32(SL); DEC = AR.f32(4)
            end1 = AR.off
            AR.off = mark0
            GG = AR.bf16(2, NT); SQ = AR.bf16(2, TB); RR = AR.f32(TB); T5 = AR.f32(TB); EG = AR.f32(TB)
            assert AR.off <= ARENA_F32 and end1 <= ARENA_F32
            P.pool_sync()
            dgw = P.dma("pool", GW2[0:16, :, :].rearrange("p a b -> p (a b)"), gw2_in[:, :], "gw2")
            P.wait_dma("pe", dgw)
            nc.vector.tensor_scalar(out=NGB, in0=gb, scalar1=-1.0, scalar2=None, op0=ALU.mult)
            tglr = P.load_tile(w_evin[32])
            apglr = P.tile_ap(tglr).rearrange("p (k n) -> p k n", k=16)
            gl_tok = None
            for d in range(2):
                for tb in range(NTB):
                    ts = slice(tb * TB, (tb + 1) * TB)
                    b = P.bank()
                    tp = P.group(P.ps[0:16, b, :], [(apglr[:, kc, d * 16:(d + 1) * 16], HM[:, kc, ts]) for kc in range(16)], [b], tiles=[tglr])
                    P.wait_tok("act", tp)
                    ins = nc.scalar.activation(out=GL[0:16, d, ts], in_=P.ps[0:16, b, :], func=ACT.Copy)
                    gl_tok = P.mark("act", ins)
                    P.bank_free[b] = gl_tok
            P.release(tglr)
            sst_dma = None
            import os
            STOP = int(os.environ.get("GLA_STOP", "99"))
            for hd in range(4 if STOP > 1 else 0):
                P.barrier()
                tq = P.load_tile(w_evin[8 + hd])
                tk = P.load_tile(w_evin[12 + hd])
                tv = [P.load_tile(w_evin[16 + 2 * hd + i]) for i in range(2)]
                apq = P.tile_ap(tq).rearrange("p (k n) -> p k n", k=16)
                apk = P.tile_ap(tk).rearrange("p (k n) -> p k n", k=16)
                apv = [P.tile_ap(t).rearrange("p (k n) -> p k n", k=16) for t in tv]
                P.sp_sync()
                dgs = P.dma("sp", GS0[:].rearrange("p a b -> p (a b)"), gs0_in[hd], "gs0")
                for d in range(2 if STOP > 2 else 0):
                    order = list(range(NSL)) if d == 0 else list(range(NSL - 1, -1, -1))
                    for si, s in enumerate(order):
                        ts = slice(s * SL, (s + 1) * SL)
                        P.barrier()
                        b = P.bank()
                        tp = P.group(P.ps[:, b, 0:SL], [(apq[:, kc, :], HM[:, kc, ts]) for kc in range(16)], [b], tiles=[tq])
                        P.wait_tok("act", tp)
                        P.bank_free[b] = P.mark("act", nc.scalar.activation(out=QT, in_=P.ps[:, b, 0:SL], func=ACT.Copy, scale=128.0 ** -0.5))
                        b = P.bank()
                        tp = P.group(P.ps[:, b, 0:SL], [(apk[:, kc, :], HM[:, kc, ts]) for kc in range(16)], [b], tiles=[tk])
                        P.wait_tok("act", tp)
                        P.bank_free[b] = P.mark("act", nc.scalar.activation(out=KT, in_=P.ps[:, b, 0:SL], func=ACT.Copy))
                        vt_tok = None
                        for tt in range(2):
                            b = P.bank()
                            tsl = slice(s * SL + tt * 128, s * SL + (tt + 1) * 128)
                            for vc in range(2):
                                tp = P.group(P.ps[:, b, vc * 128:(vc + 1) * 128], [(HM[:, kc, tsl], apv[vc][:, kc, :]) for kc in range(16)],
                                             [b] if vc == 0 else [], tiles=[tv[vc]])
                            P.wait_tok("act", tp)
                            vt_tok = P.mark("act", nc.scalar.activation(out=VT[:, tt, :], in_=P.ps[:, b, 0:SL], func=ACT.Copy))
                            P.bank_free[b] = vt_tok
                        if STOP <= 3:
                            continue
                        b = P.bank()
                        tp = P.group(P.ps[:, b, 0:SL], [(GW2[0:16, d, hd * 128:(hd + 1) * 128], GL[0:16, d, ts])], [b], waits=[gl_tok])
                        P.wait_tok("act", tp)
                        nc.scalar.activation(out=E1, in_=P.ps[:, b, 0:SL], func=ACT.Exp, scale=-1.0, bias=NGB[:, d * 4 + hd:d * 4 + hd + 1])
                        tA = P.mark("act", nc.scalar.activation(out=LA, in_=E1, func=ACT.Ln, bias=1.0))
                        P.bank_free[b] = tA
                        P.wait_tok("dve", tA)
                        tsc = P.mark("dve", nc.vector.tensor_tensor_scan(out=PP, data0=RM, data1=LA, initial=0.0, op0=ALU.mult, op1=ALU.add))
                        P.fence("dve", tsc)
                        TOT = PP[:, 63::64]
                        nc.vector.tensor_tensor(out=TP.rearrange("p (c t) -> p c t", c=4), in0=PP.rearrange("p (c t) -> p c t", c=4),
                                                in1=TOT.unsqueeze(2).to_broadcast([128, 4, 64]), op=ALU.subtract)
                        if d == 0:
                            ins = nc.vector.tensor_copy(out=PB, in_=PP)
                            sc = (-1.0 / 16, 1.0 / 16, 1.0 / 16)
                        else:
                            nc.vector.scalar_tensor_tensor(out=PB, in0=TP, scalar=-1.0, in1=LA, op0=ALU.mult, op1=ALU.add)
                            ins = nc.vector.tensor_tensor(out=TP, in0=PP, in1=LA, op=ALU.subtract)
                            sc = (-1.0 / 16, 1.0 / 16, -1.0 / 16)
                        tD = P.mark("dve", ins)
                        P.wait_tok("act", tD)
                        nc.scalar.activation(out=E1, in_=PB, func=ACT.Exp, scale=sc[0])
                        nc.scalar.activation(out=E2, in_=PB, func=ACT.Exp, scale=sc[1])
                        nc.scalar.activation(out=E3, in_=TP, func=ACT.Exp, scale=sc[2])
                        tA2 = P.mark("act", nc.scalar.activation(out=DEC, in_=TOT, func=ACT.Exp, scale=-1.0 / 16))
                        P.wait_tok("dve", tA2)
                        nc.vector.tensor_tensor(out=QTd, in0=QT, in1=E1, op=ALU.mult)
                        nc.vector.tensor_tensor(out=KTd, in0=KT, in1=E2, op=ALU.mult)
                        tD2 = P.mark("dve", nc.vector.tensor_tensor(out=KEd, in0=KT, in1=E3, op=ALU.mult))
                        if STOP <= 4:
                            continue
                        ba = P.bank()
                        for pp in range(2):
                            tl_ = slice(pp * 128, (pp + 1) * 128)
                            tp = P.group(P.ps[:, ba, tl_], [(KTd[:, tl_], QTd[:, tl_])], [ba] if pp == 0 else [], waits=[tD2])
                        P.wait_tok("dve", tp)
                        for pp in range(2):
                            ins = nc.vector.tensor_tensor(out=AT[:, pp, :], in0=P.ps[:, ba, pp * 128:(pp + 1) * 128], in1=MASK[d], op=ALU.mult)
                        tAT = P.mark("dve", ins)
                        P.bank_free[ba] = tAT
                        if STOP == 5 and os.environ.get("GLA_SUB") == "a":
                            continue
                        for pp in range(2):
                            bt = P.bank()
                            tp = P.transpose(P.ps[:, bt, 0:128], KEd[:, pp * 128:(pp + 1) * 128], IDENT, [bt], waits=[tD2])
                            P.wait_tok("act", tp)
                            tKET = P.mark("act", nc.scalar.activation(out=KET[:, pp, :], in_=P.ps[:, bt, 0:128], func=ACT.Copy))
                            P.bank_free[bt] = tKET
                        if STOP <= 5:
                            continue
                        bk = [P.bank() for _ in range(4)]
                        for n in range(4):
                            pp, half = n // 2, n % 2
                            rows = slice(half * 64, half * 64 + 64)
                            tp = P.group(P.ps[:, bk[n], 0:256], [(KET[rows, pp, :], VT[rows, pp, :])],
                                         [bk[n]], waits=[tKET, vt_tok])
                        P.wait_tok("dve", tp)
                        P.wait_dma("dve", dgs)
                        if si == 0:
                            nc.vector.tensor_copy(out=S, in_=GS0[:, d, :])
                        else:
                            nc.vector.tensor_scalar(out=S, in0=S, scalar1=CARRY, scalar2=None, op0=ALU.mult)
                        corder = [0, 1, 2, 3] if d == 0 else [3, 2, 1, 0]
                        for n in corder:
                            pp, half = n // 2, n % 2
                            nc.vector.tensor_copy(out=Sb[:, n, :], in_=S)
                            ins = nc.vector.scalar_tensor_tensor(out=S, in0=S, scalar=DEC[:, n:n + 1], in1=P.ps[:, bk[n], 0:256], op0=ALU.mult, op1=ALU.add)
                        tS = P.mark("dve", ins)
                        for n in range(4):
                            P.bank_free[bk[n]] = tS
                        if sst_dma is not None:
                            P.wait_dma("dve", sst_dma)
                        tSS = P.mark("dve", nc.vector.tensor_copy(out=SST, in_=S))
                        P.wait_tok("sp", tSS)
                        sst_dma = P.dma("sp", gla_out[(s * 2 + d) * 4 + hd], SST, "sst")
                        if STOP <= 6:
                            continue
                        bo = P.bank()
                        first = True
                        for pp in range(2):
                            for vc in range(2):
                                col0 = (pp * 2 + vc) * 128
                                vs = slice(vc * 128, (vc + 1) * 128)
                                items = [(P.ps[:, bo, col0:col0 + 128], VT[:, pp, vs], AT[:, pp, :])]
                                for half in range(2):
                                    n = pp * 2 + half
                                    items.append((P.ps[:, bo, col0 + half * 64:col0 + (half + 1) * 64], Sb[:, n, vs], QTd[:, n * 64:(n + 1) * 64]))
                                tp = P.group2(items, [bo] if first else [], waits=[tAT, tS, vt_tok])
                                first = False
                        P.wait_tok("dve", tp)
                        for pp in range(2):
                            for vc in range(2):
                                col0 = (pp * 2 + vc) * 128
                                dst = OS[:, vc, s * SL + pp * 128:s * SL + (pp + 1) * 128]
                                if d == 0:
                                    ins = nc.vector.tensor_copy(out=dst, in_=P.ps[:, bo, col0:col0 + 128])
                                else:
                                    ins = nc.vector.tensor_tensor(out=dst, in0=dst, in1=P.ps[:, bo, col0:col0 + 128], op=ALU.add)
                        P.bank_free[bo] = P.mark("dve", ins)
                for t in (tq, tk, tv[0], tv[1]):
                    P.release(t)
                P.barrier()
                tg = [P.load_tile(w_evin[24 + 2 * hd + i]) for i in range(2)]
                to = [P.load_tile(w_evout[8 + 2 * hd + i]) for i in range(2)]
                apg = [P.tile_ap(t).rearrange("p (k n) -> p k n", k=16) for t in tg]
                apo = [P.tile_ap(t) for t in to]
                ogt = None
                for tb in range(NTB):
                    ts = slice(tb * TB, (tb + 1) * TB)
                    for vc in range(2):
                        b = P.bank()
                        tp = P.group(P.ps[:, b, :], [(apg[vc][:, kc, :], HM[:, kc, ts]) for kc in range(16)], [b], tiles=[tg[vc]])
                        P.wait_tok("act", tp)
                        P.wait_tok("act", ogt)
                        tA = P.mark("act", nc.scalar.activation(out=EG, in_=P.ps[:, b, :], func=ACT.Exp, scale=-1.0))
                        P.wait_tok("dve", tA)
                        nc.vector.tensor_scalar(out=EG, in0=EG, scalar1=1.0, scalar2=None, op0=ALU.add)
                        nc.vector.reciprocal(out=EG, in_=EG)
                        ins = nc.vector.tensor_tensor(out=GG[:, vc, ts], in0=EG, in1=P.ps[:, b, :], op=ALU.mult)
                        ogt = P.mark("dve", ins)
                        P.bank_free[b] = ogt
                    P.wait_tok("act", ogt)
                    for vc in range(2):
                        ins = nc.scalar.activation(out=SQ[:, vc, :], in_=OS[:, vc, ts], func=ACT.Square)
                    tSq = P.mark("act", ins)
                    b = P.bank()
                    tp = P.group(P.ps[:, b, :], [(ONES[:, :], SQ[:, vc, :]) for vc in range(2)], [b], waits=[tSq])
                    P.wait_tok("act", tp)
                    nc.scalar.activation(out=RR, in_=P.ps[:, b, :], func=ACT.Ln, scale=1.0 / 256, bias=EPSB[:, 0:1])
                    tR = P.mark("act", nc.scalar.activation(out=RR, in_=RR, func=ACT.Exp, scale=-0.5))
                    P.bank_free[b] = tR
                    P.wait_tok("dve", tR)
                    for vc in range(2):
                        nc.vector.scalar_tensor_tensor(out=T5, in0=OS[:, vc, ts], scalar=gng[:, vc:vc + 1], in1=RR, op0=ALU.mult, op1=ALU.mult)
                        ins = nc.vector.tensor_tensor(out=GG[:, vc, ts], in0=T5, in1=GG[:, vc, ts], op=ALU.mult)
                    ogt = P.mark("dve", ins)
                P.release(tg[0]); P.release(tg[1])
                for oc in range(NCH):
                    for tb in range(NTB):
                        ts = slice(tb * TB, (tb + 1) * TB)
                        b = P.bank()
                        tp = P.group(P.ps[:, b, :], [(apo[kc][:, oc * 128:(oc + 1) * 128], GG[:, kc, ts]) for kc in range(2)], [b], tiles=to, waits=[ogt])
                        P.wait_tok("dve", tp)
                        ins = nc.vector.scalar_tensor_tensor(out=XT[:, oc, ts], in0=P.ps[:, b, :], scalar=G2[:, oc:oc + 1], in1=XT[:, oc, ts], op0=ALU.mult, op1=ALU.add)
                        P.bank_free[b] = P.mark("dve", ins)
                P.release(to[0]); P.release(to[1])
            P.barrier()
            if sst_dma is not None:
                for e in ("dve", "act", "sp"):
                    P.wait_dma(e, sst_dma)

        def s5_part(G2):
            PI = math.pi
            P.barrier()
            AR.reset()
            Bw = AR.bf16(2, 2, 8 * 128)
            Cw = AR.bf16(2, 2, 32 * 32)
            COEF = AR.f32(2, 2, 64)
            S5H0 = AR.f32(2, 64)
            S5ST = AR.f32(NSL * 2, 64)
            UT = AR.bf16(8, NT)
            ut_off = AR.off - 4096
            RT = Arena(P.ring[:].rearrange("p s n -> p (s n)").bitcast(F32))
            RTN = NSLOT * TSZ // 2
            mA = ccol("s5ma", 2)
            mB = ccol("s5mb", 2)
            CARRY = pcol("carry", 1)
            P.sp_sync()
            d0 = P.dma("sp", S5H0[:].rearrange("p a b -> p (a b)"), s5h0_in[:, :], "s5h0")

            def alloc(n):
                v = RT.ap[:, RT.off:RT.off + n]
                RT.off += n
                assert RT.off <= RTN, "ring temp overflow"
                return v

            def coefs(lre, lim, ls, n, want_f):
                DT = alloc(n); ZR = alloc(n); ZI = alloc(n); MAG = alloc(n); KF = alloc(n); KI = alloc(n).bitcast(I32)
                W = alloc(n); M = alloc(n); SN = alloc(n); CS = alloc(n)
                tA = P.mark("act", nc.scalar.activation(out=DT, in_=ls, func=ACT.Exp))
                P.wait_tok("dve", tA)
                nc.vector.tensor_tensor(out=ZR, in0=lre, in1=DT, op=ALU.mult)
                tD = P.mark("dve", nc.vector.tensor_tensor(out=ZI, in0=lim, in1=DT, op=ALU.mult))
                P.wait_tok("act", tD)
                tM = P.mark("act", nc.scalar.activation(out=MAG, in_=ZR, func=ACT.Exp))
                res = {}
                for name, shift, dst in (("sin", 0.0, SN), ("cos", PI / 2, CS)):
                    src = ZI
                    if shift:
                        nc.vector.tensor_scalar(out=DT, in0=ZI, scalar1=shift, scalar2=None, op0=ALU.add)
                        src = DT
                    nc.vector.tensor_scalar(out=KF, in0=src, scalar1=1.0 / (2 * PI), scalar2=None, op0=ALU.mult)
                    nc.vector.tensor_copy(out=KI, in_=KF)
                    nc.vector.tensor_copy(out=KF, in_=KI)
                    nc.vector.scalar_tensor_tensor(out=W, in0=KF, scalar=-2 * PI, in1=src, op0=ALU.mult, op1=ALU.add)
                    nc.vector.tensor_scalar(out=M, in0=W, scalar1=PI, scalar2=None, op0=ALU.is_gt)
                    nc.vector.scalar_tensor_tensor(out=W, in0=M, scalar=-2 * PI, in1=W, op0=ALU.mult, op1=ALU.add)
                    nc.vector.tensor_scalar(out=M, in0=W, scalar1=-PI, scalar2=None, op0=ALU.is_lt)
                    nc.vector.scalar_tensor_tensor(out=W, in0=M, scalar=2 * PI, in1=W, op0=ALU.mult, op1=ALU.add)
                    nc.vector.tensor_scalar(out=W, in0=W, scalar1=PI, scalar2=-PI, op0=ALU.min, op1=ALU.max)
                    tW = P.mark("dve", nc.vector.tensor_copy(out=M, in_=W))
                    P.wait_tok("act", tW)
                    tS = P.mark("act", nc.scalar.activation(out=dst, in_=M, func=ACT.Sin))
                    P.wait_tok("dve", tS)
                P.wait_tok("dve", tM)
                ABR = alloc(n); ABI = alloc(n)
                nc.vector.tensor_tensor(out=ABR, in0=MAG, in1=CS, op=ALU.mult)
                nc.vector.tensor_tensor(out=ABI, in0=MAG, in1=SN, op=ALU.mult)
                res["ab_re"], res["ab_im"] = ABR, ABI
                if want_f:
                    NRE = KF; DEN = W; FRE = alloc(n); FIM = alloc(n)
                    nc.vector.tensor_scalar(out=NRE, in0=ABR, scalar1=-1.0, scalar2=None, op0=ALU.add)
                    nc.vector.tensor_tensor(out=DEN, in0=lre, in1=lre, op=ALU.mult)
                    nc.vector.tensor_tensor(out=M, in0=lim, in1=lim, op=ALU.mult)
                    nc.vector.tensor_tensor(out=DEN, in0=DEN, in1=M, op=ALU.add)
                    nc.vector.reciprocal(out=DEN, in_=DEN)
                    nc.vector.tensor_tensor(out=FRE, in0=NRE, in1=lre, op=ALU.mult)
                    nc.vector.tensor_tensor(out=M, in0=ABI, in1=lim, op=ALU.mult)
                    nc.vector.tensor_tensor(out=FRE, in0=FRE, in1=M, op=ALU.add)
                    nc.vector.tensor_tensor(out=FRE, in0=FRE, in1=DEN, op=ALU.mult)
                    nc.vector.tensor_tensor(out=FIM, in0=ABI, in1=lre, op=ALU.mult)
                    nc.vector.tensor_tensor(out=M, in0=NRE, in1=lim, op=ALU.mult)
                    nc.vector.tensor_tensor(out=FIM, in0=FIM, in1=M, op=ALU.subtract)
                    nc.vector.tensor_tensor(out=FIM, in0=FIM, in1=DEN, op=ALU.mult)
                    res["f_re"], res["f_im"] = FRE, FIM
                return res

            RT.off = 0
            LB = alloc(192)
            CB = alloc(2048)
            dl = P.dma("sp", LB, s5lb_in[:, :], "s5l")
            dc = P.dma("sp", CB, s5cb_in[:, :], "s5l")
            for e in ("act", "dve"):
                P.wait_dma(e, dc)
            LBv = LB.rearrange("p (k n) -> p k n", k=3)
            r = coefs(LBv[:, 0, :], LBv[:, 1, :], LBv[:, 2, :], 64, False)
            for d in range(2):
                ar = r["ab_re"][:, d * 32:(d + 1) * 32]
                ai = r["ab_im"][:, d * 32:(d + 1) * 32]
                nc.vector.tensor_copy(out=COEF[:, d, 0, 0:32], in_=ar)
                nc.vector.tensor_copy(out=COEF[:, d, 0, 32:64], in_=ar)
                nc.vector.tensor_scalar(out=COEF[:, d, 1, 0:32], in0=ai, scalar1=-1.0, scalar2=None, op0=ALU.mult)
                nc.vector.tensor_copy(out=COEF[:, d, 1, 32:64], in_=ai)
            CBv = CB.rearrange("p (c d q n) -> p c d q n", c=2, d=2, q=32)
            for d in range(2):
                for c in range(2):
                    for g2 in range(2):
                        dst = Cw[:, d, c, :].rearrange("p (q m) -> p q m", q=32)[:, :, g2 * 16:(g2 + 1) * 16]
                        nc.vector.tensor_scalar(out=dst, in0=CBv[:, c, d, :, :], scalar1=mB[:, g2:g2 + 1], scalar2=(1.0 if c == 0 else -1.0), op0=ALU.mult, op1=ALU.mult)
            for d in range(2):
              for hh in range(2):
                P.sp_sync()
                RT.off = 0
                LA_ = alloc(768)
                BA = alloc(512)
                dl = P.dma("sp", LA_.rearrange("p (k n) -> p k n", k=3), s5la_in[:, :].rearrange("p (k d h n) -> p k d h n", k=3, d=2, h=2)[:, :, d, hh, :], "s5l")
                dc = P.dma("sp", BA.rearrange("p (c n) -> p c n", c=2), s5ba_in[:, :].rearrange("p (c d h n) -> p c d h n", c=2, d=2, h=2)[:, :, d, hh, :], "s5l")
                for e in ("act", "dve"):
                    P.wait_dma(e, dc)
                LAv = LA_.rearrange("p (k n) -> p k n", k=3)
                r = coefs(LAv[:, 0, :], LAv[:, 1, :], LAv[:, 2, :], 256, True)
                BAv = BA.rearrange("p (c n) -> p c n", c=2)
                BBR = alloc(256); BBI = alloc(256); TM = alloc(256)
                nc.vector.tensor_tensor(out=BBR, in0=r["f_re"], in1=BAv[:, 0, :], op=ALU.mult)
                nc.vector.tensor_tensor(out=TM, in0=r["f_im"], in1=BAv[:, 1, :], op=ALU.mult)
                nc.vector.tensor_tensor(out=BBR, in0=BBR, in1=TM, op=ALU.subtract)
                nc.vector.tensor_tensor(out=BBI, in0=r["f_re"], in1=BAv[:, 1, :], op=ALU.mult)
                nc.vector.tensor_tensor(out=TM, in0=r["f_im"], in1=BAv[:, 0, :], op=ALU.mult)
                nc.vector.tensor_tensor(out=BBI, in0=BBI, in1=TM, op=ALU.add)
                for c, src in ((0, BBR), (1, BBI)):
                    for g2 in range(2):
                        dst = Bw[:, d, c, hh * 512:(hh + 1) * 512].rearrange("p (h m) -> p h m", h=4)[:, :, g2 * 64:(g2 + 1) * 64]
                        nc.vector.tensor_scalar(out=dst, in0=src.rearrange("p (h m) -> p h m", h=4), scalar1=mA[:, g2:g2 + 1], scalar2=None, op0=ALU.mult)
            P.barrier()
            import os
            S5STOP = int(os.environ.get("S5_STOP", "99"))
            tfree = P.mark("dve", nc.vector.memset(P.scr[:, 2:3], 0.0))
            for s_ in range(NSLOT):
                P.slot_free[s_] = tfree
            if S5STOP <= 1:
                dump(COEF[:].rearrange("p a b c -> p (a b c)"))
                dump(XT[:, 0, :])
                dump(ARN[:, 0:2048])
                return
            for ch in range(8):
                t = P.load_tile(w_evin[ch])
                ap = P.tile_ap(t).rearrange("p (k n) -> p k n", k=16)
                for tb in range(NTB):
                    ts = slice(tb * TB, (tb + 1) * TB)
                    b = P.bank()
                    tp = P.group(P.ps[:, b, :], [(ap[:, kc, :], HM[:, kc, ts]) for kc in range(16)], [b], tiles=[t])
                    P.wait_tok("act", tp)
                    P.bank_free[b] = P.mark("act", nc.scalar.activation(out=UT[:, ch, ts], in_=P.ps[:, b, :], func=ACT.Copy))
                P.release(t)
            P.barrier()
            YS = HM[:].rearrange("p c t -> p (c t)").bitcast(F32).rearrange("p (c t) -> p c t", c=8)
            sd = vcol("s5_d", 0, 8)
            for ch in range(8):
                nc.vector.tensor_scalar(out=YS[:, ch, :], in0=UT[:, ch, :], scalar1=sd[:, ch:ch + 1], scalar2=None, op0=ALU.mult)
            P.barrier()
            if S5STOP <= 2:
                return
            RT.off = 0
            Hbuf = [alloc(1024).rearrange("p (t m) -> p t m", t=16) for _ in range(2)]
            Hbb = [alloc(512).bitcast(BF16).rearrange("p (t m) -> p t m", t=16) for _ in range(2)]
            T1 = AR.f32(64); T2 = AR.f32(64); HIN = AR.f32(64)
            Cw3 = alloc(1024).bitcast(BF16).rearrange("p (d c m) -> p d c m", d=2, c=2)
            nc.vector.memset(Cw3[:].rearrange("p d c m -> p (d c m)"), 0.0)
            for d_ in range(2):
                for c_ in range(2):
                    nc.vector.tensor_copy(out=Cw3[:, d_, c_, :].rearrange("p (h m) -> p h m", h=8)[:, :, 32:64],
                                          in_=Cw[:, d_, c_, :].rearrange("p (h r m) -> p h r m", h=8, r=4)[:, :, 3, :])
            Bw3 = alloc(2048).bitcast(BF16).rearrange("p (d c m) -> p d c m", d=2, c=2)
            m96 = ccol("m96", 1)
            tb3 = P.mark("dve", nc.vector.tensor_scalar(out=Bw3[:].rearrange("p d c m -> p (d c m)"), in0=Bw[:].rearrange("p d c m -> p (d c m)"), scalar1=m96, scalar2=None, op0=ALU.mult))
            P.wait_tok("pe", tb3)
            P.wait_dma("dve", d0)
            NBLK = NT // 16
            ST = AR.f32(64)
            seq = []
            for d in range(2):
                blocks = list(range(NBLK)) if d == 0 else list(range(NBLK - 1, -1, -1))
                for bi, blk in enumerate(blocks):
                    seq.append((d, bi, blk))
            NS = len(seq)
            tE = [None] * NS
            tB = [None] * NS
            yfree = [None, None]
            pe_y = [None, None]
            hb_done = [None, None]

            def emit_expand(i):
                d, bi, blk = seq[i]
                t0 = blk * 16
                P.wait_tok("pe", tB[i - 1] if i > 0 else None)
                last = None
                for c in range(2):
                    for q in range(32):
                        rg = q % 4
                        col = c * 128 + (q // 4) * 16
                        if rg < 3:
                            rows = slice(32 * rg, 32 * rg + 32)
                            wsrc = Bw[rows, d, c, (q // 4) * 128:(q // 4 + 1) * 128]
                        else:
                            rows = slice(64, 128)
                            wsrc = Bw3[rows, d, c, (q // 4) * 128:(q // 4 + 1) * 128]
                        last = nc.tensor.matmul(P.ps[:, rg, col:col + 16], wsrc, UT[rows, q // 4, t0:t0 + 16], start=True, stop=True)
                tE[i] = P.mark("pe", last)
                P.last_pe = tE[i]

            def emit_bcopy(i):
                par = i % 2
                H = Hbuf[par]
                P.wait_tok("act", tE[i])
                ins = None
                for rg in range(4):
                    o_ap = H[:].rearrange("p t (c q r) -> p t c q r", c=2, r=4)[:, :, :, :, rg]
                    i_ap = P.ps[:, rg, 0:256].rearrange("p (c q t) -> p t c q", c=2, q=8)
                    ins = nc.scalar.activation(out=o_ap, in_=i_ap, func=ACT.Copy)
                tB[i] = P.mark("act", ins)

            emit_expand(0)
            emit_bcopy(0)
            pend = None
            for i in range(NS):
                d, bi, blk = seq[i]
                par = i % 2
                t0 = blk * 16
                A1 = COEF[:, d, 0, :]
                A2 = COEF[:, d, 1, :]
                H = Hbuf[par]
                if i + 1 < NS:
                    emit_expand(i + 1)
                    emit_bcopy(i + 1)
                P.wait_tok("dve", tB[i])
                steps = list(range(16)) if d == 0 else list(range(15, -1, -1))
                slot_first = (t0 % SL == 0) if d == 0 else ((t0 + 16) % SL == 0)
                slot_last = ((t0 + 16) % SL == 0) if d == 0 else (t0 % SL == 0)
                if slot_first:
                    if bi == 0:
                        nc.vector.tensor_copy(out=HIN, in_=S5H0[:, d, :])
                    else:
                        nc.vector.tensor_scalar(out=HIN, in0=ST, scalar1=CARRY, scalar2=None, op0=ALU.mult)
                    prev = HIN
                else:
                    prev = ST
                for t in steps:
                    nc.vector.tensor_tensor(out=T1, in0=A1, in1=prev, op=ALU.mult)
                    nc.vector.tensor_tensor(out=T2[:, 0:32], in0=A2[:, 0:32], in1=prev[:, 32:64], op=ALU.mult)
                    nc.vector.tensor_tensor(out=T2[:, 32:64], in0=A2[:, 32:64], in1=prev[:, 0:32], op=ALU.mult)
                    nc.vector.tensor_tensor(out=T1, in0=T1, in1=T2, op=ALU.add)
                    nc.vector.tensor_tensor(out=H[:, t, :], in0=T1, in1=H[:, t, :], op=ALU.add)
                    prev = H[:, t, :]
                if slot_last:
                    s_idx = t0 // SL
                    nc.vector.tensor_copy(out=S5ST[:, s_idx * 2 + d, :], in_=prev)
                tSc = P.mark("dve", nc.vector.tensor_copy(out=ST, in_=prev))
                if pend is not None:
                    pb, pt0 = pend
                    P.wait_tok("dve", pe_y[pb])
                    dst = YS[:, :, pt0:pt0 + 16]
                    ins = nc.vector.tensor_tensor(out=dst, in0=dst, in1=P.ps[:, 4 + pb, 0:128].rearrange("p (c t) -> p c t", c=8), op=ALU.add)
                    yfree[pb] = P.mark("dve", ins)
                if S5STOP <= 3:
                    continue
                P.wait_tok("act", tSc)
                P.wait_tok("act", pe_y[par])
                hb_done[par] = P.mark("act", nc.scalar.activation(out=Hbb[par][:].rearrange("p t m -> p (t m)"), in_=H[:].rearrange("p t m -> p (t m)"), func=ACT.Copy))
                P.wait_tok("pe", hb_done[par])
                P.wait_tok("pe", yfree[par])
                last = None
                for ch in range(8):
                    for rg in (3, 0, 1, 2):
                        q = 4 * ch + rg
                        for c in range(2):
                            if rg < 3:
                                o_ap = P.ps[32 * rg:32 * rg + 32, 4 + par, ch * 16:ch * 16 + 16]
                                w_ap = Cw[:, d, c, q * 32:(q + 1) * 32]
                            else:
                                o_ap = P.ps[64:128, 4 + par, ch * 16:ch * 16 + 16]
                                w_ap = Cw3[:, d, c, ch * 64:(ch + 1) * 64]
                            last = nc.tensor.matmul(o_ap, w_ap, Hbb[par][:, :, c * 32 + q], start=(c == 0), stop=(c == 1))
                pe_y[par] = P.mark("pe", last)
                P.last_pe = pe_y[par]
                pend = (par, t0)
            if S5STOP <= 3:
                P.barrier()
                for b in range(8):
                    P.bank_free[b] = None
                return
            pb, pt0 = pend
            P.wait_tok("dve", pe_y[pb])
            dst = YS[:, :, pt0:pt0 + 16]
            nc.vector.tensor_tensor(out=dst, in0=dst, in1=P.ps[:, 4 + pb, 0:128].rearrange("p (c t) -> p c t", c=8), op=ALU.add)
            P.barrier()
            for b in range(8):
                P.bank_free[b] = None
            YA = UT
            RT.off = 0
            G1 = alloc(NT)
            for ch in range(8):
                y = YS[:, ch, :]
                tA = P.mark("act", nc.scalar.activation(out=G1, in_=y, func=ACT.Square))
                P.wait_tok("dve", tA)
                nc.vector.tensor_scalar(out=G1, in0=G1, scalar1=0.044715, scalar2=1.0, op0=ALU.mult, op1=ALU.add)
                tD = P.mark("dve", nc.vector.tensor_tensor(out=G1, in0=G1, in1=y, op=ALU.mult))
                P.wait_tok("act", tD)
                tA2 = P.mark("act", nc.scalar.activation(out=G1, in_=G1, func=ACT.Exp, scale=-GELU_C))
                P.wait_tok("dve", tA2)
                nc.vector.tensor_scalar(out=G1, in0=G1, scalar1=1.0, scalar2=None, op0=ALU.add)
                nc.vector.reciprocal(out=G1, in_=G1)
                tD2 = P.mark("dve", nc.vector.tensor_tensor(out=YA[:, ch, :], in0=G1, in1=y, op=ALU.mult))
                P.wait_tok("act", tD2)
            P.barrier()
            tfree = P.mark("dve", nc.vector.memset(P.scr[:, 2:3], 0.0))
            for s_ in range(NSLOT):
                P.slot_free[s_] = tfree
            YG = HM[:, 0:8, :]
            NGLB = S5H0[:, 0, 0:8]
            nc.vector.tensor_scalar(out=NGLB, in0=vcol("glu_b", 0, 8), scalar1=-1.0, scalar2=None, op0=ALU.mult)
            ygt = None
            P.barrier()
            for oc in range(8):
                t = P.load_tile(w_glu[oc], n=1024)
                ap = P.tile_ap(t)[:, 0:1024].rearrange("p (k n) -> p k n", k=8)
                for tb in range(NTB):
                    ts = slice(tb * TB, (tb + 1) * TB)
                    b = P.bank()
                    tp = P.group(P.ps[:, b, :], [(ap[:, kc, :], YA[:, kc, ts]) for kc in range(8)], [b], tiles=[t])
                    P.wait_tok("act", tp)
                    tA = P.mark("act", nc.scalar.activation(out=P.ps[:, b, :], in_=P.ps[:, b, :], func=ACT.Exp, scale=-1.0, bias=NGLB[:, oc:oc + 1]))
                    P.wait_tok("dve", tA)
                    nc.vector.tensor_scalar(out=P.ps[:, b, :], in0=P.ps[:, b, :], scalar1=1.0, scalar2=None, op0=ALU.add)
                    nc.vector.reciprocal(out=P.ps[:, b, :], in_=P.ps[:, b, :])
                    ygt = P.mark("dve", nc.vector.tensor_tensor(out=YG[:, oc, ts], in0=P.ps[:, b, :], in1=YA[:, oc, ts], op=ALU.mult))
                    P.bank_free[b] = ygt
                P.release(t)
            for half in range(2):
                tiles = [P.load_tile(w_evout[half * 4 + j]) for j in range(4)]
                aps = [P.tile_ap(t) for t in tiles]
                for oc in range(NCH):
                    for tb in range(NTB):
                        ts = slice(tb * TB, (tb + 1) * TB)
                        b = P.bank()
                        tp = P.group(P.ps[:, b, :], [(aps[j][:, oc * 128:(oc + 1) * 128], YG[:, half * 4 + j, ts]) for j in range(4)], [b], tiles=tiles, waits=[ygt])
                        P.wait_tok("dve", tp)
                        ins = nc.vector.scalar_tensor_tensor(out=XT[:, oc, ts], in0=P.ps[:, b, :], scalar=G2[:, oc:oc + 1], in1=XT[:, oc, ts], op0=ALU.mult, op1=ALU.add)
                        P.bank_free[b] = P.mark("dve", ins)
                for t in tiles:
                    P.release(t)
            P.barrier()
            P.wait("sp", P.ms["dve"], P.msn["dve"])
            ds = P.dma("sp", s5_out[:, :], S5ST[:].rearrange("p a b -> p (a b)"), "s5o")
            for e in ("dve", "act", "pe"):
                P.wait_dma(e, ds)

        P.wait_dma("dve", ldv)
        lru_prep()
        for l in range(2):
            mvt = ada(l)
            hm = norm(MV[:, l, 0, :], MV[:, l, 1, :], mvt)
            ffn(2 * l, MV[:, l, 2, :], hm)
            hm = norm(MV[:, l, 3, :], MV[:, l, 4, :], mvt)
            if l == 0 and mix_gla:
                gla_part(MV[:, l, 5, :])
            if l == 0 and mix_s5:
                s5_part(MV[:, l, 5, :])
            if l == 1 and mix_odd:
                odd_mixer(l, MV[:, l, 5, :])
            hm = norm(MV[:, l, 6, :], MV[:, l, 7, :], mvt)
            ffn(2 * l + 1, MV[:, l, 8, :], hm)
        fin = norm(vcol("final_g"), None, None, final=True)
        P.barrier()
        for e in ("sp",):
            P.wait_tok(e, fin)
            P.wait("sp", P.ms["dve"], P.msn["dve"])
            P.wait("sp", P.ms["act"], P.msn["act"])
        o1 = P.dma("sp", yT_out[:, :], XT[:].rearrange("p c t -> p (c t)"), "out")
        o2 = P.dma("sp", lru_out[:, :], LRUST[:].rearrange("p s d c -> p (s d c)"), "out")
        P.wait_dma("sp", o2)
    return nc


def _offsets(items):
    off = {}
    o = 0
    for n, w in items:
        off[n] = o
        o += w
    return off, o


VOFF, NVEC = _offsets([("norm_g", 96), ("final_g", 16), ("ada_b", 288), ("conv_w", 64), ("conv_b", 16),
                       ("lru_ba", 32), ("lru_bx", 32), ("lru_lam", 32), ("gla_gb", 8), ("gla_ng", 2), ("s5_d", 8), ("glu_b", 8)])
COFF, NCST = _offsets([("ident", 128), ("jrev", 128), ("maskf", 128), ("maskb", 128), ("rm", SL), ("s5ma", 2), ("s5mb", 2), ("m96", 1)])
POFF, NPC = _offsets([("cvm", 3 * SL), ("carry", 1), ("lh0", 2 * NCH)])


def pvec(v):
    v = np.asarray(v, np.float32).reshape(-1, 128)
    return np.ascontiguousarray(v.T)


def tile_cols(W, c0, ncols=128):
    K = W.shape[0]
    t = W[:, c0:c0 + ncols].reshape(K // 128, 128, ncols).transpose(1, 0, 2)
    return t.reshape(128, -1)


def prep_shared(inp):
    sh = {}
    ada_w = inp["ada_w"]
    wa = np.empty((2 * 144, 128, TSZ), np.float32)
    for l in range(2):
        wa[l * 144:(l + 1) * 144] = ada_w[l].reshape(16, 128, 144, 128).transpose(2, 1, 0, 3).reshape(144, 128, TSZ)
    sh["w_ada"] = wa
    wi = np.empty((4 * 88, 128, TSZ), np.float32)
    wo = np.empty((4 * 44, 128, TSZ), np.float32)
    for l in range(2):
        for w in range(2):
            f = 2 * l + w
            Wi = inp["ffn_w_in"][l, w].reshape(16, 128, 88, 128).transpose(2, 1, 0, 3).reshape(88, 128, TSZ)
            wi[f * 88:(f + 1) * 88:2] = Wi[0:44]
            wi[f * 88 + 1:(f + 1) * 88:2] = Wi[44:88]
            wo[f * 44:(f + 1) * 44] = inp["ffn_w_out"][l, w].reshape(44, 128, TSZ)
    sh["w_in"] = wi
    sh["w_out"] = wo
    sh["w_odin"] = np.ascontiguousarray(inp["od_w_in"][0].reshape(16, 128, 32, 128).transpose(2, 1, 0, 3).reshape(32, 128, TSZ))
    sh["w_odout"] = np.ascontiguousarray(inp["od_w_out"][0].reshape(16, 128, TSZ))
    wl = np.stack([inp["lru_wa"][0], inp["lru_wx"][0]], 0)
    wl = wl.reshape(2, 2, 8, 2, 128, 256).transpose(2, 4, 0, 1, 3, 5)
    sh["w_lru"] = np.ascontiguousarray(wl).reshape(8, 128, TSZ)
    vec = np.zeros((128, NVEC), np.float32)

    def put(name, arr):
        a = pvec(np.asarray(arr).reshape(-1))
        vec[:, VOFF[name]:VOFF[name] + a.shape[1]] = a
    put("norm_g", inp["norm_g"])
    put("final_g", inp["final_norm_g"])
    put("ada_b", inp["ada_b"])
    put("conv_w", inp["lru_conv_w"][0])
    put("conv_b", inp["lru_conv_b"][0])
    put("lru_ba", inp["lru_ba"][0])
    put("lru_bx", inp["lru_bx"][0])
    put("lru_lam", inp["lru_lam"][0])
    put("gla_gb", inp["gla_gate_b"][0])
    put("gla_ng", inp["gla_norm_g"][0])
    evin = np.zeros((2048, 33 * 128), np.float32)
    evin[:, :4128] = inp["ev_w_in"][0]
    sh["w_evin"] = np.ascontiguousarray(evin.reshape(16, 128, 33, 128).transpose(2, 1, 0, 3).reshape(33, 128, TSZ))
    sh["w_evout"] = np.ascontiguousarray(inp["ev_w_out"][0].reshape(16, 128, TSZ))
    sh["gw2"] = np.ascontiguousarray(inp["gla_gate_w2"][0].transpose(1, 0, 2).reshape(16, 1024))
    put("s5_d", inp["s5_d"][0])
    put("glu_b", inp["s5_glu_b"][0])
    sh["w_glu"] = np.ascontiguousarray(inp["s5_glu_w"][0].reshape(8, 128, 8, 128).transpose(2, 1, 0, 3).reshape(8, 128, 1024))
    lre, lim = inp["s5_lam_re"][0], inp["s5_lam_im"][0]
    ls = np.broadcast_to(inp["s5_log_step"][0][:, :, None], (2, 64, 64))
    def layB(a):
        return a.reshape(2, 32, 2, 64).transpose(2, 3, 0, 1).reshape(128, 64)
    sh["s5lb"] = np.ascontiguousarray(np.stack([layB(lre), layB(lim), layB(ls)], 1).reshape(128, 192)).astype(np.float32)
    def layCB(a):
        return a.reshape(2, 32, 2, 16, 64).transpose(2, 4, 0, 1, 3).reshape(128, 2, 32, 16)
    sh["s5cb"] = np.ascontiguousarray(np.stack([layCB(inp["s5_c_re"][0]), layCB(inp["s5_c_im"][0])], 1).reshape(128, 2048))
    def layA(a):
        t = a.reshape(2, 8, 8, 64).transpose(2, 0, 1, 3)
        return np.broadcast_to(t[:, None], (8, 16, 2, 8, 64)).reshape(128, 1024)
    sh["s5la"] = np.ascontiguousarray(np.stack([layA(lre), layA(lim), layA(ls)], 1).reshape(128, 3072)).astype(np.float32)
    def layBA(a):
        return a.reshape(2, 8, 8, 64, 16).transpose(2, 4, 0, 1, 3).reshape(128, 1024)
    sh["s5ba"] = np.ascontiguousarray(np.stack([layBA(inp["s5_b_re"][0]), layBA(inp["s5_b_im"][0])], 1).reshape(128, 2048))
    sh["vecs"] = vec
    cst = np.zeros((128, NCST), np.float32)
    cst[:, COFF["ident"]:COFF["ident"] + 128] = np.eye(128, dtype=np.float32)
    cst[:, COFF["jrev"]:COFF["jrev"] + 128] = np.eye(128, dtype=np.float32)[::-1]
    jj = np.arange(128)[:, None]
    ii = np.arange(128)[None, :]
    same = (jj // 64) == (ii // 64)
    cst[:, COFF["maskf"]:COFF["maskf"] + 128] = (same & (jj <= ii)).astype(np.float32)
    cst[:, COFF["maskb"]:COFF["maskb"] + 128] = (same & (jj >= ii)).astype(np.float32)
    cst[:, COFF["rm"]:COFF["rm"] + SL] = (np.arange(SL) % 64 != 0).astype(np.float32)[None, :]
    pidx = np.arange(128)
    for g2 in range(2):
        cst[:, COFF["s5ma"] + g2] = ((pidx // 16) % 2 == g2).astype(np.float32)
        cst[:, COFF["s5mb"] + g2] = ((pidx // 64) == g2).astype(np.float32)
    cst[:, COFF["m96"]] = (pidx >= 96).astype(np.float32)
    sh["consts"] = cst
    return sh


PROMPT_ASSIGN = [[0, 1, 2], [3, 4, 5], [6, 7, 8], [9, 10, 11], [12, 13], [14, 15]]


def prep_core(inp, core):
    m = {}
    x = np.zeros((NT, D), np.float32)
    pc = np.zeros((128, NPC), np.float32)
    t = np.arange(SL)
    if core < 2:
        x[:] = inp["x_sample"][core]
        cond = inp["c"][core]
        seg = 64
        pc[:, POFF["carry"]] = 1.0
        lh0 = np.stack([pvec(inp["state_lru"][core, 0, d]) for d in range(2)], 1)
        pc[:, POFF["lh0"]:POFF["lh0"] + 32] = lh0.reshape(128, 32)
    else:
        for s, b in enumerate(PROMPT_ASSIGN[core - 2]):
            x[SL * s:SL * (s + 1)] = inp["x_prompt"][b]
        cond = inp["c_ctx"]
        seg = 256
    cvm = np.stack([(t % seg >= 2), (t % seg >= 1), (t % seg <= seg - 2)], 0).astype(np.float32)
    pc[:, POFF["cvm"]:POFF["cvm"] + 3 * SL] = cvm.reshape(1, -1)
    m["pcore"] = pc
    gs0 = np.zeros((4, 128, 2, SL), np.float32)
    if core < 2:
        gs0[:] = inp["state_gla"][core, 0].transpose(1, 2, 0, 3)
    m["gs0"] = gs0.reshape(4, 128, 2 * SL)
    h0 = np.zeros((128, 2, 2, 32), np.float32)
    if core < 2:
        for c, nm in enumerate(("state_s5_re", "state_s5_im")):
            h0[:, :, c, :] = inp[nm][core, 0].reshape(2, 32, 2, 64).transpose(2, 3, 0, 1).reshape(128, 2, 32)
    m["s5h0"] = h0.reshape(128, 128)
    m["xT"] = np.ascontiguousarray(x.T.reshape(NCH, 128, NT).transpose(1, 0, 2)).reshape(128, NCH * NT)
    m["cond"] = pvec(cond)
    return m


def assemble(inp, results):
    yp = np.zeros((16, 256, D), np.float32)
    ys = np.zeros((2, 1024, D), np.float32)
    st_re = np.zeros((16, 1, 2, 64, 64), np.float32)
    st_im = np.zeros((16, 1, 2, 64, 64), np.float32)
    st_gla = np.zeros((16, 1, 2, 4, 128, 256), np.float32)
    st_lru = np.zeros((16, 1, 2, D), np.float32)
    for core in range(N_CORES):
        yT = results[core]["yT"].reshape(128, NCH, NT)
        y = yT.transpose(2, 1, 0).reshape(NT, D)
        if core < 2:
            ys[core] = y
        else:
            ls = results[core]["lru_st"].reshape(128, NSL, 2, NCH)
            gs = results[core]["gla_st"].reshape(NSL, 2, 4, 128, SL)
            s5 = results[core]["s5_st"].reshape(2, 64, NSL, 2, 2, 32)
            for s, b in enumerate(PROMPT_ASSIGN[core - 2]):
                yp[b] = y[SL * s:SL * (s + 1)]
                st_lru[b, 0] = ls[:, s].transpose(1, 2, 0).reshape(2, D)
                st_gla[b, 0] = gs[s]
                st_re[b, 0] = s5[:, :, s, :, 0, :].transpose(2, 3, 0, 1).reshape(2, 64, 64)
                st_im[b, 0] = s5[:, :, s, :, 1, :].transpose(2, 3, 0, 1).reshape(2, 64, 64)
    return yp, ys, st_re, st_im, st_gla, st_lru


def kernel(**inputs):
    inp = {k: np.asarray(v) for k, v in inputs.items()}
    shared = prep_shared(inp)
    in_maps = []
    for core in range(N_CORES):
        m = dict(shared)
        m.update(prep_core(inp, core))
        in_maps.append(m)
    nc = build_program()
    res = run_bass_kernel_spmd(nc, in_maps, core_ids=list(range(N_CORES)))
    return assemble(inp, res.results)
```

```python
import contextlib
import math
import numpy as np
import concourse.bass as bass
import concourse.mybir as mybir
from concourse.bass_utils import run_bass_kernel_spmd

F32 = mybir.dt.float32
BF16 = mybir.dt.bfloat16
I32 = mybir.dt.int32
ACT = mybir.ActivationFunctionType
ALU = mybir.AluOpType

D = 2048
NCH = 16
NT = 1024
TB = 512
NTB = 2
SL = 256
NSL = 4
DFF = 5632
NSLOT = 10
TSZ = 2048
EPS = 1e-6
N_CORES = 8
ARENA_F32 = 13056
GELU_C = 1.5957691216057308


class Tok:
    __slots__ = ("eng", "val")

    def __init__(self, eng, val):
        self.eng = eng
        self.val = val


class Prog:
    def __init__(self, nc, es):
        self.nc = nc
        self.es = es
        self.engs = {"pe": nc.tensor, "act": nc.scalar, "dve": nc.vector, "pool": nc.gpsimd, "sp": nc.sync}
        self.ms = {k: es.enter_context(nc.semaphore("ms_" + k)) for k in ("pe", "act", "dve")}
        self.msn = {k: 0 for k in self.ms}
        self.waited = {}
        self.dsems = {}
        self.ps = es.enter_context(nc.psum_tensor("ps", [128, 8, 512], F32))
        self.bank_free = [None] * 8
        self.bank_rr = 0
        self.reserved = set()
        self.ring = es.enter_context(nc.sbuf_tensor("ring", [128, NSLOT, TSZ], BF16))
        self.slot_ld = [es.enter_context(nc.semaphore("ld%d" % s)) for s in range(NSLOT)]
        self.slot_ldn = [0] * NSLOT
        self.slot_free = [None] * NSLOT
        self.slot_rr = 0
        self.slot_open = [False] * NSLOT
        self.last_pe = None
        self.scr = es.enter_context(nc.sbuf_tensor("scr", [128, 4], F32))

    def wait(self, eng, sem, val):
        key = (eng, id(sem))
        if self.waited.get(key, 0) >= val:
            return
        self.waited[key] = val
        self.engs[eng].wait_ge(sem, val)

    def wait_tok(self, eng, tok):
        if tok is None or tok.eng == eng:
            return
        self.wait(eng, self.ms[tok.eng], tok.val)

    def fence(self, eng, tok):
        self.wait(eng, self.ms[tok.eng], tok.val)

    def mark(self, eng, ins):
        ins.then_inc(self.ms[eng], 1)
        self.msn[eng] += 1
        return Tok(eng, self.msn[eng])

    def dma(self, eng, out, in_, name):
        if name not in self.dsems:
            self.dsems[name] = [self.es.enter_context(self.nc.semaphore("d_" + name)), 0]
        s = self.dsems[name]
        self.engs[eng].dma_start(out=out, in_=in_).then_inc(s[0], 16)
        s[1] += 16
        return (s[0], s[1])

    def wait_dma(self, eng, d):
        self.wait(eng, d[0], d[1])

    def barrier(self):
        nc = self.nc
        ta = self.mark("act", nc.scalar.activation(out=self.scr[:, 0:1], in_=self.scr[:, 0:1], func=ACT.Copy))
        td = self.mark("dve", nc.vector.memset(self.scr[:, 1:2], 0.0))
        for e in ("pe", "act", "dve"):
            self.wait_tok(e, ta)
            self.wait_tok(e, td)
            self.wait_tok(e, self.last_pe)

    def sp_sync(self):
        self.barrier()
        self.wait("sp", self.ms["act"], self.msn["act"])
        self.wait("sp", self.ms["dve"], self.msn["dve"])
        self.wait_tok("sp", self.last_pe)

    def pool_sync(self):
        self.barrier()
        self.wait("pool", self.ms["act"], self.msn["act"])
        self.wait("pool", self.ms["dve"], self.msn["dve"])
        self.wait_tok("pool", self.last_pe)

    def bank(self):
        b = self.bank_rr
        while b in self.reserved:
            b = (b + 1) % 8
        self.bank_rr = (b + 1) % 8
        return b

    def bank_pair(self):
        b = self.bank_rr
        if b % 2:
            b = (b + 1) % 8
        while b in self.reserved or (b + 1) in self.reserved:
            b = (b + 2) % 8
        self.bank_rr = (b + 2) % 8
        return b, b + 1

    def load_tile(self, src_ap, n=TSZ):
        s = self.slot_rr
        for _ in range(NSLOT):
            if not self.slot_open[s]:
                break
            s = (s + 1) % NSLOT
        self.slot_rr = (s + 1) % NSLOT
        assert not self.slot_open[s], "ring slot %d still open" % s
        self.slot_open[s] = True
        self.wait_tok("pool", self.slot_free[s])
        self.nc.gpsimd.dma_start(out=self.ring[:, s, 0:n], in_=src_ap).then_inc(self.slot_ld[s], 16)
        self.slot_ldn[s] += 16
        return (s, self.slot_ldn[s])

    def release(self, t):
        self.slot_open[t[0]] = False

    def tile_ap(self, t):
        return self.ring[:, t[0], :]

    def transpose(self, out_ap, in_ap, ident, banks, waits=()):
        for b in banks:
            self.wait_tok("pe", self.bank_free[b])
        for w in waits:
            self.wait_tok("pe", w)
        ins = self.nc.tensor.transpose(out=out_ap, in_=in_ap, identity=ident)
        tok = self.mark("pe", ins)
        self.last_pe = tok
        return tok

    def group2(self, items, banks, tiles=(), waits=()):
        for b in banks:
            self.wait_tok("pe", self.bank_free[b])
        for t in tiles:
            self.wait("pe", self.slot_ld[t[0]], t[1])
        for w in waits:
            self.wait_tok("pe", w)
        n = len(items)
        ins = None
        for i, (o, l, r) in enumerate(items):
            ins = self.nc.tensor.matmul(o, l, r, start=(i == 0), stop=(i == n - 1))
        tok = self.mark("pe", ins)
        self.last_pe = tok
        for t in tiles:
            self.slot_free[t[0]] = tok
        return tok

    def group(self, out_ap, pairs, banks, tiles=(), waits=(), mark=True):
        for b in banks:
            self.wait_tok("pe", self.bank_free[b])
        for t in tiles:
            self.wait("pe", self.slot_ld[t[0]], t[1])
        for w in waits:
            self.wait_tok("pe", w)
        n = len(pairs)
        ins = None
        for i, (l, r) in enumerate(pairs):
            ins = self.nc.tensor.matmul(out_ap, l, r, start=(i == 0), stop=(i == n - 1))
        if not mark:
            return None
        tok = self.mark("pe", ins)
        self.last_pe = tok
        for t in tiles:
            self.slot_free[t[0]] = tok
        return tok


class Arena:
    def __init__(self, ap):
        self.ap = ap
        self.off = 0

    def reset(self):
        self.off = 0

    def f32(self, *shape):
        n = int(np.prod(shape))
        v = self.ap[:, self.off:self.off + n]
        self.off += n
        assert self.off <= ARENA_F32, ("arena overflow", self.off)
        if len(shape) == 2:
            return v.rearrange("p (a b) -> p a b", a=shape[0])
        if len(shape) == 3:
            return v.rearrange("p (a b c) -> p a b c", a=shape[0], b=shape[1])
        return v

    def bf16(self, *shape):
        n = int(np.prod(shape))
        assert n % 2 == 0
        v = self.ap[:, self.off:self.off + n // 2].bitcast(BF16)
        self.off += n // 2
        assert self.off <= ARENA_F32, ("arena overflow", self.off)
        if len(shape) == 2:
            return v.rearrange("p (a b) -> p a b", a=shape[0])
        if len(shape) == 3:
            return v.rearrange("p (a b c) -> p a b c", a=shape[0], b=shape[1])
        return v


def build_program(mix_even=True, mix_odd=True, debug=False, mix_gla=None, mix_s5=None):
    if mix_gla is None:
        mix_gla = mix_even
    if mix_s5 is None:
        mix_s5 = mix_even
    nc = bass.Bass("TRN2", target_bir_lowering=False)
    es = contextlib.ExitStack()
    dbg_list = []

    def din(name, shape):
        return nc.dram_tensor(name, list(shape), F32, kind="ExternalInput").ap()

    def dout(name, shape):
        return nc.dram_tensor(name, list(shape), F32, kind="ExternalOutput").ap()

    xT_in = din("xT", [128, NCH * NT])
    cond_in = din("cond", [128, NCH])
    vecs_in = din("vecs", [128, NVEC])
    cst_in = din("consts", [128, NCST])
    pc_in = din("pcore", [128, NPC])
    w_ada = din("w_ada", [2 * 144, 128, TSZ])
    w_in = din("w_in", [4 * 88, 128, TSZ])
    w_out = din("w_out", [4 * 44, 128, TSZ])
    w_odin = din("w_odin", [32, 128, TSZ])
    w_odout = din("w_odout", [16, 128, TSZ])
    w_lru = din("w_lru", [8, 128, TSZ])
    w_evin = din("w_evin", [33, 128, TSZ])
    w_evout = din("w_evout", [16, 128, TSZ])
    gw2_in = din("gw2", [16, 1024])
    gs0_in = din("gs0", [4, 128, 2 * SL])
    w_glu = din("w_glu", [8, 128, 1024])
    s5lb_in = din("s5lb", [128, 192])
    s5cb_in = din("s5cb", [128, 2048])
    s5la_in = din("s5la", [128, 3072])
    s5ba_in = din("s5ba", [128, 2048])
    s5h0_in = din("s5h0", [128, 128])
    yT_out = dout("yT", [128, NCH * NT])
    lru_out = dout("lru_st", [128, NSL * 2 * NCH])
    gla_out = dout("gla_st", [NSL * 2 * 4, 128, SL])
    s5_out = dout("s5_st", [128, NSL * 2 * 64])
    dbg_out = dout("dbg", [8, 128, 2048]) if debug else None

    with es:
        P = Prog(nc, es)
        sb = lambda name, shape, dt=F32: es.enter_context(nc.sbuf_tensor(name, list(shape), dt))
        XT = sb("XT", [128, NCH, NT])
        HM = sb("HM", [128, NCH, NT], BF16)
        ARN = sb("ARN", [128, ARENA_F32])
        AR = Arena(ARN[:])
        VEC = sb("VEC", [128, NVEC])
        CST = sb("CST", [128, NCST])
        PC = sb("PC", [128, NPC])
        MOD = sb("MOD", [128, 2, 144])
        MV = sb("MV", [128, 2, 9, 16])
        CA = sb("CA", [128, NCH], BF16)
        CAf = sb("CAf", [128, NCH])
        ONES = sb("ONES", [128, 128], BF16)
        EPSB = sb("EPSB", [128, 1])
        LRUST = sb("LRUST", [128, NSL, 2, NCH])
        NSP = sb("NSP", [128, 2, 2, NCH])

        state = {"xt_w": None, "hm_r": None}

        def vcol(name, j=0, w=16):
            o = VOFF[name] + j * w
            return VEC[:, o:o + w]

        def ccol(name, w):
            o = COFF[name]
            return CST[:, o:o + w]

        def pcol(name, w):
            o = POFF[name]
            return PC[:, o:o + w]

        def dump(ap2d):
            if not debug:
                return
            i = len(dbg_list)
            dbg_list.append(i)
            P.barrier()
            P.wait("sp", P.ms["dve"], P.msn["dve"])
            P.wait("sp", P.ms["act"], P.msn["act"])
            P.wait_tok("sp", P.last_pe)
            d = P.dma("sp", dbg_out[i, :, 0:ap2d.shape[1]], ap2d, "dbg")
            for e in ("pe", "act", "dve"):
                P.wait_dma(e, d)

        ld = []
        for q in range(4):
            ld.append(P.dma("sp", XT[:, 4 * q:4 * q + 4, :], xT_in[:, 4 * q * NT:(4 * q + 4) * NT].rearrange("p (c t) -> p c t", c=4), "inx%d" % q))
        ldv = P.dma("sp", VEC[:], vecs_in[:, :], "inv")
        ldk = P.dma("sp", CST[:], cst_in[:, :], "ink")
        ldp = P.dma("sp", PC[:], pc_in[:, :], "inp")
        ldc = P.dma("sp", CAf[:], cond_in[:, :], "inc")
        for e in ("act", "dve", "pool", "pe"):
            for dd in (ldc, ldv, ldk, ldp):
                P.wait_dma(e, dd)
        ones_ready = es.enter_context(nc.semaphore("ones"))
        nc.gpsimd.memset(EPSB[:], EPS)
        nc.gpsimd.memset(P.scr[:], 0.0)
        nc.gpsimd.memset(ONES[:], 1.0).then_inc(ones_ready, 1)
        for e in ("pe", "act", "dve"):
            P.wait(e, ones_ready, 1)
        ins = nc.scalar.activation(out=CA[:], in_=CAf[:], func=ACT.Silu)
        t_ca = P.mark("act", ins)

        def ada_gen(l, bank=None):
            if bank is None:
                b = P.bank()
            else:
                b = bank
                P.reserved.add(b)
            tok = None
            for i in range(144):
                t = P.load_tile(w_ada[l * 144 + i])
                ap = P.tile_ap(t).rearrange("p (k n) -> p k n", k=16)
                tok = P.group(P.ps[:, b, i:i + 1], [(ap[:, kc, :], CA[:, kc:kc + 1]) for kc in range(16)],
                              [b] if i == 0 else [], tiles=[t], waits=[t_ca])
                P.release(t)
                yield None
            P.wait_tok("dve", tok)
            ab = VOFF["ada_b"] + l * 144
            nc.vector.tensor_tensor(out=MOD[:, l, :], in0=P.ps[:, b, 0:144], in1=VEC[:, ab:ab + 144], op=ALU.add)
            for s in range(3):
                sh = MOD[:, l, (3 * s) * 16:(3 * s) * 16 + 16]
                sc = MOD[:, l, (3 * s + 1) * 16:(3 * s + 1) * 16 + 16]
                g = MOD[:, l, (3 * s + 2) * 16:(3 * s + 2) * 16 + 16]
                ng = vcol("norm_g", l * 3 + s)
                nc.vector.scalar_tensor_tensor(out=MV[:, l, 3 * s, :], in0=sc, scalar=1.0, in1=ng, op0=ALU.add, op1=ALU.mult)
                nc.vector.tensor_copy(out=MV[:, l, 3 * s + 1, :], in_=sh)
                ins = nc.vector.tensor_scalar(out=MV[:, l, 3 * s + 2, :], in0=g, scalar1=(1.0 if s == 1 else 0.5), scalar2=None, op0=ALU.mult)
            tk = P.mark("dve", ins)
            P.bank_free[b] = tk
            P.reserved.discard(b)
            ada_tok[l] = tk
            yield tk

        ada_tok = {}

        def ada(l):
            for _ in ada_gen(l):
                pass
            return ada_tok[l]

        def norm(A, sh, mv_tok, final=False):
            AR.reset()
            RSs = [AR.f32(TB), AR.f32(TB)]
            TMP = AR.f32(TB)
            P.barrier()
            prev = None
            for tb in range(NTB):
                ts = slice(tb * TB, (tb + 1) * TB)
                RS = RSs[tb]
                for q in range(4):
                    P.wait_dma("act", ld[q])
                    P.wait_dma("dve", ld[q])
                for c in range(NCH):
                    ins = nc.scalar.activation(out=HM[:, c, ts], in_=XT[:, c, ts], func=ACT.Square)
                tS = P.mark("act", ins)
                b = P.bank()
                tP = P.group(P.ps[:, b, :], [(ONES[:, :], HM[:, c, ts]) for c in range(NCH)], [b], waits=[tS])
                P.wait_tok("act", tP)
                ins = nc.scalar.activation(out=RS, in_=P.ps[:, b, :], func=ACT.Sqrt, scale=1.0 / D, bias=EPSB[:, 0:1])
                tR = P.mark("act", ins)
                P.bank_free[b] = tR
                P.wait_tok("dve", tR)
                P.wait_tok("dve", mv_tok)
                nc.vector.reciprocal(out=RS, in_=RS)
                for c in range(NCH):
                    if final:
                        ins = nc.vector.scalar_tensor_tensor(out=XT[:, c, ts], in0=XT[:, c, ts], scalar=A[:, c:c + 1], in1=RS, op0=ALU.mult, op1=ALU.mult)
                    else:
                        nc.vector.scalar_tensor_tensor(out=TMP, in0=XT[:, c, ts], scalar=A[:, c:c + 1], in1=RS, op0=ALU.mult, op1=ALU.mult)
                        ins = nc.vector.tensor_scalar(out=HM[:, c, ts], in0=TMP, scalar1=sh[:, c:c + 1], scalar2=None, op0=ALU.add)
                prev = P.mark("dve", ins)
            P.barrier()
            return prev

        def ffn(f, G, hm_tok):
            AR.reset()
            PN = 8
            H = AR.bf16(PN, NT)
            lastB = None
            for p0 in range(0, 44, PN):
                npan = min(PN, 44 - p0)
                tokD = None
                for j in range(npan):
                    hc = p0 + j
                    ta = P.load_tile(w_in[f * 88 + 2 * hc])
                    tb_ = P.load_tile(w_in[f * 88 + 2 * hc + 1])
                    apa = P.tile_ap(ta).rearrange("p (k n) -> p k n", k=16)
                    apb = P.tile_ap(tb_).rearrange("p (k n) -> p k n", k=16)
                    for tb in range(NTB):
                        ts = slice(tb * TB, (tb + 1) * TB)
                        ba, bb = P.bank_pair()
                        P.group(P.ps[:, ba, :], [(apa[:, kc, :], HM[:, kc, ts]) for kc in range(16)], [ba], tiles=[ta], waits=[hm_tok])
                        tpb = P.group(P.ps[:, bb, :], [(apb[:, kc, :], HM[:, kc, ts]) for kc in range(16)], [bb], tiles=[tb_])
                        P.wait_tok("act", tpb)
                        P.wait_tok("act", lastB)
                        ins = nc.scalar.activation(out=H[:, j, ts], in_=P.ps[:, ba, :], func=ACT.Silu)
                        tA = P.mark("act", ins)
                        P.wait_tok("dve", tA)
                        ins = nc.vector.tensor_tensor(out=H[:, j, ts], in0=H[:, j, ts], in1=P.ps[:, bb, :], op=ALU.mult)
                        tokD = P.mark("dve", ins)
                        P.bank_free[ba] = tokD
                        P.bank_free[bb] = tokD
                    P.release(ta)
                    P.release(tb_)
                tiles = [P.load_tile(w_out[f * 44 + p0 + j]) for j in range(npan)]
                aps = [P.tile_ap(t) for t in tiles]
                for oc in range(NCH):
                    for tb in range(NTB):
                        ts = slice(tb * TB, (tb + 1) * TB)
                        b = P.bank()
                        pairs = [(aps[j][:, oc * 128:(oc + 1) * 128], H[:, j, ts]) for j in range(npan)]
                        lastB = P.group(P.ps[:, b, :], pairs, [b], tiles=tiles, waits=[tokD])
                        P.wait_tok("dve", lastB)
                        ins = nc.vector.scalar_tensor_tensor(out=XT[:, oc, ts], in0=P.ps[:, b, :], scalar=G[:, oc:oc + 1], in1=XT[:, oc, ts], op0=ALU.mult, op1=ALU.add)
                        P.bank_free[b] = P.mark("dve", ins)
                for t in tiles:
                    P.release(t)

        def lru_prep():
            lam = VEC[:, VOFF["lru_lam"]:VOFF["lru_lam"] + 32]
            t = NSP[:, 0, :, :].rearrange("p d c -> p (d c)")
            t2 = NSP[:, 1, :, :].rearrange("p d c -> p (d c)")
            P.wait_dma("act", ldv)
            nc.scalar.activation(out=t, in_=lam, func=ACT.Exp, scale=-1.0)
            ins = nc.scalar.activation(out=t, in_=t, func=ACT.Ln, bias=1.0)
            tk = P.mark("act", ins)
            P.wait_tok("dve", tk)
            nc.vector.tensor_scalar(out=t2, in0=t, scalar1=-16.0, scalar2=None, op0=ALU.mult)
            nc.vector.tensor_scalar(out=t, in0=t, scalar1=-8.0, scalar2=None, op0=ALU.mult)

        def odd_mixer(l, G2):
            P.barrier()
            AR.reset()
            XC = AR.f32(2, NT)
            HS = AR.f32(2, NT)
            XCb = AR.bf16(2, NT)
            Y = AR.bf16(2, NT)
            T12 = AR.f32(2, 2 * SL)
            T1 = T12[:, 0, :]
            T2 = T12[:, 1, :]
            T3 = AR.f32(2 * SL)
            T12b = AR.f32(2, 2 * SL)
            T3b = AR.f32(2 * SL)
            HSB = AR.f32(2, NT)
            T4 = AR.f32(2, SL)
            XF = AR.f32(2, SL)
            XFb = AR.bf16(2, SL)
            TT = AR.f32(128)
            HIN = AR.f32(4)
            PEB = AR.f32(2)
            IDENT = ccol("ident", 128)
            JREV = ccol("jrev", 128)
            CVM = pcol("cvm", 3 * SL).rearrange("p (j t) -> p j t", j=3)
            CARRY = pcol("carry", 1)
            LH0 = pcol("lh0", 2 * NCH).rearrange("p (d c) -> p d c", d=2)
            cw = VEC[:, VOFF["conv_w"]:VOFF["conv_w"] + 64].rearrange("p (j c) -> p j c", j=4)
            cb = vcol("conv_b")
            nba = VEC[:, VOFF["lru_ba"]:VOFF["lru_ba"] + 32].rearrange("p (d c) -> p d c", d=2)
            nbx = VEC[:, VOFF["lru_bx"]:VOFF["lru_bx"] + 32].rearrange("p (d c) -> p d c", d=2)
            NB = AR.f32(2, 2, NCH)
            nc.vector.tensor_scalar(out=NB[:, 0, :, :], in0=nba, scalar1=-1.0, scalar2=None, op0=ALU.mult)
            nc.vector.tensor_scalar(out=NB[:, 1, :, :], in0=nbx, scalar1=-1.0, scalar2=None, op0=ALU.mult)
            P.barrier()
            st = {"prev": None}

            def lru_pair(TS, key, apl, tl, d, h, src_f32, src_bf, out_fn, hin_fn, wtok, pre_scan=None):
                T12_, T3_ = TS
                T1_ = T12_[:, 0, :]
                T2_ = T12_[:, 1, :]
                br, bi = P.bank_pair()
                tpi = None
                for oc in range(2):
                    ocs = slice(oc * 128, (oc + 1) * 128)
                    cs = slice(oc * SL, (oc + 1) * SL)
                    P.group(P.ps[:, br, cs], [(apl[:, 0, d, kc, ocs], src_bf(kc)) for kc in range(2)], [br] if oc == 0 else [], tiles=[tl], waits=[wtok])
                    tpi = P.group(P.ps[:, bi, cs], [(apl[:, 1, d, kc, ocs], src_bf(kc)) for kc in range(2)], [bi] if oc == 0 else [], tiles=[tl])
                yield
                P.wait_tok("act", tpi)
                P.wait_tok("act", st.get(key))
                for oc in range(2):
                    ch = 2 * h + oc
                    cs = slice(oc * SL, (oc + 1) * SL)
                    nc.scalar.activation(out=T1_[:, cs], in_=P.ps[:, br, cs], func=ACT.Exp, scale=-1.0, bias=NB[:, 0, d, ch:ch + 1])
                    ins = nc.scalar.activation(out=T2_[:, cs], in_=P.ps[:, bi, cs], func=ACT.Exp, scale=-1.0, bias=NB[:, 1, d, ch:ch + 1])
                tA = P.mark("act", ins)
                P.bank_free[br] = tA
                P.bank_free[bi] = tA
                yield
                P.wait_tok("dve", tA)
                T12f = T12_[:].rearrange("p a b -> p (a b)")
                nc.vector.tensor_scalar(out=T12f, in0=T12f, scalar1=1.0, scalar2=None, op0=ALU.add)
                tD = P.mark("dve", nc.vector.reciprocal(out=T12f, in_=T12f))
                yield
                P.wait_tok("act", tD)
                for oc in range(2):
                    ch = 2 * h + oc
                    cs = slice(oc * SL, (oc + 1) * SL)
                    nc.scalar.activation(out=T3_[:, cs], in_=T1_[:, cs], func=ACT.Exp, scale=NSP[:, 1, d, ch:ch + 1])
                    ins = nc.scalar.activation(out=T1_[:, cs], in_=T1_[:, cs], func=ACT.Exp, scale=NSP[:, 0, d, ch:ch + 1])
                tA2 = P.mark("act", ins)
                yield
                P.wait_tok("dve", tA2)
                nc.vector.tensor_scalar(out=T3_, in0=T3_, scalar1=-1.0, scalar2=1.0, op0=ALU.mult, op1=ALU.add)
                nc.vector.tensor_scalar(out=T3_, in0=T3_, scalar1=1e-30, scalar2=None, op0=ALU.max)
                T2v = T2_.rearrange("p (a b) -> p a b", a=2)
                tD2 = P.mark("dve", nc.vector.tensor_tensor(out=T2v, in0=T2v, in1=src_f32, op=ALU.mult))
                yield
                P.wait_tok("act", tD2)
                nc.scalar.activation(out=T3_, in_=T3_, func=ACT.Ln)
                tA3 = P.mark("act", nc.scalar.activation(out=T3_, in_=T3_, func=ACT.Exp, scale=0.5))
                yield
                P.wait_tok("dve", tA3)
                P.wait_tok("dve", pre_scan)
                nc.vector.tensor_tensor(out=T2_, in0=T2_, in1=T3_, op=ALU.mult)
                ins = None
                for oc in range(2):
                    cs = slice(oc * SL, (oc + 1) * SL)
                    ins = nc.vector.tensor_tensor_scan(out=out_fn(oc), data0=T1_[:, cs], data1=T2_[:, cs], initial=hin_fn(oc), op0=ALU.mult, op1=ALU.add)
                st[key] = P.mark("dve", ins)
                P.fence("dve", st[key])

            def flip_block(src128, waits):
                b = P.bank()
                tp = P.transpose(P.ps[:, b, 0:128], src128, IDENT, [b], waits=waits)
                P.wait_tok("act", tp)
                P.wait_tok("act", st.get("tt"))
                ins = nc.scalar.activation(out=TT, in_=P.ps[:, b, 0:128], func=ACT.Copy)
                ta = P.mark("act", ins)
                P.bank_free[b] = ta
                b2 = P.bank()
                tp2 = P.group(P.ps[:, b2, 0:128], [(TT, JREV)], [b2], waits=[ta])
                st["tt"] = tp2
                return b2, tp2

            for h in range(8):
                tx = [P.load_tile(w_odin[16 + 2 * h + cc]) for cc in range(2)]
                tl = P.load_tile(w_lru[h])
                tg = [P.load_tile(w_odin[2 * h + cc]) for cc in range(2)]
                to = [P.load_tile(w_odout[2 * h])]
                apx = [P.tile_ap(t).rearrange("p (k n) -> p k n", k=16) for t in tx]
                apg = [P.tile_ap(t).rearrange("p (k n) -> p k n", k=16) for t in tg]
                apl = P.tile_ap(tl).rearrange("p (w d k n) -> p w d k n", w=2, d=2, k=2)
                xc_tok = None
                for s in range(NSL):
                    ts = slice(s * SL, (s + 1) * SL)
                    for cc in range(2):
                        ch = 2 * h + cc
                        b = P.bank()
                        tp = P.group(P.ps[:, b, 0:SL], [(apx[cc][:, kc, :], HM[:, kc, ts]) for kc in range(16)], [b], tiles=[tx[cc]])
                        P.wait_tok("dve", tp)
                        xs = P.ps[:, b, 0:SL]
                        xc = XC[:, cc, ts]
                        nc.vector.tensor_scalar(out=xc, in0=xs, scalar1=cw[:, 2, ch:ch + 1], scalar2=cb[:, ch:ch + 1], op0=ALU.mult, op1=ALU.add)
                        nc.vector.tensor_tensor(out=T1[:, 2:SL], in0=xs[:, 0:SL - 2], in1=CVM[:, 0, 2:SL], op=ALU.mult)
                        nc.vector.scalar_tensor_tensor(out=xc[:, 2:SL], in0=T1[:, 2:SL], scalar=cw[:, 0, ch:ch + 1], in1=xc[:, 2:SL], op0=ALU.mult, op1=ALU.add)
                        nc.vector.tensor_tensor(out=T1[:, 1:SL], in0=xs[:, 0:SL - 1], in1=CVM[:, 1, 1:SL], op=ALU.mult)
                        nc.vector.scalar_tensor_tensor(out=xc[:, 1:SL], in0=T1[:, 1:SL], scalar=cw[:, 1, ch:ch + 1], in1=xc[:, 1:SL], op0=ALU.mult, op1=ALU.add)
                        nc.vector.tensor_tensor(out=T1[:, 0:SL - 1], in0=xs[:, 1:SL], in1=CVM[:, 2, 0:SL - 1], op=ALU.mult)
                        ins = nc.vector.scalar_tensor_tensor(out=xc[:, 0:SL - 1], in0=T1[:, 0:SL - 1], scalar=cw[:, 3, ch:ch + 1], in1=xc[:, 0:SL - 1], op0=ALU.mult, op1=ALU.add)
                        P.bank_free[b] = P.mark("dve", ins)
                        ins = nc.vector.tensor_copy(out=XCb[:, cc, ts], in_=xc)
                        xc_tok = P.mark("dve", ins)
                P.release(tx[0]); P.release(tx[1])
                if h == 0:
                    dump(XC[:].rearrange("p a t -> p (a t)"))
                to.append(P.load_tile(w_odout[2 * h + 1]))
                apo = [P.tile_ap(t) for t in to]
                P.barrier()
                def fwd_task():
                    for s in range(NSL):
                        ts = slice(s * SL, (s + 1) * SL)
                        for oc in range(2):
                            ch = 2 * h + oc
                            if s == 0:
                                nc.vector.tensor_copy(out=HIN[:, oc:oc + 1], in_=LH0[:, 0, ch:ch + 1])
                            else:
                                nc.vector.tensor_scalar(out=HIN[:, oc:oc + 1], in0=HS[:, oc, s * SL - 1:s * SL], scalar1=CARRY, scalar2=None, op0=ALU.mult)
                        yield from lru_pair((T12, T3), "pf", apl, tl, 0, h, XC[:, :, ts], lambda kc: XCb[:, kc, ts], lambda oc: HS[:, oc, ts], lambda oc: HIN[:, oc:oc + 1], xc_tok)
                        for oc in range(2):
                            ch = 2 * h + oc
                            nc.vector.tensor_copy(out=LRUST[:, s, 0, ch:ch + 1], in_=HS[:, oc, (s + 1) * SL - 1:(s + 1) * SL])
                        yield

                def bwd_task():
                    unflip_tok = None
                    for s in range(NSL - 1, -1, -1):
                        ftok = None
                        for cc in range(2):
                            for hb in range(2):
                                b2, tp2 = flip_block(XC[:, cc, s * SL + hb * 128:s * SL + (hb + 1) * 128], [xc_tok])
                                yield
                                P.wait_tok("act", tp2)
                                P.wait_tok("act", st.get("pb"))
                                dst = slice((1 - hb) * 128, (2 - hb) * 128)
                                nc.scalar.activation(out=XF[:, cc, dst], in_=P.ps[:, b2, 0:128], func=ACT.Copy)
                                ins = nc.scalar.activation(out=XFb[:, cc, dst], in_=P.ps[:, b2, 0:128], func=ACT.Copy)
                                ftok = P.mark("act", ins)
                                P.bank_free[b2] = ftok
                        yield
                        P.wait_tok("dve", ftok)
                        for oc in range(2):
                            ch = 2 * h + oc
                            if s == NSL - 1:
                                nc.vector.tensor_copy(out=HIN[:, 2 + oc:3 + oc], in_=LH0[:, 1, ch:ch + 1])
                            else:
                                nc.vector.tensor_scalar(out=HIN[:, 2 + oc:3 + oc], in0=PEB[:, oc:oc + 1], scalar1=CARRY, scalar2=None, op0=ALU.mult)
                        yield from lru_pair((T12b, T3b), "pb", apl, tl, 1, h, XF[:, :, :], lambda kc: XFb[:, kc, :], lambda oc: T4[:, oc, :], lambda oc: HIN[:, 2 + oc:3 + oc], ftok, pre_scan=unflip_tok)
                        tk = None
                        for oc in range(2):
                            ch = 2 * h + oc
                            nc.vector.tensor_copy(out=PEB[:, oc:oc + 1], in_=T4[:, oc, SL - 1:SL])
                            tk = P.mark("dve", nc.vector.tensor_copy(out=LRUST[:, s, 1, ch:ch + 1], in_=T4[:, oc, SL - 1:SL]))
                        for oc in range(2):
                            for hb in range(2):
                                b2, tp2 = flip_block(T4[:, oc, hb * 128:(hb + 1) * 128], [tk])
                                unflip_tok = tp2
                                yield
                                P.wait_tok("act", tp2)
                                dst = slice(s * SL + (1 - hb) * 128, s * SL + (2 - hb) * 128)
                                P.bank_free[b2] = P.mark("act", nc.scalar.activation(out=HSB[:, oc, dst], in_=P.ps[:, b2, 0:128], func=ACT.Copy))
                        yield

                st["pf"] = None
                st["pb"] = None
                tasks = [fwd_task(), bwd_task()]
                while tasks:
                    for tsk in list(tasks):
                        try:
                            next(tsk)
                        except StopIteration:
                            tasks.remove(tsk)
                P.release(tl)
                P.barrier()
                if h == 0:
                    dump(HS[:].rearrange("p a t -> p (a t)"))
                ytok = None
                T1g = T1[:, 0:SL]
                for s in range(NSL):
                    ts = slice(s * SL, (s + 1) * SL)
                    for cc in range(2):
                        b = P.bank()
                        tp = P.group(P.ps[:, b, 0:SL], [(apg[cc][:, kc, :], HM[:, kc, ts]) for kc in range(16)], [b], tiles=[tg[cc]])
                        gp = P.ps[:, b, 0:SL]
                        P.wait_tok("act", tp)
                        P.wait_tok("act", ytok)
                        ins = nc.scalar.activation(out=T1g, in_=gp, func=ACT.Square)
                        tA = P.mark("act", ins)
                        P.wait_tok("dve", tA)
                        nc.vector.tensor_scalar(out=T1g, in0=T1g, scalar1=0.044715, scalar2=1.0, op0=ALU.mult, op1=ALU.add)
                        ins = nc.vector.tensor_tensor(out=T1g, in0=T1g, in1=gp, op=ALU.mult)
                        tD = P.mark("dve", ins)
                        P.wait_tok("act", tD)
                        ins = nc.scalar.activation(out=T1g, in_=T1g, func=ACT.Exp, scale=-GELU_C)
                        tA2 = P.mark("act", ins)
                        P.wait_tok("dve", tA2)
                        nc.vector.tensor_scalar(out=T1g, in0=T1g, scalar1=1.0, scalar2=None, op0=ALU.add)
                        nc.vector.reciprocal(out=T1g, in_=T1g)
                        nc.vector.tensor_tensor(out=T1g, in0=T1g, in1=gp, op=ALU.mult)
                        nc.vector.tensor_tensor(out=HS[:, cc, ts], in0=HS[:, cc, ts], in1=HSB[:, cc, ts], op=ALU.add)
                        ins = nc.vector.tensor_tensor(out=Y[:, cc, ts], in0=T1g, in1=HS[:, cc, ts], op=ALU.mult)
                        ytok = P.mark("dve", ins)
                        P.bank_free[b] = ytok
                P.release(tg[0]); P.release(tg[1])
                for oc in range(NCH):
                    for tb in range(NTB):
                        ts = slice(tb * TB, (tb + 1) * TB)
                        b = P.bank()
                        tp = P.group(P.ps[:, b, :], [(apo[kc][:, oc * 128:(oc + 1) * 128], Y[:, kc, ts]) for kc in range(2)], [b], tiles=to, waits=[ytok])
                        P.wait_tok("dve", tp)
                        ins = nc.vector.scalar_tensor_tensor(out=XT[:, oc, ts], in0=P.ps[:, b, :], scalar=G2[:, oc:oc + 1], in1=XT[:, oc, ts], op0=ALU.mult, op1=ALU.add)
                        P.bank_free[b] = P.mark("dve", ins)
                P.release(to[0]); P.release(to[1])
                P.barrier()

        def gla_part(G2, side=None):
            P.barrier()
            AR.reset()
            IDENT = ccol("ident", 128)
            MASK = [ccol("maskf", 128), ccol("maskb", 128)]
            RM = ccol("rm", SL)
            CARRY = pcol("carry", 1)
            gng = vcol("gla_ng", 0, 2)
            gb = vcol("gla_gb", 0, 8)
            GL = AR.bf16(2, NT)
            GW2 = AR.bf16(2, 512)
            NGB = AR.f32(8)
            OS = AR.f32(2, NT)
            GS0 = AR.f32(2, SL)
            mark0 = AR.off
            QT = AR.f32(SL); KT = AR.f32(SL); VT = AR.bf16(2, SL)
            LA = AR.f32(SL); PP = AR.f32(SL); TP = AR.f32(SL); PB = AR.f32(SL)
            E1 = AR.f32(SL); E2 = AR.f32(SL); E3 = AR.f32(SL)
            QTd = AR.bf16(SL); KTd = AR.bf16(SL); KEd = AR.f32(SL)
            AT = AR.bf16(2, 128); KET = AR.bf16(2, 128)
            S = AR.f32(SL); Sb = AR.bf16(4, SL); SST = AR.f32(SL); DEC = AR.f32(4)
            end1 = AR.off
            AR.off = mark0
            GG = AR.bf16(2, NT); SQ = AR.bf16(2, TB); RR = AR.f32(TB); T5 = AR.f32(TB); EG = AR.f32(TB)
            assert AR.off <= ARENA_F32 and end1 <= ARENA_F32
            P.pool_sync()
            dgw = P.dma("pool", GW2[0:16, :, :].rearrange("p a b -> p (a b)"), gw2_in[:, :], "gw2")
            P.wait_dma("pe", dgw)
            nc.vector.tensor_scalar(out=NGB, in0=gb, scalar1=-1.0, scalar2=None, op0=ALU.mult)
            tglr = P.load_tile(w_evin[32])
            apglr = P.tile_ap(tglr).rearrange("p (k n) -> p k n", k=16)
            gl_tok = None
            for d in range(2):
                for tb in range(NTB):
                    ts = slice(tb * TB, (tb + 1) * TB)
                    b = P.bank()
                    tp = P.group(P.ps[0:16, b, :], [(apglr[:, kc, d * 16:(d + 1) * 16], HM[:, kc, ts]) for kc in range(16)], [b], tiles=[tglr])
                    P.wait_tok("act", tp)
                    ins = nc.scalar.activation(out=GL[0:16, d, ts], in_=P.ps[0:16, b, :], func=ACT.Copy)
                    gl_tok = P.mark("act", ins)
                    P.bank_free[b] = gl_tok
            P.release(tglr)
            sst_dma = None
            import os
            STOP = int(os.environ.get("GLA_STOP", "99"))
            for hd in range(4 if STOP > 1 else 0):
                P.barrier()
                tq = P.load_tile(w_evin[8 + hd])
                tk = P.load_tile(w_evin[12 + hd])
                tv = [P.load_tile(w_evin[16 + 2 * hd + i]) for i in range(2)]
                apq = P.tile_ap(tq).rearrange("p (k n) -> p k n", k=16)
                apk = P.tile_ap(tk).rearrange("p (k n) -> p k n", k=16)
                apv = [P.tile_ap(t).rearrange("p (k n) -> p k n", k=16) for t in tv]
                P.sp_sync()
                dgs = P.dma("sp", GS0[:].rearrange("p a b -> p (a b)"), gs0_in[hd], "gs0")
                for d in range(2 if STOP > 2 else 0):
                    order = list(range(NSL)) if d == 0 else list(range(NSL - 1, -1, -1))
                    for si, s in enumerate(order):
                        ts = slice(s * SL, (s + 1) * SL)
                        P.barrier()
                        if side is not None:
                            for _ in range(5):
                                next(side, None)
                        b = P.bank()
                        tp = P.group(P.ps[:, b, 0:SL], [(apq[:, kc, :], HM[:, kc, ts]) for kc in range(16)], [b], tiles=[tq])
                        P.wait_tok("act", tp)
                        P.bank_free[b] = P.mark("act", nc.scalar.activation(out=QT, in_=P.ps[:, b, 0:SL], func=ACT.Copy, scale=128.0 ** -0.5))
                        b = P.bank()
                        tp = P.group(P.ps[:, b, 0:SL], [(apk[:, kc, :], HM[:, kc, ts]) for kc in range(16)], [b], tiles=[tk])
                        P.wait_tok("act", tp)
                        P.bank_free[b] = P.mark("act", nc.scalar.activation(out=KT, in_=P.ps[:, b, 0:SL], func=ACT.Copy))
                        vt_tok = None
                        for tt in range(2):
                            b = P.bank()
                            tsl = slice(s * SL + tt * 128, s * SL + (tt + 1) * 128)
                            for vc in range(2):
                                tp = P.group(P.ps[:, b, vc * 128:(vc + 1) * 128], [(HM[:, kc, tsl], apv[vc][:, kc, :]) for kc in range(16)],
                                             [b] if vc == 0 else [], tiles=[tv[vc]])
                            P.wait_tok("act", tp)
                            vt_tok = P.mark("act", nc.scalar.activation(out=VT[:, tt, :], in_=P.ps[:, b, 0:SL], func=ACT.Copy))
                            P.bank_free[b] = vt_tok
                        if STOP <= 3:
                            continue
                        b = P.bank()
                        tp = P.group(P.ps[:, b, 0:SL], [(GW2[0:16, d, hd * 128:(hd + 1) * 128], GL[0:16, d, ts])], [b], waits=[gl_tok])
                        P.wait_tok("act", tp)
                        nc.scalar.activation(out=E1, in_=P.ps[:, b, 0:SL], func=ACT.Exp, scale=-1.0, bias=NGB[:, d * 4 + hd:d * 4 + hd + 1])
                        tA = P.mark("act", nc.scalar.activation(out=LA, in_=E1, func=ACT.Ln, bias=1.0))
                        P.bank_free[b] = tA
                        P.wait_tok("dve", tA)
                        tsc = P.mark("dve", nc.vector.tensor_tensor_scan(out=PP, data0=RM, data1=LA, initial=0.0, op0=ALU.mult, op1=ALU.add))
                        P.fence("dve", tsc)
                        TOT = PP[:, 63::64]
                        nc.vector.tensor_tensor(out=TP.rearrange("p (c t) -> p c t", c=4), in0=PP.rearrange("p (c t) -> p c t", c=4),
                                                in1=TOT.unsqueeze(2).to_broadcast([128, 4, 64]), op=ALU.subtract)
                        if d == 0:
                            ins = nc.vector.tensor_copy(out=PB, in_=PP)
                            sc = (-1.0 / 16, 1.0 / 16, 1.0 / 16)
                        else:
                            nc.vector.scalar_tensor_tensor(out=PB, in0=TP, scalar=-1.0, in1=LA, op0=ALU.mult, op1=ALU.add)
                            ins = nc.vector.tensor_tensor(out=TP, in0=PP, in1=LA, op=ALU.subtract)
                            sc = (-1.0 / 16, 1.0 / 16, -1.0 / 16)
                        tD = P.mark("dve", ins)
                        P.wait_tok("act", tD)
                        nc.scalar.activation(out=E1, in_=PB, func=ACT.Exp, scale=sc[0])
                        nc.scalar.activation(out=E2, in_=PB, func=ACT.Exp, scale=sc[1])
                        nc.scalar.activation(out=E3, in_=TP, func=ACT.Exp, scale=sc[2])
                        tA2 = P.mark("act", nc.scalar.activation(out=DEC, in_=TOT, func=ACT.Exp, scale=-1.0 / 16))
                        P.wait_tok("dve", tA2)
                        nc.vector.tensor_tensor(out=QTd, in0=QT, in1=E1, op=ALU.mult)
                        nc.vector.tensor_tensor(out=KTd, in0=KT, in1=E2, op=ALU.mult)
                        tD2 = P.mark("dve", nc.vector.tensor_tensor(out=KEd, in0=KT, in1=E3, op=ALU.mult))
                        if STOP <= 4:
                            continue
                        ba = P.bank()
                        for pp in range(2):
                            tl_ = slice(pp * 128, (pp + 1) * 128)
                            tp = P.group(P.ps[:, ba, tl_], [(KTd[:, tl_], QTd[:, tl_])], [ba] if pp == 0 else [], waits=[tD2])
                        P.wait_tok("dve", tp)
                        for pp in range(2):
                            ins = nc.vector.tensor_tensor(out=AT[:, pp, :], in0=P.ps[:, ba, pp * 128:(pp + 1) * 128], in1=MASK[d], op=ALU.mult)
                        tAT = P.mark("dve", ins)
                        P.bank_free[ba] = tAT
                        if STOP == 5 and os.environ.get("GLA_SUB") == "a":
                            continue
                        for pp in range(2):
                            bt = P.bank()
                            tp = P.transpose(P.ps[:, bt, 0:128], KEd[:, pp * 128:(pp + 1) * 128], IDENT, [bt], waits=[tD2])
                            P.wait_tok("act", tp)
                            tKET = P.mark("act", nc.scalar.activation(out=KET[:, pp, :], in_=P.ps[:, bt, 0:128], func=ACT.Copy))
                            P.bank_free[bt] = tKET
                        if STOP <= 5:
                            continue
                        bk = [P.bank() for _ in range(4)]
                        for n in range(4):
                            pp, half = n // 2, n % 2
                            rows = slice(half * 64, half * 64 + 64)
                            tp = P.group(P.ps[:, bk[n], 0:256], [(KET[rows, pp, :], VT[rows, pp, :])],
                                         [bk[n]], waits=[tKET, vt_tok])
                        P.wait_tok("dve", tp)
                        P.wait_dma("dve", dgs)
                        if si == 0:
                            nc.vector.tensor_copy(out=S, in_=GS0[:, d, :])
                        else:
                            nc.vector.tensor_scalar(out=S, in0=S, scalar1=CARRY, scalar2=None, op0=ALU.mult)
                        corder = [0, 1, 2, 3] if d == 0 else [3, 2, 1, 0]
                        for n in corder:
                            pp, half = n // 2, n % 2
                            nc.vector.tensor_copy(out=Sb[:, n, :], in_=S)
                            ins = nc.vector.scalar_tensor_tensor(out=S, in0=S, scalar=DEC[:, n:n + 1], in1=P.ps[:, bk[n], 0:256], op0=ALU.mult, op1=ALU.add)
                        tS = P.mark("dve", ins)
                        for n in range(4):
                            P.bank_free[bk[n]] = tS
                        if sst_dma is not None:
                            P.wait_dma("dve", sst_dma)
                        tSS = P.mark("dve", nc.vector.tensor_copy(out=SST, in_=S))
                        P.wait_tok("sp", tSS)
                        sst_dma = P.dma("sp", gla_out[(s * 2 + d) * 4 + hd], SST, "sst")
                        if STOP <= 6:
                            continue
                        bo = P.bank()
                        first = True
                        for pp in range(2):
                            for vc in range(2):
                                col0 = (pp * 2 + vc) * 128
                                vs = slice(vc * 128, (vc + 1) * 128)
                                items = [(P.ps[:, bo, col0:col0 + 128], VT[:, pp, vs], AT[:, pp, :])]
                                for half in range(2):
                                    n = pp * 2 + half
                                    items.append((P.ps[:, bo, col0 + half * 64:col0 + (half + 1) * 64], Sb[:, n, vs], QTd[:, n * 64:(n + 1) * 64]))
                                tp = P.group2(items, [bo] if first else [], waits=[tAT, tS, vt_tok])
                                first = False
                        P.wait_tok("dve", tp)
                        for pp in range(2):
                            for vc in range(2):
                                col0 = (pp * 2 + vc) * 128
                                dst = OS[:, vc, s * SL + pp * 128:s * SL + (pp + 1) * 128]
                                if d == 0:
                                    ins = nc.vector.tensor_copy(out=dst, in_=P.ps[:, bo, col0:col0 + 128])
                                else:
                                    ins = nc.vector.tensor_tensor(out=dst, in0=dst, in1=P.ps[:, bo, col0:col0 + 128], op=ALU.add)
                        P.bank_free[bo] = P.mark("dve", ins)
                for t in (tq, tk, tv[0], tv[1]):
                    P.release(t)
                P.barrier()
                tg = [P.load_tile(w_evin[24 + 2 * hd + i]) for i in range(2)]
                to = [P.load_tile(w_evout[8 + 2 * hd + i]) for i in range(2)]
                apg = [P.tile_ap(t).rearrange("p (k n) -> p k n", k=16) for t in tg]
                apo = [P.tile_ap(t) for t in to]
                ogt = None
                for tb in range(NTB):
                    ts = slice(tb * TB, (tb + 1) * TB)
                    for vc in range(2):
                        b = P.bank()
                        tp = P.group(P.ps[:, b, :], [(apg[vc][:, kc, :], HM[:, kc, ts]) for kc in range(16)], [b], tiles=[tg[vc]])
                        P.wait_tok("act", tp)
                        P.wait_tok("act", ogt)
                        tA = P.mark("act", nc.scalar.activation(out=EG, in_=P.ps[:, b, :], func=ACT.Exp, scale=-1.0))
                        P.wait_tok("dve", tA)
                        nc.vector.tensor_scalar(out=EG, in0=EG, scalar1=1.0, scalar2=None, op0=ALU.add)
                        nc.vector.reciprocal(out=EG, in_=EG)
                        ins = nc.vector.tensor_tensor(out=GG[:, vc, ts], in0=EG, in1=P.ps[:, b, :], op=ALU.mult)
                        ogt = P.mark("dve", ins)
                        P.bank_free[b] = ogt
                    P.wait_tok("act", ogt)
                    for vc in range(2):
                        ins = nc.scalar.activation(out=SQ[:, vc, :], in_=OS[:, vc, ts], func=ACT.Square)
                    tSq = P.mark("act", ins)
                    b = P.bank()
                    tp = P.group(P.ps[:, b, :], [(ONES[:, :], SQ[:, vc, :]) for vc in range(2)], [b], waits=[tSq])
                    P.wait_tok("act", tp)
                    nc.scalar.activation(out=RR, in_=P.ps[:, b, :], func=ACT.Ln, scale=1.0 / 256, bias=EPSB[:, 0:1])
                    tR = P.mark("act", nc.scalar.activation(out=RR, in_=RR, func=ACT.Exp, scale=-0.5))
                    P.bank_free[b] = tR
                    P.wait_tok("dve", tR)
                    for vc in range(2):
                        nc.vector.scalar_tensor_tensor(out=T5, in0=OS[:, vc, ts], scalar=gng[:, vc:vc + 1], in1=RR, op0=ALU.mult, op1=ALU.mult)
                        ins = nc.vector.tensor_tensor(out=GG[:, vc, ts], in0=T5, in1=GG[:, vc, ts], op=ALU.mult)
                    ogt = P.mark("dve", ins)
                P.release(tg[0]); P.release(tg[1])
                for oc in range(NCH):
                    for tb in range(NTB):
                        ts = slice(tb * TB, (tb + 1) * TB)
                        b = P.bank()
                        tp = P.group(P.ps[:, b, :], [(apo[kc][:, oc * 128:(oc + 1) * 128], GG[:, kc, ts]) for kc in range(2)], [b], tiles=to, waits=[ogt])
                        P.wait_tok("dve", tp)
                        ins = nc.vector.scalar_tensor_tensor(out=XT[:, oc, ts], in0=P.ps[:, b, :], scalar=G2[:, oc:oc + 1], in1=XT[:, oc, ts], op0=ALU.mult, op1=ALU.add)
                        P.bank_free[b] = P.mark("dve", ins)
                P.release(to[0]); P.release(to[1])
            P.barrier()
            if sst_dma is not None:
                for e in ("dve", "act", "sp"):
                    P.wait_dma(e, sst_dma)

        def s5_part(G2):
            PI = math.pi
            P.barrier()
            AR.reset()
            Bw = AR.bf16(2, 2, 8 * 128)
            Cw = AR.bf16(2, 2, 32 * 32)
            COEF = AR.f32(2, 2, 64)
            S5H0 = AR.f32(2, 64)
            S5ST = AR.f32(NSL * 2, 64)
            UT = AR.bf16(8, NT)
            ut_off = AR.off - 4096
            RT = Arena(P.ring[:].rearrange("p s n -> p (s n)").bitcast(F32))
            RTN = NSLOT * TSZ // 2
            mA = ccol("s5ma", 2)
            mB = ccol("s5mb", 2)
            CARRY = pcol("carry", 1)
            P.sp_sync()
            d0 = P.dma("sp", S5H0[:].rearrange("p a b -> p (a b)"), s5h0_in[:, :], "s5h0")

            def alloc(n):
                v = RT.ap[:, RT.off:RT.off + n]
                RT.off += n
                assert RT.off <= RTN, "ring temp overflow"
                return v

            def coefs(lre, lim, ls, n, want_f):
                DT = alloc(n); ZR = alloc(n); ZI = alloc(n); MAG = alloc(n); KF = alloc(n); KI = alloc(n).bitcast(I32)
                W = alloc(n); M = alloc(n); SN = alloc(n); CS = alloc(n)
                tA = P.mark("act", nc.scalar.activation(out=DT, in_=ls, func=ACT.Exp))
                P.wait_tok("dve", tA)
                nc.vector.tensor_tensor(out=ZR, in0=lre, in1=DT, op=ALU.mult)
                tD = P.mark("dve", nc.vector.tensor_tensor(out=ZI, in0=lim, in1=DT, op=ALU.mult))
                P.wait_tok("act", tD)
                tM = P.mark("act", nc.scalar.activation(out=MAG, in_=ZR, func=ACT.Exp))
                res = {}
                for name, shift, dst in (("sin", 0.0, SN), ("cos", PI / 2, CS)):
                    src = ZI
                    if shift:
                        nc.vector.tensor_scalar(out=DT, in0=ZI, scalar1=shift, scalar2=None, op0=ALU.add)
                        src = DT
                    nc.vector.tensor_scalar(out=KF, in0=src, scalar1=1.0 / (2 * PI), scalar2=None, op0=ALU.mult)
                    nc.vector.tensor_copy(out=KI, in_=KF)
                    nc.vector.tensor_copy(out=KF, in_=KI)
                    nc.vector.scalar_tensor_tensor(out=W, in0=KF, scalar=-2 * PI, in1=src, op0=ALU.mult, op1=ALU.add)
                    nc.vector.tensor_scalar(out=M, in0=W, scalar1=PI, scalar2=None, op0=ALU.is_gt)
                    nc.vector.scalar_tensor_tensor(out=W, in0=M, scalar=-2 * PI, in1=W, op0=ALU.mult, op1=ALU.add)
                    nc.vector.tensor_scalar(out=M, in0=W, scalar1=-PI, scalar2=None, op0=ALU.is_lt)
                    nc.vector.scalar_tensor_tensor(out=W, in0=M, scalar=2 * PI, in1=W, op0=ALU.mult, op1=ALU.add)
                    nc.vector.tensor_scalar(out=W, in0=W, scalar1=PI, scalar2=-PI, op0=ALU.min, op1=ALU.max)
                    tW = P.mark("dve", nc.vector.tensor_copy(out=M, in_=W))
                    P.wait_tok("act", tW)
                    tS = P.mark("act", nc.scalar.activation(out=dst, in_=M, func=ACT.Sin))
                    P.wait_tok("dve", tS)
                P.wait_tok("dve", tM)
                ABR = alloc(n); ABI = alloc(n)
                nc.vector.tensor_tensor(out=ABR, in0=MAG, in1=CS, op=ALU.mult)
                nc.vector.tensor_tensor(out=ABI, in0=MAG, in1=SN, op=ALU.mult)
                res["ab_re"], res["ab_im"] = ABR, ABI
                if want_f:
                    NRE = KF; DEN = W; FRE = alloc(n); FIM = alloc(n)
                    nc.vector.tensor_scalar(out=NRE, in0=ABR, scalar1=-1.0, scalar2=None, op0=ALU.add)
                    nc.vector.tensor_tensor(out=DEN, in0=lre, in1=lre, op=ALU.mult)
                    nc.vector.tensor_tensor(out=M, in0=lim, in1=lim, op=ALU.mult)
                    nc.vector.tensor_tensor(out=DEN, in0=DEN, in1=M, op=ALU.add)
                    nc.vector.reciprocal(out=DEN, in_=DEN)
                    nc.vector.tensor_tensor(out=FRE, in0=NRE, in1=lre, op=ALU.mult)
                    nc.vector.tensor_tensor(out=M, in0=ABI, in1=lim, op=ALU.mult)
                    nc.vector.tensor_tensor(out=FRE, in0=FRE, in1=M, op=ALU.add)
                    nc.vector.tensor_tensor(out=FRE, in0=FRE, in1=DEN, op=ALU.mult)
                    nc.vector.tensor_tensor(out=FIM, in0=ABI, in1=lre, op=ALU.mult)
                    nc.vector.tensor_tensor(out=M, in0=NRE, in1=lim, op=ALU.mult)
                    nc.vector.tensor_tensor(out=FIM, in0=FIM, in1=M, op=ALU.subtract)
                    nc.vector.tensor_tensor(out=FIM, in0=FIM, in1=DEN, op=ALU.mult)
                    res["f_re"], res["f_im"] = FRE, FIM
                return res

            RT.off = 0
            LB = alloc(192)
            CB = alloc(2048)
            dl = P.dma("sp", LB, s5lb_in[:, :], "s5l")
            dc = P.dma("sp", CB, s5cb_in[:, :], "s5l")
            for e in ("act", "dve"):
                P.wait_dma(e, dc)
            LBv = LB.rearrange("p (k n) -> p k n", k=3)
            r = coefs(LBv[:, 0, :], LBv[:, 1, :], LBv[:, 2, :], 64, False)
            for d in range(2):
                ar = r["ab_re"][:, d * 32:(d + 1) * 32]
                ai = r["ab_im"][:, d * 32:(d + 1) * 32]
                nc.vector.tensor_copy(out=COEF[:, d, 0, 0:32], in_=ar)
                nc.vector.tensor_copy(out=COEF[:, d, 0, 32:64], in_=ar)
                nc.vector.tensor_scalar(out=COEF[:, d, 1, 0:32], in0=ai, scalar1=-1.0, scalar2=None, op0=ALU.mult)
                nc.vector.tensor_copy(out=COEF[:, d, 1, 32:64], in_=ai)
            CBv = CB.rearrange("p (c d q n) -> p c d q n", c=2, d=2, q=32)
            for d in range(2):
                for c in range(2):
                    for g2 in range(2):
                        dst = Cw[:, d, c, :].rearrange("p (q m) -> p q m", q=32)[:, :, g2 * 16:(g2 + 1) * 16]
                        nc.vector.tensor_scalar(out=dst, in0=CBv[:, c, d, :, :], scalar1=mB[:, g2:g2 + 1], scalar2=(1.0 if c == 0 else -1.0), op0=ALU.mult, op1=ALU.mult)
            for d in range(2):
              for hh in range(2):
                P.sp_sync()
                RT.off = 0
                LA_ = alloc(768)
                BA = alloc(512)
                dl = P.dma("sp", LA_.rearrange("p (k n) -> p k n", k=3), s5la_in[:, :].rearrange("p (k d h n) -> p k d h n", k=3, d=2, h=2)[:, :, d, hh, :], "s5l")
                dc = P.dma("sp", BA.rearrange("p (c n) -> p c n", c=2), s5ba_in[:, :].rearrange("p (c d h n) -> p c d h n", c=2, d=2, h=2)[:, :, d, hh, :], "s5l")
                for e in ("act", "dve"):
                    P.wait_dma(e, dc)
                LAv = LA_.rearrange("p (k n) -> p k n", k=3)
                r = coefs(LAv[:, 0, :], LAv[:, 1, :], LAv[:, 2, :], 256, True)
                BAv = BA.rearrange("p (c n) -> p c n", c=2)
                BBR = alloc(256); BBI = alloc(256); TM = alloc(256)
                nc.vector.tensor_tensor(out=BBR, in0=r["f_re"], in1=BAv[:, 0, :], op=ALU.mult)
                nc.vector.tensor_tensor(out=TM, in0=r["f_im"], in1=BAv[:, 1, :], op=ALU.mult)
                nc.vector.tensor_tensor(out=BBR, in0=BBR, in1=TM, op=ALU.subtract)
                nc.vector.tensor_tensor(out=BBI, in0=r["f_re"], in1=BAv[:, 1, :], op=ALU.mult)
                nc.vector.tensor_tensor(out=TM, in0=r["f_im"], in1=BAv[:, 0, :], op=ALU.mult)
                nc.vector.tensor_tensor(out=BBI, in0=BBI, in1=TM, op=ALU.add)
                for c, src in ((0, BBR), (1, BBI)):
                    for g2 in range(2):
                        dst = Bw[:, d, c, hh * 512:(hh + 1) * 512].rearrange("p (h m) -> p h m", h=4)[:, :, g2 * 64:(g2 + 1) * 64]
                        nc.vector.tensor_scalar(out=dst, in0=src.rearrange("p (h m) -> p h m", h=4), scalar1=mA[:, g2:g2 + 1], scalar2=None, op0=ALU.mult)
            P.barrier()
            import os
            S5STOP = int(os.environ.get("S5_STOP", "99"))
            tfree = P.mark("dve", nc.vector.memset(P.scr[:, 2:3], 0.0))
            for s_ in range(NSLOT):
                P.slot_free[s_] = tfree
            if S5STOP <= 1:
                dump(COEF[:].rearrange("p a b c -> p (a b c)"))
                dump(XT[:, 0, :])
                dump(ARN[:, 0:2048])
                return
            for ch in range(8):
                t = P.load_tile(w_evin[ch])
                ap = P.tile_ap(t).rearrange("p (k n) -> p k n", k=16)
                for tb in range(NTB):
                    ts = slice(tb * TB, (tb + 1) * TB)
                    b = P.bank()
                    tp = P.group(P.ps[:, b, :], [(ap[:, kc, :], HM[:, kc, ts]) for kc in range(16)], [b], tiles=[t])
                    P.wait_tok("act", tp)
                    P.bank_free[b] = P.mark("act", nc.scalar.activation(out=UT[:, ch, ts], in_=P.ps[:, b, :], func=ACT.Copy))
                P.release(t)
            P.barrier()
            YS = HM[:].rearrange("p c t -> p (c t)").bitcast(F32).rearrange("p (c t) -> p c t", c=8)
            sd = vcol("s5_d", 0, 8)
            for ch in range(8):
                nc.vector.tensor_scalar(out=YS[:, ch, :], in0=UT[:, ch, :], scalar1=sd[:, ch:ch + 1], scalar2=None, op0=ALU.mult)
            P.barrier()
            if S5STOP <= 2:
                return
            RT.off = 0
            Hbuf = [alloc(1024).rearrange("p (t m) -> p t m", t=16) for _ in range(2)]
            Hbb = [alloc(512).bitcast(BF16).rearrange("p (t m) -> p t m", t=16) for _ in range(2)]
            T1 = AR.f32(64); T2 = AR.f32(64); HIN = AR.f32(64)
            Cw3 = alloc(1024).bitcast(BF16).rearrange("p (d c m) -> p d c m", d=2, c=2)
            nc.vector.memset(Cw3[:].rearrange("p d c m -> p (d c m)"), 0.0)
            for d_ in range(2):
                for c_ in range(2):
                    nc.vector.tensor_copy(out=Cw3[:, d_, c_, :].rearrange("p (h m) -> p h m", h=8)[:, :, 32:64],
                                          in_=Cw[:, d_, c_, :].rearrange("p (h r m) -> p h r m", h=8, r=4)[:, :, 3, :])
            Bw3 = alloc(2048).bitcast(BF16).rearrange("p (d c m) -> p d c m", d=2, c=2)
            m96 = ccol("m96", 1)
            tb3 = P.mark("dve", nc.vector.tensor_scalar(out=Bw3[:].rearrange("p d c m -> p (d c m)"), in0=Bw[:].rearrange("p d c m -> p (d c m)"), scalar1=m96, scalar2=None, op0=ALU.mult))
            P.wait_tok("pe", tb3)
            P.wait_dma("dve", d0)
            NBLK = NT // 16
            ST = AR.f32(64)
            seq = []
            for d in range(2):
                blocks = list(range(NBLK)) if d == 0 else list(range(NBLK - 1, -1, -1))
                for bi, blk in enumerate(blocks):
                    seq.append((d, bi, blk))
            NS = len(seq)
            tE = [None] * NS
            tB = [None] * NS
            yfree = [None, None]
            pe_y = [None, None]
            hb_done = [None, None]

            def emit_expand(i):
                d, bi, blk = seq[i]
                t0 = blk * 16
                P.wait_tok("pe", tB[i - 1] if i > 0 else None)
                last = None
                for c in range(2):
                    for q in range(32):
                        rg = q % 4
                        col = c * 128 + (q // 4) * 16
                        if rg < 3:
                            rows = slice(32 * rg, 32 * rg + 32)
                            wsrc = Bw[rows, d, c, (q // 4) * 128:(q // 4 + 1) * 128]
                        else:
                            rows = slice(64, 128)
                            wsrc = Bw3[rows, d, c, (q // 4) * 128:(q // 4 + 1) * 128]
                        last = nc.tensor.matmul(P.ps[:, rg, col:col + 16], wsrc, UT[rows, q // 4, t0:t0 + 16], start=True, stop=True)
                tE[i] = P.mark("pe", last)
                P.last_pe = tE[i]

            def emit_bcopy(i):
                par = i % 2
                H = Hbuf[par]
                P.wait_tok("act", tE[i])
                ins = None
                for rg in range(4):
                    o_ap = H[:].rearrange("p t (c q r) -> p t c q r", c=2, r=4)[:, :, :, :, rg]
                    i_ap = P.ps[:, rg, 0:256].rearrange("p (c q t) -> p t c q", c=2, q=8)
                    ins = nc.scalar.activation(out=o_ap, in_=i_ap, func=ACT.Copy)
                tB[i] = P.mark("act", ins)

            emit_expand(0)
            emit_bcopy(0)
            pend = None
            for i in range(NS):
                d, bi, blk = seq[i]
                par = i % 2
                t0 = blk * 16
                A1 = COEF[:, d, 0, :]
                A2 = COEF[:, d, 1, :]
                H = Hbuf[par]
                if i + 1 < NS:
                    emit_expand(i + 1)
                    emit_bcopy(i + 1)
                P.wait_tok("dve", tB[i])
                steps = list(range(16)) if d == 0 else list(range(15, -1, -1))
                slot_first = (t0 % SL == 0) if d == 0 else ((t0 + 16) % SL == 0)
                slot_last = ((t0 + 16) % SL == 0) if d == 0 else (t0 % SL == 0)
                if slot_first:
                    if bi == 0:
                        nc.vector.tensor_copy(out=HIN, in_=S5H0[:, d, :])
                    else:
                        nc.vector.tensor_scalar(out=HIN, in0=ST, scalar1=CARRY, scalar2=None, op0=ALU.mult)
                    prev = HIN
                else:
                    prev = ST
                for t in steps:
                    nc.vector.tensor_tensor(out=T1, in0=A1, in1=prev, op=ALU.mult)
                    nc.vector.tensor_tensor(out=T2[:, 0:32], in0=A2[:, 0:32], in1=prev[:, 32:64], op=ALU.mult)
                    nc.vector.tensor_tensor(out=T2[:, 32:64], in0=A2[:, 32:64], in1=prev[:, 0:32], op=ALU.mult)
                    nc.vector.tensor_tensor(out=T1, in0=T1, in1=T2, op=ALU.add)
                    nc.vector.tensor_tensor(out=H[:, t, :], in0=T1, in1=H[:, t, :], op=ALU.add)
                    prev = H[:, t, :]
                if slot_last:
                    s_idx = t0 // SL
                    nc.vector.tensor_copy(out=S5ST[:, s_idx * 2 + d, :], in_=prev)
                tSc = P.mark("dve", nc.vector.tensor_copy(out=ST, in_=prev))
                if pend is not None:
                    pb, pt0 = pend
                    P.wait_tok("dve", pe_y[pb])
                    dst = YS[:, :, pt0:pt0 + 16]
                    ins = nc.vector.tensor_tensor(out=dst, in0=dst, in1=P.ps[:, 4 + pb, 0:128].rearrange("p (c t) -> p c t", c=8), op=ALU.add)
                    yfree[pb] = P.mark("dve", ins)
                if S5STOP <= 3:
                    continue
                P.wait_tok("act", tSc)
                P.wait_tok("act", pe_y[par])
                hb_done[par] = P.mark("act", nc.scalar.activation(out=Hbb[par][:].rearrange("p t m -> p (t m)"), in_=H[:].rearrange("p t m -> p (t m)"), func=ACT.Copy))
                P.wait_tok("pe", hb_done[par])
                P.wait_tok("pe", yfree[par])
                last = None
                for ch in range(8):
                    for rg in (3, 0, 1, 2):
                        q = 4 * ch + rg
                        for c in range(2):
                            if rg < 3:
                                o_ap = P.ps[32 * rg:32 * rg + 32, 4 + par, ch * 16:ch * 16 + 16]
                                w_ap = Cw[:, d, c, q * 32:(q + 1) * 32]
                            else:
                                o_ap = P.ps[64:128, 4 + par, ch * 16:ch * 16 + 16]
                                w_ap = Cw3[:, d, c, ch * 64:(ch + 1) * 64]
                            last = nc.tensor.matmul(o_ap, w_ap, Hbb[par][:, :, c * 32 + q], start=(c == 0), stop=(c == 1))
                pe_y[par] = P.mark("pe", last)
                P.last_pe = pe_y[par]
                pend = (par, t0)
            if S5STOP <= 3:
                P.barrier()
                for b in range(8):
                    P.bank_free[b] = None
                return
            pb, pt0 = pend
            P.wait_tok("dve", pe_y[pb])
            dst = YS[:, :, pt0:pt0 + 16]
            nc.vector.tensor_tensor(out=dst, in0=dst, in1=P.ps[:, 4 + pb, 0:128].rearrange("p (c t) -> p c t", c=8), op=ALU.add)
            P.barrier()
            for b in range(8):
                P.bank_free[b] = None
            YA = UT
            RT.off = 0
            G1 = alloc(NT)
            for ch in range(8):
                y = YS[:, ch, :]
                tA = P.mark("act", nc.scalar.activation(out=G1, in_=y, func=ACT.Square))
                P.wait_tok("dve", tA)
                nc.vector.tensor_scalar(out=G1, in0=G1, scalar1=0.044715, scalar2=1.0, op0=ALU.mult, op1=ALU.add)
                tD = P.mark("dve", nc.vector.tensor_tensor(out=G1, in0=G1, in1=y, op=ALU.mult))
                P.wait_tok("act", tD)
                tA2 = P.mark("act", nc.scalar.activation(out=G1, in_=G1, func=ACT.Exp, scale=-GELU_C))
                P.wait_tok("dve", tA2)
                nc.vector.tensor_scalar(out=G1, in0=G1, scalar1=1.0, scalar2=None, op0=ALU.add)
                nc.vector.reciprocal(out=G1, in_=G1)
                tD2 = P.mark("dve", nc.vector.tensor_tensor(out=YA[:, ch, :], in0=G1, in1=y, op=ALU.mult))
                P.wait_tok("act", tD2)
            P.barrier()
            tfree = P.mark("dve", nc.vector.memset(P.scr[:, 2:3], 0.0))
            for s_ in range(NSLOT):
                P.slot_free[s_] = tfree
            YG = HM[:, 0:8, :]
            NGLB = S5H0[:, 0, 0:8]
            nc.vector.tensor_scalar(out=NGLB, in0=vcol("glu_b", 0, 8), scalar1=-1.0, scalar2=None, op0=ALU.mult)
            ygt = None
            P.barrier()
            for oc in range(8):
                t = P.load_tile(w_glu[oc], n=1024)
                ap = P.tile_ap(t)[:, 0:1024].rearrange("p (k n) -> p k n", k=8)
                for tb in range(NTB):
                    ts = slice(tb * TB, (tb + 1) * TB)
                    b = P.bank()
                    tp = P.group(P.ps[:, b, :], [(ap[:, kc, :], YA[:, kc, ts]) for kc in range(8)], [b], tiles=[t])
                    P.wait_tok("act", tp)
                    tA = P.mark("act", nc.scalar.activation(out=P.ps[:, b, :], in_=P.ps[:, b, :], func=ACT.Exp, scale=-1.0, bias=NGLB[:, oc:oc + 1]))
                    P.wait_tok("dve", tA)
                    nc.vector.tensor_scalar(out=P.ps[:, b, :], in0=P.ps[:, b, :], scalar1=1.0, scalar2=None, op0=ALU.add)
                    nc.vector.reciprocal(out=P.ps[:, b, :], in_=P.ps[:, b, :])
                    ygt = P.mark("dve", nc.vector.tensor_tensor(out=YG[:, oc, ts], in0=P.ps[:, b, :], in1=YA[:, oc, ts], op=ALU.mult))
                    P.bank_free[b] = ygt
                P.release(t)
            for half in range(2):
                tiles = [P.load_tile(w_evout[half * 4 + j]) for j in range(4)]
                aps = [P.tile_ap(t) for t in tiles]
                for oc in range(NCH):
                    for tb in range(NTB):
                        ts = slice(tb * TB, (tb + 1) * TB)
                        b = P.bank()
                        tp = P.group(P.ps[:, b, :], [(aps[j][:, oc * 128:(oc + 1) * 128], YG[:, half * 4 + j, ts]) for j in range(4)], [b], tiles=tiles, waits=[ygt])
                        P.wait_tok("dve", tp)
                        ins = nc.vector.scalar_tensor_tensor(out=XT[:, oc, ts], in0=P.ps[:, b, :], scalar=G2[:, oc:oc + 1], in1=XT[:, oc, ts], op0=ALU.mult, op1=ALU.add)
                        P.bank_free[b] = P.mark("dve", ins)
                for t in tiles:
                    P.release(t)
            P.barrier()
            P.wait("sp", P.ms["dve"], P.msn["dve"])
            ds = P.dma("sp", s5_out[:, :], S5ST[:].rearrange("p a b -> p (a b)"), "s5o")
            for e in ("dve", "act", "pe"):
                P.wait_dma(e, ds)

        P.wait_dma("dve", ldv)
        lru_prep()
        side = None
        for l in range(2):
            if l == 0:
                mvt = ada(0)
            else:
                for _ in side:
                    pass
                mvt = ada_tok[1]
            hm = norm(MV[:, l, 0, :], MV[:, l, 1, :], mvt)
            ffn(2 * l, MV[:, l, 2, :], hm)
            hm = norm(MV[:, l, 3, :], MV[:, l, 4, :], mvt)
            if l == 0:
                side = ada_gen(1, bank=7)
            if l == 0 and mix_gla:
                gla_part(MV[:, l, 5, :], side)
            if l == 0 and mix_s5:
                s5_part(MV[:, l, 5, :])
            if l == 1 and mix_odd:
                odd_mixer(l, MV[:, l, 5, :])
            hm = norm(MV[:, l, 6, :], MV[:, l, 7, :], mvt)
            ffn(2 * l + 1, MV[:, l, 8, :], hm)
        fin = norm(vcol("final_g"), None, None, final=True)
        P.barrier()
        for e in ("sp",):
            P.wait_tok(e, fin)
            P.wait("sp", P.ms["dve"], P.msn["dve"])
            P.wait("sp", P.ms["act"], P.msn["act"])
        o1 = P.dma("sp", yT_out[:, :], XT[:].rearrange("p c t -> p (c t)"), "out")
        o2 = P.dma("sp", lru_out[:, :], LRUST[:].rearrange("p s d c -> p (s d c)"), "out")
        P.wait_dma("sp", o2)
    return nc


def _offsets(items):
    off = {}
    o = 0
    for n, w in items:
        off[n] = o
        o += w
    return off, o


VOFF, NVEC = _offsets([("norm_g", 96), ("final_g", 16), ("ada_b", 288), ("conv_w", 64), ("conv_b", 16),
                       ("lru_ba", 32), ("lru_bx", 32), ("lru_lam", 32), ("gla_gb", 8), ("gla_ng", 2), ("s5_d", 8), ("glu_b", 8)])
COFF, NCST = _offsets([("ident", 128), ("jrev", 128), ("maskf", 128), ("maskb", 128), ("rm", SL), ("s5ma", 2), ("s5mb", 2), ("m96", 1)])
POFF, NPC = _offsets([("cvm", 3 * SL), ("carry", 1), ("lh0", 2 * NCH)])


def pvec(v):
    v = np.asarray(v, np.float32).reshape(-1, 128)
    return np.ascontiguousarray(v.T)


def tile_cols(W, c0, ncols=128):
    K = W.shape[0]
    t = W[:, c0:c0 + ncols].reshape(K // 128, 128, ncols).transpose(1, 0, 2)
    return t.reshape(128, -1)


def prep_shared(inp):
    sh = {}
    ada_w = inp["ada_w"]
    wa = np.empty((2 * 144, 128, TSZ), np.float32)
    for l in range(2):
        wa[l * 144:(l + 1) * 144] = ada_w[l].reshape(16, 128, 144, 128).transpose(2, 1, 0, 3).reshape(144, 128, TSZ)
    sh["w_ada"] = wa
    wi = np.empty((4 * 88, 128, TSZ), np.float32)
    wo = np.empty((4 * 44, 128, TSZ), np.float32)
    for l in range(2):
        for w in range(2):
            f = 2 * l + w
            Wi = inp["ffn_w_in"][l, w].reshape(16, 128, 88, 128).transpose(2, 1, 0, 3).reshape(88, 128, TSZ)
            wi[f * 88:(f + 1) * 88:2] = Wi[0:44]
            wi[f * 88 + 1:(f + 1) * 88:2] = Wi[44:88]
            wo[f * 44:(f + 1) * 44] = inp["ffn_w_out"][l, w].reshape(44, 128, TSZ)
    sh["w_in"] = wi
    sh["w_out"] = wo
    sh["w_odin"] = np.ascontiguousarray(inp["od_w_in"][0].reshape(16, 128, 32, 128).transpose(2, 1, 0, 3).reshape(32, 128, TSZ))
    sh["w_odout"] = np.ascontiguousarray(inp["od_w_out"][0].reshape(16, 128, TSZ))
    wl = np.stack([inp["lru_wa"][0], inp["lru_wx"][0]], 0)
    wl = wl.reshape(2, 2, 8, 2, 128, 256).transpose(2, 4, 0, 1, 3, 5)
    sh["w_lru"] = np.ascontiguousarray(wl).reshape(8, 128, TSZ)
    vec = np.zeros((128, NVEC), np.float32)

    def put(name, arr):
        a = pvec(np.asarray(arr).reshape(-1))
        vec[:, VOFF[name]:VOFF[name] + a.shape[1]] = a
    put("norm_g", inp["norm_g"])
    put("final_g", inp["final_norm_g"])
    put("ada_b", inp["ada_b"])
    put("conv_w", inp["lru_conv_w"][0])
    put("conv_b", inp["lru_conv_b"][0])
    put("lru_ba", inp["lru_ba"][0])
    put("lru_bx", inp["lru_bx"][0])
    put("lru_lam", inp["lru_lam"][0])
    put("gla_gb", inp["gla_gate_b"][0])
    put("gla_ng", inp["gla_norm_g"][0])
    evin = np.zeros((2048, 33 * 128), np.float32)
    evin[:, :4128] = inp["ev_w_in"][0]
    sh["w_evin"] = np.ascontiguousarray(evin.reshape(16, 128, 33, 128).transpose(2, 1, 0, 3).reshape(33, 128, TSZ))
    sh["w_evout"] = np.ascontiguousarray(inp["ev_w_out"][0].reshape(16, 128, TSZ))
    sh["gw2"] = np.ascontiguousarray(inp["gla_gate_w2"][0].transpose(1, 0, 2).reshape(16, 1024))
    put("s5_d", inp["s5_d"][0])
    put("glu_b", inp["s5_glu_b"][0])
    sh["w_glu"] = np.ascontiguousarray(inp["s5_glu_w"][0].reshape(8, 128, 8, 128).transpose(2, 1, 0, 3).reshape(8, 128, 1024))
    lre, lim = inp["s5_lam_re"][0], inp["s5_lam_im"][0]
    ls = np.broadcast_to(inp["s5_log_step"][0][:, :, None], (2, 64, 64))
    def layB(a):
        return a.reshape(2, 32, 2, 64).transpose(2, 3, 0, 1).reshape(128, 64)
    sh["s5lb"] = np.ascontiguousarray(np.stack([layB(lre), layB(lim), layB(ls)], 1).reshape(128, 192)).astype(np.float32)
    def layCB(a):
        return a.reshape(2, 32, 2, 16, 64).transpose(2, 4, 0, 1, 3).reshape(128, 2, 32, 16)
    sh["s5cb"] = np.ascontiguousarray(np.stack([layCB(inp["s5_c_re"][0]), layCB(inp["s5_c_im"][0])], 1).reshape(128, 2048))
    def layA(a):
        t = a.reshape(2, 8, 8, 64).transpose(2, 0, 1, 3)
        return np.broadcast_to(t[:, None], (8, 16, 2, 8, 64)).reshape(128, 1024)
    sh["s5la"] = np.ascontiguousarray(np.stack([layA(lre), layA(lim), layA(ls)], 1).reshape(128, 3072)).astype(np.float32)
    def layBA(a):
        return a.reshape(2, 8, 8, 64, 16).transpose(2, 4, 0, 1, 3).reshape(128, 1024)
    sh["s5ba"] = np.ascontiguousarray(np.stack([layBA(inp["s5_b_re"][0]), layBA(inp["s5_b_im"][0])], 1).reshape(128, 2048))
    sh["vecs"] = vec
    cst = np.zeros((128, NCST), np.float32)
    cst[:, COFF["ident"]:COFF["ident"] + 128] = np.eye(128, dtype=np.float32)
    cst[:, COFF["jrev"]:COFF["jrev"] + 128] = np.eye(128, dtype=np.float32)[::-1]
    jj = np.arange(128)[:, None]
    ii = np.arange(128)[None, :]
    same = (jj // 64) == (ii // 64)
    cst[:, COFF["maskf"]:COFF["maskf"] + 128] = (same & (jj <= ii)).astype(np.float32)
    cst[:, COFF["maskb"]:COFF["maskb"] + 128] = (same & (jj >= ii)).astype(np.float32)
    cst[:, COFF["rm"]:COFF["rm"] + SL] = (np.arange(SL) % 64 != 0).astype(np.float32)[None, :]
    pidx = np.arange(128)
    for g2 in range(2):
        cst[:, COFF["s5ma"] + g2] = ((pidx // 16) % 2 == g2).astype(np.float32)
        cst[:, COFF["s5mb"] + g2] = ((pidx // 64) == g2).astype(np.float32)
    cst[:, COFF["m96"]] = (pidx >= 96).astype(np.float32)
    sh["consts"] = cst
    return sh


PROMPT_ASSIGN = [[0, 1, 2], [3, 4, 5], [6, 7, 8], [9, 10, 11], [12, 13], [14, 15]]


def prep_core(inp, core):
    m = {}
    x = np.zeros((NT, D), np.float32)
    pc = np.zeros((128, NPC), np.float32)
    t = np.arange(SL)
    if core < 2:
        x[:] = inp["x_sample"][core]
        cond = inp["c"][core]
        seg = 64
        pc[:, POFF["carry"]] = 1.0
        lh0 = np.stack([pvec(inp["state_lru"][core, 0, d]) for d in range(2)], 1)
        pc[:, POFF["lh0"]:POFF["lh0"] + 32] = lh0.reshape(128, 32)
    else:
        for s, b in enumerate(PROMPT_ASSIGN[core - 2]):
            x[SL * s:SL * (s + 1)] = inp["x_prompt"][b]
        cond = inp["c_ctx"]
        seg = 256
    cvm = np.stack([(t % seg >= 2), (t % seg >= 1), (t % seg <= seg - 2)], 0).astype(np.float32)
    pc[:, POFF["cvm"]:POFF["cvm"] + 3 * SL] = cvm.reshape(1, -1)
    m["pcore"] = pc
    gs0 = np.zeros((4, 128, 2, SL), np.float32)
    if core < 2:
        gs0[:] = inp["state_gla"][core, 0].transpose(1, 2, 0, 3)
    m["gs0"] = gs0.reshape(4, 128, 2 * SL)
    h0 = np.zeros((128, 2, 2, 32), np.float32)
    if core < 2:
        for c, nm in enumerate(("state_s5_re", "state_s5_im")):
            h0[:, :, c, :] = inp[nm][core, 0].reshape(2, 32, 2, 64).transpose(2, 3, 0, 1).reshape(128, 2, 32)
    m["s5h0"] = h0.reshape(128, 128)
    m["xT"] = np.ascontiguousarray(x.T.reshape(NCH, 128, NT).transpose(1, 0, 2)).reshape(128, NCH * NT)
    m["cond"] = pvec(cond)
    return m


def assemble(inp, results):
    yp = np.zeros((16, 256, D), np.float32)
    ys = np.zeros((2, 1024, D), np.float32)
    st_re = np.zeros((16, 1, 2, 64, 64), np.float32)
    st_im = np.zeros((16, 1, 2, 64, 64), np.float32)
    st_gla = np.zeros((16, 1, 2, 4, 128, 256), np.float32)
    st_lru = np.zeros((16, 1, 2, D), np.float32)
    for core in range(N_CORES):
        yT = results[core]["yT"].reshape(128, NCH, NT)
        y = yT.transpose(2, 1, 0).reshape(NT, D)
        if core < 2:
            ys[core] = y
        else:
            ls = results[core]["lru_st"].reshape(128, NSL, 2, NCH)
            gs = results[core]["gla_st"].reshape(NSL, 2, 4, 128, SL)
            s5 = results[core]["s5_st"].reshape(2, 64, NSL, 2, 2, 32)
            for s, b in enumerate(PROMPT_ASSIGN[core - 2]):
                yp[b] = y[SL * s:SL * (s + 1)]
                st_lru[b, 0] = ls[:, s].transpose(1, 2, 0).reshape(2, D)
                st_gla[b, 0] = gs[s]
                st_re[b, 0] = s5[:, :, s, :, 0, :].transpose(2, 3, 0, 1).reshape(2, 64, 64)
                st_im[b, 0] = s5[:, :, s, :, 1, :].transpose(2, 3, 0, 1).reshape(2, 64, 64)
    return yp, ys, st_re, st_im, st_gla, st_lru


def kernel(**inputs):
    inp = {k: np.asarray(v) for k, v in inputs.items()}
    shared = prep_shared(inp)
    in_maps = []
    for core in range(N_CORES):
        m = dict(shared)
        m.update(prep_core(inp, core))
        in_maps.append(m)
    nc = build_program()
    res = run_bass_kernel_spmd(nc, in_maps, core_ids=list(range(N_CORES)))
    return assemble(inp, res.results)
```
